# Optimizing a Trainium2 kernel written in Bass

```python
import jax, jax.numpy as jnp
from jax import lax
import numpy as np

D_MODEL = 1024
BATCH = 8
SEQ = 4096
DEPTH = 4
DEC_BATCH = 2
DEC_SEQ = 16384
PAST_LEN = 128

N_MEM = 256
D_POOL = D_MODEL
POOL_WINDOWS = (2, 4, 8, 16)
N_POOL_GROUPS = len(POOL_WINDOWS)
POOL_GROUP = D_POOL // N_POOL_GROUPS
D_LRU = D_MODEL
N_LRU_BLOCKS = 8
LRU_BLOCK = D_LRU // N_LRU_BLOCKS
CONV_WIDTH = 4
CONV_LEFT = 2
LRU_C = 8.0
N_XHEADS = 4
D_XATTN = D_MODEL
XHEAD_DIM = D_XATTN // N_XHEADS
N_BRANCHES = 3
D_IN = D_POOL + D_LRU + D_XATTN + N_BRANCHES * D_MODEL
D_FF = 4 * D_MODEL
EPS = 1e-6

kernel_name = "hybrid_pool_rglru_memxattn_encoder"


def _rms_norm(x, gain):
    xf = x.astype(jnp.float32)
    y = xf * lax.rsqrt(jnp.mean(xf * xf, axis=-1, keepdims=True) + EPS)
    return (y * gain.astype(jnp.float32)).astype(x.dtype)


def _pool_mixer(u, pool_w, pool_scale):
    B, S, _ = u.shape
    uf = u.astype(jnp.float32)
    cs = jnp.pad(jnp.cumsum(uf, axis=1), ((0, 0), (1, 0), (0, 0)))
    t = jnp.arange(S)
    outs = []
    for g, w in enumerate(POOL_WINDOWS):
        h = w // 2
        lo_c, hi_c = g * POOL_GROUP, (g + 1) * POOL_GROUP
        csg = jnp.pad(cs[..., lo_c:hi_c], ((0, 0), (h, h), (0, 0)), mode="edge")
        win_sum = csg[:, 2 * h:2 * h + S] - csg[:, :S]
        count = (jnp.minimum(t + h, S) - jnp.maximum(t - h, 0)).astype(jnp.float32)
        pooled = win_sum / count[None, :, None] - uf[..., lo_c:hi_c]
        outs.append(jnp.einsum("bsc,cd->bsd", pooled.astype(u.dtype), pool_w[g]))
    return jnp.concatenate(outs, axis=-1) * pool_scale


def _conv_centred(u, conv_w, conv_b):
    S = u.shape[1]
    up = jnp.pad(u, ((0, 0), (CONV_LEFT, CONV_WIDTH - 1 - CONV_LEFT), (0, 0)))
    y = conv_b + up[:, 0:S] * conv_w[0]
    for k in range(1, CONV_WIDTH):
        y = y + up[:, k:k + S] * conv_w[k]
    return y


def _lin_combine(e1, e2):
    a1, b1 = e1
    a2, b2 = e2
    return (a1 * a2, a2 * b1 + b2)


def _rglru_direction(xc, wa, ba, wx, bx, lam, reverse):
    B, S, C = xc.shape
    xb = xc.reshape(B, S, N_LRU_BLOCKS, LRU_BLOCK)
    r = jax.nn.sigmoid((jnp.einsum("bshi,hij->bshj", xb, wa).reshape(B, S, C) + ba).astype(jnp.float32))
    i = jax.nn.sigmoid((jnp.einsum("bshi,hij->bshj", xb, wx).reshape(B, S, C) + bx).astype(jnp.float32))
    log_a = -LRU_C * r * jax.nn.softplus(-lam.astype(jnp.float32))
    a = jnp.exp(log_a)
    mult = jnp.sqrt(-jnp.expm1(2.0 * log_a))
    b = mult * i * xc.astype(jnp.float32)
    _, h = lax.associative_scan(_lin_combine, (a, b), reverse=reverse, axis=1)
    return h


def _memory_attention(q, mem_n, w_kv):
    B, S, _ = q.shape
    M = mem_n.shape[1]
    kv = jnp.einsum("bmd,de->bme", mem_n, w_kv)
    k = kv[..., :D_XATTN].reshape(B, M, N_XHEADS, XHEAD_DIM)
    v = kv[..., D_XATTN:].reshape(B, M, N_XHEADS, XHEAD_DIM)
    qh = q.reshape(B, S, N_XHEADS, XHEAD_DIM)
    s = jnp.einsum("bshd,bmhd->bhsm", qh, k).astype(jnp.float32) * (XHEAD_DIM ** -0.5)
    p = jax.nn.softmax(s, axis=-1).astype(v.dtype)
    return jnp.einsum("bhsm,bmhd->bshd", p, v).reshape(B, S, D_XATTN)


def _mixer_sublayer(x, mem, g_pre, g_post, g_mem, w_in, pool_w, pool_scale, conv_w, conv_b,
                    lru_wa, lru_ba, lru_wx, lru_bx, lru_lambda, w_kv, w_out):
    B, S, _ = x.shape
    h = _rms_norm(x, g_pre)
    z = jnp.einsum("bsd,de->bse", h, w_in)
    o1, o2, o3 = D_POOL, D_POOL + D_LRU, D_POOL + D_LRU + D_XATTN
    u_pool, u_lru, q, gate_logits = z[..., :o1], z[..., o1:o2], z[..., o2:o3], z[..., o3:]
    y_pool = _pool_mixer(u_pool, pool_w, pool_scale).astype(jnp.float32)
    xc = _conv_centred(u_lru, conv_w, conv_b)
    y_lru = (_rglru_direction(xc, lru_wa[0], lru_ba[0], lru_wx[0], lru_bx[0], lru_lambda[0], False)
             + _rglru_direction(xc, lru_wa[1], lru_ba[1], lru_wx[1], lru_bx[1], lru_lambda[1], True))
    y_mem = _memory_attention(q, _rms_norm(mem, g_mem), w_kv).astype(jnp.float32)
    gates = jax.nn.sigmoid(gate_logits.astype(jnp.float32)).reshape(B, S, N_BRANCHES, D_MODEL)
    merged = gates[..., 0, :] * y_pool + gates[..., 1, :] * y_lru + gates[..., 2, :] * y_mem
    out = jnp.einsum("bsd,de->bse", merged.astype(x.dtype), w_out)
    return x + _rms_norm(out, g_post)


def _mlp_sublayer(x, g_pre, g_post, w1, w2):
    h = _rms_norm(x, g_pre)
    a = jax.nn.relu(jnp.einsum("bsd,df->bsf", h, w1))
    y = jnp.einsum("bsf,fd->bsd", a * a, w2)
    return x + _rms_norm(y, g_post)


def _trunk(x, mem, norm_mix_pre, norm_mix_post, norm_mem, w_in, pool_w, pool_scale, conv_w, conv_b,
           lru_wa, lru_ba, lru_wx, lru_bx, lru_lambda, w_kv, w_out,
           norm_mlp_pre, norm_mlp_post, mlp_w1, mlp_w2):
    for l in range(DEPTH):
        x = _mixer_sublayer(x, mem, norm_mix_pre[l], norm_mix_post[l], norm_mem[l], w_in[l],
                            pool_w[l], pool_scale[l], conv_w[l], conv_b[l],
                            lru_wa[l], lru_ba[l], lru_wx[l], lru_bx[l], lru_lambda[l],
                            w_kv[l], w_out[l])
        x = _mlp_sublayer(x, norm_mlp_pre[l], norm_mlp_post[l], mlp_w1[l], mlp_w2[l])
    return x


def setup_inputs(seed: int = 0) -> dict:
    key = jax.random.key(seed)
    ks = jax.random.split(key, 24)
    f32 = jnp.float32

    def nrm(k, shape, scale):
        return jax.random.normal(k, shape, f32) * scale

    def gain(k):
        return 1.0 + 0.05 * jax.random.normal(k, (DEPTH, D_MODEL), f32)

    a0 = jax.random.uniform(ks[13], (DEPTH, 2, D_LRU), f32, minval=0.9, maxval=0.999)
    return {
        "x_prompt": jax.random.normal(ks[0], (BATCH, SEQ, D_MODEL), f32),
        "x_sample": jax.random.normal(ks[1], (DEC_BATCH, DEC_SEQ, D_MODEL), f32),
        "mem_prompt": jax.random.normal(ks[2], (BATCH, N_MEM, D_MODEL), f32),
        "mem_sample": jax.random.normal(ks[3], (DEC_BATCH, N_MEM, D_MODEL), f32),
        "norm_mix_pre": gain(ks[4]),
        "norm_mix_post": gain(ks[5]),
        "norm_mem": gain(ks[6]),
        "w_in": nrm(ks[7], (DEPTH, D_MODEL, D_IN), D_MODEL ** -0.5),
        "pool_w": nrm(ks[8], (DEPTH, N_POOL_GROUPS, POOL_GROUP, POOL_GROUP), POOL_GROUP ** -0.5),
        "pool_scale": 1.0 + 0.1 * jax.random.normal(ks[9], (DEPTH, D_POOL), f32),
        "conv_w": nrm(ks[10], (DEPTH, CONV_WIDTH, D_LRU), CONV_WIDTH ** -0.5),
        "conv_b": nrm(ks[11], (DEPTH, D_LRU), 0.02),
        "lru_wa": nrm(ks[12], (DEPTH, 2, N_LRU_BLOCKS, LRU_BLOCK, LRU_BLOCK), LRU_BLOCK ** -0.5),
        "lru_ba": nrm(ks[14], (DEPTH, 2, D_LRU), 0.02),
        "lru_wx": nrm(ks[15], (DEPTH, 2, N_LRU_BLOCKS, LRU_BLOCK, LRU_BLOCK), LRU_BLOCK ** -0.5),
        "lru_bx": nrm(ks[16], (DEPTH, 2, D_LRU), 0.02),
        "lru_lambda": jnp.log(a0) - jnp.log1p(-a0),
        "w_kv": nrm(ks[17], (DEPTH, D_MODEL, 2 * D_XATTN), D_MODEL ** -0.5),
        "w_out": nrm(ks[18], (DEPTH, D_MODEL, D_MODEL), D_MODEL ** -0.5),
        "norm_mlp_pre": gain(ks[19]),
        "norm_mlp_post": gain(ks[20]),
        "mlp_w1": nrm(ks[21], (DEPTH, D_MODEL, D_FF), D_MODEL ** -0.5),
        "mlp_w2": nrm(ks[22], (DEPTH, D_FF, D_MODEL), D_FF ** -0.5),
    }


def reference(x_prompt, x_sample, mem_prompt, mem_sample, norm_mix_pre, norm_mix_post, norm_mem,
              w_in, pool_w, pool_scale, conv_w, conv_b, lru_wa, lru_ba, lru_wx, lru_bx, lru_lambda,
              w_kv, w_out, norm_mlp_pre, norm_mlp_post, mlp_w1, mlp_w2):
    y_prompt = _trunk(x_prompt, mem_prompt, norm_mix_pre, norm_mix_post, norm_mem, w_in, pool_w,
                      pool_scale, conv_w, conv_b, lru_wa, lru_ba, lru_wx, lru_bx, lru_lambda,
                      w_kv, w_out, norm_mlp_pre, norm_mlp_post, mlp_w1, mlp_w2)
    y_sample = _trunk(x_sample, mem_sample, norm_mix_pre, norm_mix_post, norm_mem, w_in, pool_w,
                      pool_scale, conv_w, conv_b, lru_wa, lru_ba, lru_wx, lru_bx, lru_lambda,
                      w_kv, w_out, norm_mlp_pre, norm_mlp_post, mlp_w1, mlp_w2)
    return (y_prompt, y_sample)
```

```python
import contextlib
import numpy as np
import concourse.bass as bass
import concourse.mybir as mybir
from concourse.bass_utils import run_bass_kernel_spmd

F32 = mybir.dt.float32
BF16 = mybir.dt.bfloat16
AF = mybir.ActivationFunctionType
ALU = mybir.AluOpType

D = 1024
NCH = 8
T = 512
HALO = 8
TH = T + 2 * HALO
SEG = 4096
TPS = SEG // T
NMEM = 256
DEPTH = 4
EPS = 1e-6
POOL_W = (2, 4, 8, 16)

CV_NAMES = [("g_mix_pre", 8), ("g_mix_post", 8), ("g_mem", 8), ("pool_scale", 8), ("conv_w", 32),
            ("conv_b", 8), ("lru_ba", 16), ("lru_bx", 16), ("lru_lam", 16), ("g_mlp_pre", 8),
            ("g_mlp_post", 8)]
CV_OFF = {}
_o = 0
for _n, _c in CV_NAMES:
    CV_OFF[_n] = _o
    _o += _c
CPL = _o
CV_EDGE = DEPTH * CPL
CV_LINK = CV_EDGE + 256
NCV = CV_LINK + 1


class Reg:
    __slots__ = ("lw", "rd")

    def __init__(self):
        self.lw = None
        self.rd = []


class Buf:
    def __init__(self, name, t, nreg=1, space="sbuf"):
        self.name = name
        self.t = t
        self.regs = [Reg() for _ in range(nreg)]
        self.space = space
        self.sem = None
        self.dcount = 0
        self.last_dma = None


class Kern:
    ENGS = ("pe", "act", "dve", "pool", "sp")

    def __init__(self, nc):
        self.nc = nc
        self.prog = {e: [] for e in self.ENGS}
        self.cnt = {e: 0 for e in self.ENGS}
        self.waited = {e: {} for e in self.ENGS}
        self.semh = {}
        for e in ("pe", "act", "dve", "pool"):
            self.semh[e] = nc.alloc_semaphore(name="sem_" + e)
        self.dma_sems = []
        self.free_slots = []
        self.sw_events = []
        self.nslots = 0
        self.phase_bufs = []
        self.nbuf = 0
        self.ps_rr = 0

    def _deps(self, eng, R, W):
        deps = []
        for buf, regs in R:
            for r in (range(len(buf.regs)) if regs is None else regs):
                g = buf.regs[r]
                if g.lw is not None:
                    deps.append(g.lw)
        for buf, regs in W:
            for r in (range(len(buf.regs)) if regs is None else regs):
                g = buf.regs[r]
                if g.lw is not None:
                    deps.append(g.lw)
                deps.extend(g.rd)
        return deps

    def _commit(self, ev, R, W):
        for buf, regs in R:
            for r in (range(len(buf.regs)) if regs is None else regs):
                buf.regs[r].rd.append(ev)
        for buf, regs in W:
            for r in (range(len(buf.regs)) if regs is None else regs):
                g = buf.regs[r]
                g.lw = ev
                g.rd = []

    def _waits(self, eng, deps):
        need = {}
        for key, val in deps:
            if need.get(key, 0) < val:
                need[key] = val
        out = []
        for key, val in need.items():
            if key == eng and eng == "pe":
                continue
            if self.waited[eng].get(key, 0) >= val:
                continue
            self.waited[eng][key] = val
            out.append((self.semh[key], val))
        return out

    def op(self, eng, fn, R=(), W=()):
        deps = self._deps(eng, R, W)
        waits = self._waits(eng, deps)
        self.cnt[eng] += 1
        ev = (eng, self.cnt[eng])
        sem = self.semh[eng]

        def run(e, waits=waits, fn=fn, sem=sem):
            for s, v in waits:
                e.wait_ge(s, v)
            fn(e).then_inc(sem, 1)
        self.prog[eng].append(run)
        self._commit(ev, R, W)

    def dma(self, q, out_ap, in_ap, sb, R=(), W=(), cast=False):
        if cast:
            self.nslots += 1
            key = ("s", self.nslots)
            sem = self.nc.alloc_semaphore(name="w_%d" % self.nslots)
            self.semh[key] = sem
            deps = self._deps(q, R, W)
            if sb.last_dma is not None:
                deps.append(sb.last_dma)
            waits = self._waits(q, deps)
            ev = (key, 16)
            sb.last_dma = ev
            self.sw_events.append(ev)

            def run(e, waits=waits, sem=sem, out_ap=out_ap, in_ap=in_ap):
                for s_, v in waits:
                    e.wait_ge(s_, v)
                e.dma_start(out=out_ap, in_=in_ap, max_dma_last_dim=8192).then_inc(sem, 16)
            self.prog[q].append(run)
            self._commit(ev, R, W)
            return
        if sb.sem is None:
            if self.free_slots:
                sb.semkey, sb.sem, sb.dcount = self.free_slots.pop()
            else:
                sb.sem = self.nc.alloc_semaphore(name="d_" + sb.name)
                self.nslots += 1
                sb.semkey = ("d", self.nslots)
                self.semh[sb.semkey] = sb.sem
            self.dma_sems.append(sb)
        deps = self._deps(q, R, W)
        if sb.last_dma is not None:
            deps.append(sb.last_dma)
        waits = self._waits(q, deps)
        sb.dcount += 16
        ev = (sb.semkey, sb.dcount)
        sb.last_dma = ev
        sem = sb.sem
        kw = {}
        if cast:
            kw["max_dma_last_dim"] = 8192

        def run(e, waits=waits, sem=sem, out_ap=out_ap, in_ap=in_ap, kw=kw):
            for s, v in waits:
                e.wait_ge(s, v)
            e.dma_start(out=out_ap, in_=in_ap, **kw).then_inc(sem, 16)
        self.prog[q].append(run)
        self._commit(ev, R, W)

    def barrier(self):
        for e in self.ENGS:
            deps = [(o, self.cnt[o]) for o in ("pe", "act", "dve", "pool") if self.cnt[o] > 0]
            deps += [(b.semkey, b.dcount) for b in self.dma_sems]
            deps += self.sw_events
            waits = self._waits(e, deps)

            def run(eng, waits=waits):
                for s, v in waits:
                    eng.wait_ge(s, v)
            self.prog[e].append(run)

    def flush(self, name):
        nc = self.nc
        prog = self.prog
        with nc.Block(name) as block:
            @block.tensor
            def _(e):
                for f in prog["pe"]:
                    f(e)

            @block.scalar
            def _(e):
                for f in prog["act"]:
                    f(e)

            @block.vector
            def _(e):
                for f in prog["dve"]:
                    f(e)

            @block.gpsimd
            def _(e):
                for f in prog["pool"]:
                    f(e)

            @block.sync
            def _(e):
                for f in prog["sp"]:
                    f(e)
        self.prog = {e: [] for e in self.ENGS}
        for b in self.phase_bufs:
            if b.sem is not None:
                self.free_slots.append((b.semkey, b.sem, b.dcount))
                self.dma_sems.remove(b)
                b.sem = None
        self.phase_bufs = []
        self.sw_events = []


def build_program(nseg=4, depth=DEPTH):
    TOK = nseg * SEG
    NT = TOK // T
    nc = bass.Bass("TRN2", target_bir_lowering=False)
    K = Kern(nc)

    def din(name, shape, dt=F32):
        return nc.dram_tensor(name, list(shape), dt, kind="ExternalInput")

    xT = din("xT", [D, TOK])
    memT = din("memT", [nseg, D, NMEM])
    cvd = din("cv", [128, NCV])
    w_in = din("w_in", [DEPTH, D, 6 * D])
    pool_w = din("pool_w", [DEPTH, 4, 256, 256])
    lru_wa = din("lru_wa", [DEPTH, 2, 8, 128, 128])
    lru_wx = din("lru_wx", [DEPTH, 2, 8, 128, 128])
    w_kv = din("w_kv", [DEPTH, D, 2 * D])
    w_out = din("w_out", [DEPTH, D, D])
    mlp_w1 = din("mlp_w1", [DEPTH, D, 4 * D])
    mlp_w2 = din("mlp_w2", [DEPTH, 4 * D, D])
    yT = nc.dram_tensor("yT", [D, TOK], F32, kind="ExternalOutput")

    def dscr(name, shape, dt, nreg):
        return Buf(name, nc.dram_tensor(name, list(shape), dt), nreg, "dram")

    XTd = dscr("XTd", [D, TOK], F32, NT)
    HTd = dscr("HTd", [D, TOK + 2 * HALO], BF16, NT + 2)
    MTd = dscr("MTd", [D, TOK], F32, NT)
    GABd = dscr("GABd", [D, TOK], F32, NT)
    MLd = dscr("MLd", [D, TOK], BF16, NT)
    MTbd = dscr("MTbd", [D, TOK], BF16, NT)
    GABbd = dscr("GABbd", [D, TOK], BF16, NT)
    H2Td = dscr("H2Td", [D, TOK], BF16, NT)
    HIDd = dscr("HIDd", [4 * D, TOK], BF16, NT)
    KTd = dscr("KTd", [nseg, D, NMEM], BF16, nseg)
    Vd = dscr("Vd", [nseg, NMEM, D], BF16, nseg)
    xin = Buf("xin", xT, NT, "dram")
    yout = Buf("yout", yT, NT, "dram")

    def fm(dbuf, c0, n):
        return dbuf.t[:, c0:c0 + n].rearrange("(c p) n -> p c n", p=128)

    es = contextlib.ExitStack()

    def sb(name, shape, dt, nreg=1, stack=None):
        K.nbuf += 1
        name = "%s_%d" % (name, K.nbuf)
        t = (stack or es).enter_context(nc.sbuf_tensor(name, list(shape), dt))
        b = Buf(name, t, nreg)
        if stack is not None:
            K.phase_bufs.append(b)
        return b

    with es:
        CV = sb("CV", [128, NCV], F32)
        CC = sb("CC", [128, DEPTH * 16], F32)
        CC2 = sb("CC2", [128, DEPTH * 16], F32)
        ONES = sb("ONES", [128, 128], BF16)
        ONES2 = sb("ONES2", [128, 128], BF16)
        EPSC = sb("EPSC", [128, 1], F32)
        HBB = sb("HBB", [128, DEPTH * 32], F32)
        HC = sb("HC", [128, DEPTH * 16], F32)
        ZER = sb("ZER", [128, T], F32)
        ZB = sb("ZB", [128, 8, HALO], BF16)
        CF = sb("CF", [128, 8], F32)
        HBS = sb("HBS", [128, 8, NT], F32)
        PBS = sb("PBS", [128, 8, NT], F32)
        CBK = sb("CBK", [128, 8, NT], F32)
        TMPC = sb("TMPC", [128, 8], F32)
        PS = [Buf("ps%d" % i, es.enter_context(nc.psum_tensor("ps%d" % i, [128, T], F32)), 1, "psum")
              for i in range(8)]

        def ps():
            K.ps_rr = (K.ps_rr + 1) % 8
            return PS[K.ps_rr]

        def cvc(l, name, idx=0):
            c = l * CPL + CV_OFF[name] + idx
            return CV.t[:, c:c + 1]

        LINK = lambda: CV.t[:, CV_LINK:CV_LINK + 1]

        K.dma("sp", CV.t[:, :], cvd[:, :], CV, W=[(CV, None)])
        K.op("pool", lambda e: e.memset(ONES.t[:, :], 1.0), W=[(ONES, None)])
        K.op("pool", lambda e: e.memset(ZER.t[:, :], 0.0), W=[(ZER, None)])
        K.op("pool", lambda e: e.memset(ONES2.t[:, :], 2.0), W=[(ONES2, None)])
        K.op("pool", lambda e: e.memset(EPSC.t[:, :], EPS), W=[(EPSC, None)])
        K.op("pool", lambda e: e.memset(ZB.t[:, :, :], 0.0), W=[(ZB, None)])
        K.dma("sp", fm(HTd, 0, HALO), ZB.t[:, :, :], ZB, R=[(ZB, None)], W=[(HTd, [NT])])
        K.dma("sp", fm(HTd, TOK + HALO, HALO), ZB.t[:, :, :], ZB, R=[(ZB, None)], W=[(HTd, [NT + 1])])
        for l in range(depth):
            src = CV.t[:, l * CPL + CV_OFF["lru_lam"]: l * CPL + CV_OFF["lru_lam"] + 16]
            dst = CC.t[:, l * 16:(l + 1) * 16]
            dst2 = CC2.t[:, l * 16:(l + 1) * 16]
            K.op("act", lambda e, s=src, d=dst: e.activation(out=d, in_=s, func=AF.Exp, scale=-1.0),
                 R=[(CV, None)], W=[(CC, None)])
            K.op("act", lambda e, d=dst: e.activation(out=d, in_=d, func=AF.Ln, bias=1.0, scale=1.0),
                 R=[(CC, None)], W=[(CC, None)])
            K.op("dve", lambda e, d=dst: e.tensor_scalar(out=d, in0=d, scalar1=-8.0, scalar2=None, op0=ALU.mult),
                 R=[(CC, None)], W=[(CC, None)])
            K.op("dve", lambda e, d=dst, d2=dst2: e.tensor_scalar(out=d2, in0=d, scalar1=2.0, scalar2=None, op0=ALU.mult),
                 R=[(CC, None)], W=[(CC2, None)])
            K.op("dve", lambda e, d=dst, l=l: e.tensor_scalar(out=HC.t[:, l * 16:(l + 1) * 16], in0=d, scalar1=0.5, scalar2=None, op0=ALU.mult),
                 R=[(CC, None)], W=[(HC, None)])
            bsrc = CV.t[:, l * CPL + CV_OFF["lru_ba"]: l * CPL + CV_OFF["lru_ba"] + 32]
            K.op("dve", lambda e, bsrc=bsrc, l=l: e.tensor_scalar(out=HBB.t[:, l * 32:(l + 1) * 32], in0=bsrc, scalar1=0.5, scalar2=None, op0=ALU.mult),
                 R=[(CV, None)], W=[(HBB, None)])
        K.barrier()
        K.flush("setup")

        def mm_group(psb, out_ap, pairs, R):
            def fn(e, pairs=pairs, out_ap=out_ap):
                n = len(pairs)
                ins = None
                for i, (l, r) in enumerate(pairs):
                    ins = e.matmul(out_ap, l, r, start=(i == 0), stop=(i == n - 1))
                return ins
            K.op("pe", fn, R=R, W=[(psb, None)])

        def rstd_from_sq(SQ, ncols, RS):
            p = ps()
            mm_group(p, p.t[:, 0:ncols], [(ONES.t[:, :], SQ.t[:, k, 0:ncols]) for k in range(NCH)],
                     R=[(ONES, None), (SQ, None)])
            K.op("act", lambda e, p=p: e.activation(out=RS.t[:, 0:ncols], in_=p.t[:, 0:ncols], func=AF.Ln,
                                                    bias=EPSC.t[:, 0:1], scale=1.0 / D),
                 R=[(p, None), (EPSC, None)], W=[(RS, None)])
            K.op("act", lambda e: e.activation(out=RS.t[:, 0:ncols], in_=RS.t[:, 0:ncols], func=AF.Exp, scale=-0.5),
                 R=[(RS, None)], W=[(RS, None)])

        def norm_to_bf16(X, ncols, SQ, RS, Hout, hoff, l, gname):
            K.op("act", lambda e: e.activation(out=SQ.t[:, :, 0:ncols], in_=X.t[:, :, 0:ncols], func=AF.Square),
                 R=[(X, None)], W=[(SQ, None)])
            rstd_from_sq(SQ, ncols, RS)
            for k in range(NCH):
                K.op("dve", lambda e, k=k: e.scalar_tensor_tensor(
                    out=Hout.t[:, k, hoff:hoff + ncols], in0=X.t[:, k, 0:ncols], scalar=cvc(l, gname, k),
                    in1=RS.t[:, 0:ncols], op0=ALU.mult, op1=ALU.mult),
                    R=[(X, [k]), (RS, None), (CV, None)], W=[(Hout, [k])])

        def load_w(W, dram_ap, q="pool"):
            shp = list(W.t.shape)
            n = shp[-1]
            if len(shp) == 3 and n > 2048:
                for n0 in range(0, n, 2048):
                    K.dma(q, W.t[:, :, n0:n0 + 2048], dram_ap[:, :, n0:n0 + 2048], W, W=[(W, None)], cast=True)
            else:
                K.dma(q, W.t[:], dram_ap, W, W=[(W, None)], cast=True)

        def w_in_cols(l, c0, n=D):
            return w_in[l, :, c0:c0 + n].rearrange("(c p) e -> p c e", p=128)

        def ht_regs(i):
            r = [i]
            r.append(i - 1 if i > 0 else NT)
            r.append(i + 1 if i < NT - 1 else NT + 1)
            return r

        def load_h_halo(Hh, i):
            K.dma("sp", Hh.t[:, :, :], fm(HTd, i * T, TH), Hh, R=[(HTd, ht_regs(i))], W=[(Hh, None)])
            if i % TPS == 0 and i > 0:
                K.op("dve", lambda e: e.tensor_scalar(out=Hh.t[:, :, 0:HALO], in0=Hh.t[:, :, 0:HALO], scalar1=LINK(),
                                                      scalar2=None, op0=ALU.mult),
                     R=[(Hh, None), (CV, None)], W=[(Hh, None)])
            if i % TPS == TPS - 1 and i < NT - 1:
                K.op("dve", lambda e: e.tensor_scalar(out=Hh.t[:, :, T + HALO:TH], in0=Hh.t[:, :, T + HALO:TH],
                                                      scalar1=LINK(), scalar2=None, op0=ALU.mult),
                     R=[(Hh, None), (CV, None)], W=[(Hh, None)])

        def proj_halo(Wt, Hh, U):
            p = ps()
            for k in range(NCH):
                mm_group(p, p.t[:, k * 16:k * 16 + 8],
                         [(Wt.t[:, d, k * 128:(k + 1) * 128], Hh.t[:, d, 0:HALO]) for d in range(NCH)],
                         R=[(Wt, None), (Hh, None)])
                mm_group(p, p.t[:, k * 16 + 8:k * 16 + 16],
                         [(Wt.t[:, d, k * 128:(k + 1) * 128], Hh.t[:, d, T + HALO:TH]) for d in range(NCH)],
                         R=[(Wt, None), (Hh, None)])
            pv = p.t[:, 0:128].rearrange("p (k c) -> p k c", c=16)
            K.op("act", lambda e, pv=pv: e.activation(out=U.t[:, :, 0:HALO], in_=pv[:, :, 0:8], func=AF.Copy),
                 R=[(p, None)], W=[(U, None)])
            K.op("act", lambda e, pv=pv: e.activation(out=U.t[:, :, T + HALO:TH], in_=pv[:, :, 8:16], func=AF.Copy),
                 R=[(p, None)], W=[(U, None)])
            for k in range(NCH):
                p = ps()
                mm_group(p, p.t[:, :], [(Wt.t[:, d, k * 128:(k + 1) * 128], Hh.t[:, d, HALO:HALO + T]) for d in range(NCH)],
                         R=[(Wt, None), (Hh, None)])
                K.op("act", lambda e, p=p, k=k: e.activation(out=U.t[:, k, HALO:HALO + T], in_=p.t[:, :], func=AF.Copy),
                     R=[(p, None)], W=[(U, [k])])

        def gate_sig(Wg, Hrhs_fn, k, out_ap, Gbuf, gregs, R, tanh=False):
            p = ps()
            mm_group(p, p.t[:, :], [(Wg.t[:, d, k * 128:(k + 1) * 128], Hrhs_fn(d)) for d in range(NCH)], R=R)
            if tanh:
                K.op("act", lambda e, p=p, out_ap=out_ap: e.activation(out=out_ap, in_=p.t[:, :], func=AF.Tanh, scale=0.5),
                     R=[(p, None)], W=[(Gbuf, gregs)])
            else:
                K.op("act", lambda e, p=p, out_ap=out_ap: e.activation(out=out_ap, in_=p.t[:, :], func=AF.Sigmoid),
                     R=[(p, None)], W=[(Gbuf, gregs)])

        class Rot:
            def __init__(self, name, shape, dt, n, stack, nreg=1):
                self.b = [sb("%s%d" % (name, j), shape, dt, nreg, stack) for j in range(n)]
                self.i = 0

            def get(self):
                self.i = (self.i + 1) % len(self.b)
                return self.b[self.i]

        for l in range(depth):
            if l == 0:
                with contextlib.ExitStack() as st:
                    XR = Rot("paX", [128, 8, T], F32, 2, st, 8)
                    SQR = Rot("paSQ", [128, 8, T], BF16, 1, st)
                    RSR = Rot("paRS", [128, T], F32, 2, st)
                    HR = Rot("paH", [128, 8, T], BF16, 2, st, 8)
                    for i in range(NT):
                        X = XR.get(); SQ = SQR.get(); RS = RSR.get(); H = HR.get()
                        K.dma("sp", X.t[:, :, :], fm(xin, i * T, T), X, R=[(xin, [i])], W=[(X, None)])
                        norm_to_bf16(X, T, SQ, RS, H, 0, l, "g_mix_pre")
                        K.dma("act", fm(HTd, HALO + i * T, T), H.t[:, :, :], H, R=[(H, None)], W=[(HTd, [i])])
                    K.barrier()
                    K.flush("pa")

            with contextlib.ExitStack() as st:
                WKV = sb("WKV", [128, 8, 2 * D], BF16, 1, st)
                load_w(WKV, w_kv[l].rearrange("(c p) e -> p c e", p=128))
                MX = Rot("p0X", [128, 8, NMEM], F32, 2, st, 8)
                MSQ = Rot("p0SQ", [128, 8, NMEM], BF16, 1, st)
                MRS = Rot("p0RS", [128, NMEM], F32, 1, st)
                MN = Rot("p0MN", [128, 8, NMEM], BF16, 1, st, 8)
                KTs = Rot("p0KT", [128, 8, NMEM], BF16, 2, st)
                Vs = Rot("p0V", [128, 2, D], BF16, 2, st)
                for s in range(nseg):
                    X = MX.get(); SQ = MSQ.get(); RS = MRS.get(); Mn = MN.get(); KTb = KTs.get(); Vb = Vs.get()
                    K.dma("sp", X.t[:, :, :], memT[s].rearrange("(c p) n -> p c n", p=128), X, W=[(X, None)])
                    norm_to_bf16(X, NMEM, SQ, RS, Mn, 0, l, "g_mem")
                    for k in range(NCH):
                        p = ps()
                        mm_group(p, p.t[:, 0:NMEM], [(WKV.t[:, d, k * 128:(k + 1) * 128], Mn.t[:, d, :]) for d in range(NCH)],
                                 R=[(WKV, None), (Mn, None)])
                        K.op("act", lambda e, p=p, k=k, KTb=KTb: e.activation(out=KTb.t[:, k, :], in_=p.t[:, 0:NMEM], func=AF.Copy),
                             R=[(p, None)], W=[(KTb, None)])
                    for mc in range(2):
                        for hv in range(2):
                            p = ps()
                            mm_group(p, p.t[:, :], [(Mn.t[:, d, mc * 128:(mc + 1) * 128], WKV.t[:, d, D + hv * 512:D + (hv + 1) * 512])
                                                    for d in range(NCH)], R=[(WKV, None), (Mn, None)])
                            K.op("act", lambda e, p=p, mc=mc, hv=hv, Vb=Vb: e.activation(
                                out=Vb.t[:, mc, hv * 512:(hv + 1) * 512], in_=p.t[:, :], func=AF.Copy),
                                R=[(p, None)], W=[(Vb, None)])
                    K.dma("sp", KTd.t[s].rearrange("(c p) n -> p c n", p=128), KTb.t[:, :, :], KTb, R=[(KTb, None)], W=[(KTd, [s])])
                    K.dma("sp", Vd.t[s].rearrange("(c p) n -> p c n", p=128), Vb.t[:, :, :], Vb, R=[(Vb, None)], W=[(Vd, [s])])
                K.barrier()
                K.flush("p0_%d" % l)

            with contextlib.ExitStack() as st:
                Wp = sb("Wp", [128, 8, D], BF16, 1, st)
                Wg = sb("Wg0", [128, 8, D], BF16, 1, st)
                PW = sb("PW", [128, 4, 2, 256], BF16, 1, st)
                load_w(Wp, w_in_cols(l, 0))
                load_w(Wg, w_in_cols(l, 3 * D))
                load_w(PW, pool_w[l].rearrange("g (c p) e -> p g c e", p=128))
                HhR = Rot("p1H", [128, 8, TH], BF16, 2, st)
                UR = Rot("p1U", [128, 8, TH], F32, 2, st, 8)
                G0R = Rot("p1G", [128, 8, T], F32, 2, st, 8)
                S2 = sb("p1S2", [128, 6, TH], F32, 1, st)
                S4 = sb("p1S4", [128, 4, TH], F32, 1, st)
                WIN = sb("p1WIN", [128, 8, T], F32, 1, st)
                PB = sb("p1PB", [128, 8, T], BF16, 4, st)
                MR = Rot("p1M", [128, 8, T], F32, 2, st, 8)
                A0, A1 = HALO, HALO + T
                ctx = {}
                tt = lambda e, o, a, b: e.tensor_tensor(out=o, in0=a, in1=b, op=ALU.add)

                def p1s0(i):
                    Hh = HhR.get(); U = UR.get(); G = G0R.get()
                    load_h_halo(Hh, i)
                    proj_halo(Wp, Hh, U)
                    for k in range(NCH):
                        gate_sig(Wg, lambda d, Hh=Hh: Hh.t[:, d, A0:A1], k, G.t[:, k, :], G, [k], R=[(Wg, None), (Hh, None)])
                    ctx[i] = (U, G)

                def p1s1(i):
                    U, G = ctx.pop(i)
                    M = MR.get()
                    K.op("pool", lambda e, U=U: tt(e, WIN.t[:, 0:2, :], U.t[:, 0:2, A0 - 1:A1 - 1], U.t[:, 0:2, A0:A1]),
                         R=[(U, [0, 1])], W=[(WIN, None)])
                    K.op("pool", lambda e, U=U: tt(e, S2.t[:, :, 1:TH], U.t[:, 2:8, 0:TH - 1], U.t[:, 2:8, 1:TH]),
                         R=[(U, [2, 3, 4, 5, 6, 7])], W=[(S2, None)])
                    K.op("pool", lambda e: tt(e, WIN.t[:, 2:4, :], S2.t[:, 0:2, A0 - 1:A1 - 1], S2.t[:, 0:2, A0 + 1:A1 + 1]),
                         R=[(S2, None)], W=[(WIN, None)])
                    K.op("pool", lambda e: tt(e, S4.t[:, :, 2:TH - 2], S2.t[:, 2:6, 1:TH - 3], S2.t[:, 2:6, 3:TH - 1]),
                         R=[(S2, None)], W=[(S4, None)])
                    K.op("pool", lambda e: tt(e, WIN.t[:, 4:6, :], S4.t[:, 0:2, A0 - 2:A1 - 2], S4.t[:, 0:2, A0 + 2:A1 + 2]),
                         R=[(S4, None)], W=[(WIN, None)])
                    K.op("pool", lambda e: tt(e, S2.t[:, 0:2, 4:TH - 4], S4.t[:, 2:4, 2:TH - 6], S4.t[:, 2:4, 6:TH - 2]),
                         R=[(S4, None)], W=[(S2, None)])
                    K.op("pool", lambda e: tt(e, WIN.t[:, 6:8, :], S2.t[:, 0:2, A0 - 4:A1 - 4], S2.t[:, 0:2, A0 + 4:A1 + 4]),
                         R=[(S2, None)], W=[(WIN, None)])
                    edges = []
                    if i % TPS == 0:
                        edges.append((0 if i == 0 else 2, 0))
                    if i % TPS == TPS - 1:
                        edges.append((1 if i == NT - 1 else 3, T - 8))
                    for kind, c0 in edges:
                        ev = CV.t[:, CV_EDGE + kind * 64:CV_EDGE + (kind + 1) * 64].rearrange("p (k c) -> p k c", c=8)
                        K.op("dve", lambda e, ev=ev, c0=c0: e.tensor_tensor(out=WIN.t[:, :, c0:c0 + 8], in0=WIN.t[:, :, c0:c0 + 8],
                                                                            in1=ev, op=ALU.mult),
                             R=[(WIN, None), (CV, None)], W=[(WIN, None)])
                    for g in range(4):
                        K.op("dve", lambda e, g=g, U=U: e.scalar_tensor_tensor(
                            out=PB.t[:, 2 * g:2 * g + 2, :], in0=WIN.t[:, 2 * g:2 * g + 2, :], scalar=1.0 / POOL_W[g],
                            in1=U.t[:, 2 * g:2 * g + 2, A0:A1], op0=ALU.mult, op1=ALU.subtract),
                            R=[(WIN, None), (U, [2 * g, 2 * g + 1])], W=[(PB, [g])])
                    for k in range(NCH):
                        g, co = k // 2, k % 2
                        p = ps()
                        mm_group(p, p.t[:, :], [(PW.t[:, g, cc, co * 128:(co + 1) * 128], PB.t[:, 2 * g + cc, :]) for cc in range(2)],
                                 R=[(PW, None), (PB, [g])])
                        K.op("dve", lambda e, p=p, k=k, G=G, M=M: e.scalar_tensor_tensor(
                            out=M.t[:, k, :], in0=p.t[:, :], scalar=cvc(l, "pool_scale", k), in1=G.t[:, k, :],
                            op0=ALU.mult, op1=ALU.mult),
                            R=[(p, None), (G, [k]), (CV, None)], W=[(M, [k])])
                    K.dma("act", fm(MTd, i * T, T), M.t[:, :, :], M, R=[(M, None)], W=[(MTd, [i])])

                for step in range(NT + 1):
                    if step < NT:
                        p1s0(step)
                    if step >= 1:
                        p1s1(step - 1)
                K.barrier()
                K.flush("p1_%d" % l)

            with contextlib.ExitStack() as st:
                Wq = sb("Wq", [128, 8, D], BF16, 1, st)
                Wg = sb("Wg2", [128, 8, D], BF16, 1, st)
                load_w(Wq, w_in_cols(l, 2 * D))
                load_w(Wg, w_in_cols(l, 5 * D))
                KTR = Rot("p2KT", [128, 8, NMEM], BF16, 2, st)
                VR = Rot("p2V", [128, 2, D], BF16, 2, st)
                HmR = Rot("p2H", [128, 8, T], BF16, 3, st)
                MR = Rot("p2M", [128, 8, T], F32, 2, st, 8)
                QR = Rot("p2Q", [128, 8, T], BF16, 2, st, 8)
                MbR = Rot("p2Mb", [128, 8, T], BF16, 2, st, 8)
                ER = Rot("p2E", [128, 2, T], BF16, 3, st)
                RR = Rot("p2R", [128, T], F32, 3, st)
                T1R = Rot("p2T1", [128, T], F32, 3, st)
                GR = Rot("p2G", [128, T], F32, 3, st)
                ctx = {}
                kvs = {}

                def p2s0(i):
                    Hm = HmR.get(); Q = QR.get()
                    K.dma("sp", Hm.t[:, :, :], fm(HTd, HALO + i * T, T), Hm, R=[(HTd, [i])], W=[(Hm, None)])
                    for k in range(NCH):
                        p = ps()
                        mm_group(p, p.t[:, :], [(Wq.t[:, d, k * 128:(k + 1) * 128], Hm.t[:, d, :]) for d in range(NCH)],
                                 R=[(Wq, None), (Hm, None)])
                        K.op("act", lambda e, p=p, k=k, Q=Q: e.activation(out=Q.t[:, k, :], in_=p.t[:, :], func=AF.Copy),
                             R=[(p, None)], W=[(Q, [k])])
                    ctx[i] = (Hm, Q)

                def p2s1(i):
                    Hm, Q = ctx.pop(i)
                    M = MR.get()
                    s = i // TPS
                    if i % TPS == 0:
                        KTb = KTR.get(); Vb = VR.get()
                        K.dma("sp", KTb.t[:, :, :], KTd.t[s].rearrange("(c p) n -> p c n", p=128), KTb, R=[(KTd, [s])], W=[(KTb, None)])
                        K.dma("sp", Vb.t[:, :, :], Vd.t[s].rearrange("(c p) n -> p c n", p=128), Vb, R=[(Vd, [s])], W=[(Vb, None)])
                        kvs["kt"], kvs["v"] = KTb, Vb
                    KTb, Vb = kvs["kt"], kvs["v"]
                    K.dma("sp", M.t[:, :, :], fm(MTd, i * T, T), M, R=[(MTd, [i])], W=[(M, None)])

                    def scores(h):
                        E = ER.get()
                        for mc in range(2):
                            p = ps()
                            mm_group(p, p.t[:, :], [(KTb.t[:, 2 * h + dc, mc * 128:(mc + 1) * 128], Q.t[:, 2 * h + dc, :]) for dc in range(2)],
                                     R=[(KTb, None), (Q, [2 * h, 2 * h + 1])])
                            K.op("act", lambda e, p=p, mc=mc, E=E: e.activation(out=E.t[:, mc, :], in_=p.t[:, :], func=AF.Exp, scale=1.0 / 16.0),
                                 R=[(p, None)], W=[(E, None)])
                        return E

                    def rest(h, E):
                        Rc = RR.get()
                        p = ps()
                        mm_group(p, p.t[:, :], [(ONES2.t[:, :], E.t[:, mc, :]) for mc in range(2)], R=[(ONES2, None), (E, None)])
                        K.op("dve", lambda e, p=p, Rc=Rc: e.reciprocal(out=Rc.t[:, :], in_=p.t[:, :]), R=[(p, None)], W=[(Rc, None)])
                        for dvc in range(2):
                            k = 2 * h + dvc
                            T1 = T1R.get(); G = GR.get()
                            p = ps()
                            mm_group(p, p.t[:, :], [(Vb.t[:, mc, k * 128:(k + 1) * 128], E.t[:, mc, :]) for mc in range(2)],
                                     R=[(Vb, None), (E, None)])
                            K.op("dve", lambda e, p=p, T1=T1, Rc=Rc: e.tensor_tensor(out=T1.t[:, :], in0=p.t[:, :], in1=Rc.t[:, :], op=ALU.mult),
                                 R=[(p, None), (Rc, None)], W=[(T1, None)])
                            gate_sig(Wg, lambda d, Hm=Hm: Hm.t[:, d, :], k, G.t[:, :], G, None, R=[(Wg, None), (Hm, None)], tanh=True)
                            K.op("dve", lambda e, T1=T1, G=G: e.scalar_tensor_tensor(out=T1.t[:, :], in0=G.t[:, :], scalar=1.0, in1=T1.t[:, :],
                                                                                     op0=ALU.add, op1=ALU.mult),
                                 R=[(T1, None), (G, None)], W=[(T1, None)])
                            K.op("pool", lambda e, T1=T1, M=M, Mb=Mb, k=k: e.tensor_tensor(out=Mb.t[:, k, :], in0=M.t[:, k, :], in1=T1.t[:, :], op=ALU.add),
                                 R=[(T1, None), (M, [k])], W=[(Mb, [k])])

                    Mb = MbR.get()
                    Es = {0: scores(0)}
                    for h in range(4):
                        if h + 1 < 4:
                            Es[h + 1] = scores(h + 1)
                        rest(h, Es.pop(h))
                    K.dma("act", fm(MTbd, i * T, T), Mb.t[:, :, :], Mb, R=[(Mb, None)], W=[(MTbd, [i])])

                for step in range(NT + 1):
                    if step < NT:
                        p2s0(step)
                    if step >= 1:
                        p2s1(step - 1)
                K.barrier()
                K.flush("p2_%d" % l)

            with contextlib.ExitStack() as st:
                Wl = sb("Wl", [128, 8, D], BF16, 1, st)
                Wg = sb("Wg1", [128, 8, D], BF16, 1, st)
                WA = sb("WA", [128, 2, 8, 128], BF16, 1, st)
                WX = sb("WX", [128, 2, 8, 128], BF16, 1, st)
                load_w(Wl, w_in_cols(l, D))
                load_w(Wg, w_in_cols(l, 4 * D))
                load_w(WA, lru_wa[l].rearrange("r b i j -> i r b j"))
                load_w(WX, lru_wx[l].rearrange("r b i j -> i r b j"))
                HhR = Rot("p3H", [128, 8, TH], BF16, 2, st)
                UL = sb("p3U", [128, 8, TH], F32, 8, st)
                XCR = Rot("p3XC", [128, 8, T], F32, 2, st, 8)
                XCbR = Rot("p3XCb", [128, 8, T], BF16, 2, st, 8)
                MLR = Rot("p3ML", [128, 8, T], BF16, 1, st, 8)
                GBR = Rot("p3GB", [128, 8, T], BF16, 2, st, 8)
                RR = Rot("p3r", [128, T], F32, 1, st)
                IXR = Rot("p3ix", [128, T], F32, 2, st)
                AR = [Rot("p3a%d" % r, [128, T], F32, 3, st) for r in range(2)]
                A2R = Rot("p3a2", [128, 2, T], F32, 2, st)
                BR = [Rot("p3b%d" % r, [128, T], F32, 2, st) for r in range(2)]
                HFR = Rot("p3hf", [128, T], F32, 2, st)
                HBR = Rot("p3hb", [128, T], F32, 2, st)
                ABR = Rot("p3ab", [128, T], F32, 2, st)
                GR = Rot("p3G", [128, T], F32, 3, st)
                A0, A1 = HALO, HALO + T
                ctx = {}

                def p3new(i):
                    Hh = HhR.get(); XC = XCR.get(); XCb = XCbR.get()
                    load_h_halo(Hh, i)
                    proj_halo(Wl, Hh, UL)
                    ctx[i] = dict(Hh=Hh, XC=XC, XCb=XCb)

                def p3conv(i, k):
                    XC = ctx[i]["XC"]
                    K.op("dve", lambda e, k=k, XC=XC: e.tensor_scalar(out=XC.t[:, k, :], in0=UL.t[:, k, A0 - 2:A1 - 2],
                                                                      scalar1=cvc(l, "conv_w", k), scalar2=cvc(l, "conv_b", k),
                                                                      op0=ALU.mult, op1=ALU.add),
                         R=[(UL, [k]), (CV, None)], W=[(XC, [k])])
                    for j in range(1, 4):
                        K.op("dve", lambda e, k=k, j=j, XC=XC: e.scalar_tensor_tensor(
                            out=XC.t[:, k, :], in0=UL.t[:, k, A0 - 2 + j:A1 - 2 + j], scalar=cvc(l, "conv_w", 8 * j + k),
                            in1=XC.t[:, k, :], op0=ALU.mult, op1=ALU.add),
                            R=[(UL, [k]), (XC, [k]), (CV, None)], W=[(XC, [k])])

                def p3xcb(i, k):
                    XC, XCb = ctx[i]["XC"], ctx[i]["XCb"]
                    K.op("act", lambda e, k=k, XC=XC, XCb=XCb: e.activation(out=XCb.t[:, k, :], in_=XC.t[:, k, :], func=AF.Copy),
                         R=[(XC, [k])], W=[(XCb, [k])])

                def p3front(i, k):
                    c = ctx[i]
                    Hh, XC, XCb = c["Hh"], c["XC"], c["XCb"]
                    G = GR.get()
                    gate_sig(Wg, lambda d, Hh=Hh: Hh.t[:, d, A0:A1], k, G.t[:, :], G, None, R=[(Wg, None), (Hh, None)], tanh=True)
                    K.op("dve", lambda e, G=G: e.tensor_scalar(out=G.t[:, :], in0=G.t[:, :], scalar1=1.0, scalar2=None, op0=ALU.add),
                         R=[(G, None)], W=[(G, None)])
                    A2 = A2R.get()
                    AB2 = []
                    IXs = []
                    for r in range(2):
                        Rt = RR.get(); IX = IXR.get(); At = AR[r].get()
                        col = l * 16 + r * 8 + k
                        cc = CC.t[:, col:col + 1]
                        hc = HC.t[:, col:col + 1]
                        hba = HBB.t[:, l * 32 + r * 8 + k:l * 32 + r * 8 + k + 1]
                        hbx = HBB.t[:, l * 32 + 16 + r * 8 + k:l * 32 + 16 + r * 8 + k + 1]
                        p = ps()
                        mm_group(p, p.t[:, :], [(WA.t[:, r, k, :], XCb.t[:, k, :])], R=[(WA, None), (XCb, [k])])
                        K.op("act", lambda e, p=p, Rt=Rt, hba=hba: e.activation(out=Rt.t[:, :], in_=p.t[:, :], func=AF.Tanh, bias=hba, scale=0.5),
                             R=[(p, None), (HBB, None)], W=[(Rt, None)])
                        px = ps()
                        mm_group(px, px.t[:, :], [(WX.t[:, r, k, :], XCb.t[:, k, :])], R=[(WX, None), (XCb, [k])])
                        K.op("act", lambda e, px=px, hbx=hbx: e.activation(out=px.t[:, :], in_=px.t[:, :], func=AF.Tanh, bias=hbx, scale=0.5),
                             R=[(px, None), (HBB, None)], W=[(px, None)])
                        K.op("dve", lambda e, px=px, IX=IX, XC=XC, k=k: e.scalar_tensor_tensor(
                            out=IX.t[:, :], in0=px.t[:, :], scalar=1.0, in1=XC.t[:, k, :], op0=ALU.add, op1=ALU.mult),
                            R=[(px, None), (XC, [k])], W=[(IX, None)])
                        K.op("act", lambda e, At=At, Rt=Rt, hc=hc: e.activation(out=At.t[:, :], in_=Rt.t[:, :], func=AF.Exp, bias=hc, scale=hc),
                             R=[(Rt, None), (HC, None)], W=[(At, None)])
                        K.op("act", lambda e, A2=A2, Rt=Rt, cc=cc, r=r: e.activation(out=A2.t[:, r, :], in_=Rt.t[:, :], func=AF.Exp, bias=cc, scale=cc),
                             R=[(Rt, None), (CC, None)], W=[(A2, None)])
                        AB2.append(At)
                        IXs.append(IX)
                    K.op("act", lambda e, A2=A2: e.activation(out=A2.t[:, :, :], in_=A2.t[:, :, :], func=AF.Sqrt, bias=1.0 / 16.0, scale=-1.0 / 16.0),
                         R=[(A2, None)], W=[(A2, None)])
                    Bs = []
                    for r in range(2):
                        Bt = BR[r].get()
                        K.op("pool", lambda e, IX=IXs[r], A2=A2, Bt=Bt, r=r: e.tensor_tensor(out=Bt.t[:, :], in0=IX.t[:, :], in1=A2.t[:, r, :], op=ALU.mult),
                             R=[(IXs[r], None), (A2, None)], W=[(Bt, None)])
                        Bs.append(Bt)
                    c[("f", k)] = (G, [(AB2[0], Bs[0]), (AB2[1], Bs[1])])

                def p3back(i, k):
                    c = ctx[i]
                    ML, GB = c["ML"], c["GB"]
                    G, AB2 = c.pop(("f", k))
                    HF = HFR.get(); HB = HBR.get(); AB = ABR.get()
                    (Af, Bf), (Ab, Bb) = AB2
                    K.op("dve", lambda e, HF=HF, Af=Af, Bf=Bf, k=k: e.tensor_tensor_scan(
                        out=HF.t[:, :], data0=Af.t[:, :], data1=Bf.t[:, :], initial=CF.t[:, k:k + 1], op0=ALU.mult, op1=ALU.add),
                        R=[(Af, None), (Bf, None), (CF, None)], W=[(HF, None)])
                    K.op("dve", lambda e, HB=HB, Ab=Ab, Bb=Bb: e.tensor_tensor_scan(
                        out=HB.t[:, ::-1], data0=Ab.t[:, ::-1], data1=Bb.t[:, ::-1], initial=0.0, op0=ALU.mult, op1=ALU.add),
                        R=[(Ab, None), (Bb, None)], W=[(HB, None)])
                    K.op("dve", lambda e, AB=AB, Ab=Ab: e.tensor_tensor_scan(
                        out=AB.t[:, ::-1], data0=Ab.t[:, ::-1], data1=ZER.t[:, ::-1], initial=1.0, op0=ALU.mult, op1=ALU.add),
                        R=[(Ab, None), (ZER, None)], W=[(AB, None)])
                    K.op("pool", lambda e, HF=HF, k=k: e.tensor_copy(out=CF.t[:, k:k + 1], in_=HF.t[:, T - 1:T]),
                         R=[(HF, None)], W=[(CF, None)])
                    K.op("pool", lambda e, HB=HB, k=k, i=i: e.tensor_copy(out=HBS.t[:, k, i:i + 1], in_=HB.t[:, 0:1]),
                         R=[(HB, None)], W=[(HBS, None)])
                    K.op("pool", lambda e, HF=HF, HB=HB: e.tensor_tensor(out=HF.t[:, :], in0=HF.t[:, :], in1=HB.t[:, :], op=ALU.add),
                         R=[(HF, None), (HB, None)], W=[(HF, None)])
                    K.op("pool", lambda e, AB=AB, k=k, i=i: e.tensor_copy(out=PBS.t[:, k, i:i + 1], in_=AB.t[:, 0:1]),
                         R=[(AB, None)], W=[(PBS, None)])
                    K.op("pool", lambda e, AB=AB, G=G, GB=GB, k=k: e.tensor_tensor(out=GB.t[:, k, :], in0=AB.t[:, :], in1=G.t[:, :], op=ALU.mult),
                         R=[(AB, None), (G, None)], W=[(GB, [k])])
                    K.op("pool", lambda e, HF=HF, G=G, ML=ML, k=k: e.tensor_tensor(out=ML.t[:, k, :], in0=HF.t[:, :], in1=G.t[:, :], op=ALU.mult),
                         R=[(HF, None), (G, None)], W=[(ML, [k])])

                for step in range(NT + 1):
                    inew = step if step < NT else None
                    iold = step - 1 if step >= 1 else None
                    if inew is not None:
                        p3new(inew)
                    if iold is not None:
                        c = ctx[iold]
                        c["ML"] = MLR.get(); c["GB"] = GBR.get()
                        if iold == 0:
                            K.op("dve", lambda e: e.memset(CF.t[:, :], 0.0), W=[(CF, None)])
                        elif iold % TPS == 0:
                            K.op("dve", lambda e: e.tensor_scalar(out=CF.t[:, :], in0=CF.t[:, :], scalar1=LINK(), scalar2=None, op0=ALU.mult),
                                 R=[(CF, None), (CV, None)], W=[(CF, None)])
                    for k in range(NCH):
                        if inew is not None:
                            p3conv(inew, k)
                        if iold is not None:
                            p3front(iold, k)
                            if k >= 1:
                                p3back(iold, k - 1)
                        if inew is not None:
                            p3xcb(inew, k)
                    if iold is not None:
                        p3back(iold, NCH - 1)
                        c = ctx.pop(iold)
                        K.dma("pool", fm(MLd, iold * T, T), c["ML"].t[:, :, :], c["ML"], R=[(c["ML"], None)], W=[(MLd, [iold])])
                        K.dma("pool", fm(GABbd, iold * T, T), c["GB"].t[:, :, :], c["GB"], R=[(c["GB"], None)], W=[(GABbd, [iold])])
                K.op("dve", lambda e: e.memset(CBK.t[:, :, NT - 1:NT], 0.0), W=[(CBK, None)])
                for i in range(NT - 2, -1, -1):
                    K.op("dve", lambda e, i=i: e.tensor_tensor(out=TMPC.t[:, :], in0=PBS.t[:, :, i + 1], in1=CBK.t[:, :, i + 1], op=ALU.mult),
                         R=[(PBS, None), (CBK, None)], W=[(TMPC, None)])
                    K.op("dve", lambda e, i=i: e.tensor_tensor(out=CBK.t[:, :, i], in0=TMPC.t[:, :], in1=HBS.t[:, :, i + 1], op=ALU.add),
                         R=[(TMPC, None), (HBS, None)], W=[(CBK, None)])
                    if (i + 1) % TPS == 0:
                        K.op("dve", lambda e, i=i: e.tensor_scalar(out=CBK.t[:, :, i], in0=CBK.t[:, :, i], scalar1=LINK(), scalar2=None, op0=ALU.mult),
                             R=[(CBK, None), (CV, None)], W=[(CBK, None)])
                K.barrier()
                K.flush("p3_%d" % l)

            with contextlib.ExitStack() as st:
                Wo = sb("Wo", [128, 8, D], BF16, 1, st)
                load_w(Wo, w_out[l].rearrange("(c p) e -> p c e", p=128))
                MR = Rot("5aM", [128, 8, T], BF16, 2, st, 8)
                GBR = Rot("5aGB", [128, 8, T], BF16, 2, st, 8)
                MLR5 = Rot("5aML", [128, 8, T], BF16, 2, st)
                XR = Rot("5aX", [128, 8, T], F32, 1, st, 8)
                MB = sb("5aMB", [128, 8, T], BF16, 8, st)
                OR_ = Rot("5aO", [128, 8, T], F32, 2, st, 8)
                SQAR = Rot("5aSQa", [128, 8, T], BF16, 2, st)
                SQB = sb("5aSQb", [128, 8, T], BF16, 8, st)
                RSR = Rot("5aRS", [128, T], F32, 2, st)
                TR = Rot("5aT", [128, T], F32, 3, st)
                xsrc = xin if l == 0 else XTd
                ctx = {}

                def p5aload(i):
                    M = MR.get(); GB = GBR.get(); ML = MLR5.get()
                    K.dma("sp", ML.t[:, :, :], fm(MLd, i * T, T), ML, R=[(MLd, [i])], W=[(ML, None)])
                    K.dma("sp", M.t[:, :, :], fm(MTbd, i * T, T), M, R=[(MTbd, [i])], W=[(M, None)])
                    K.dma("sp", GB.t[:, :, :], fm(GABbd, i * T, T), GB, R=[(GABbd, [i])], W=[(GB, None)])
                    ctx[("ld", i)] = (M, GB, ML)

                def p5as0(i):
                    M, GB, ML = ctx.pop(("ld", i))
                    O = OR_.get(); SQ = SQAR.get()
                    for k in range(NCH):
                        K.op("dve", lambda e, k=k, M=M, GB=GB, i=i: e.scalar_tensor_tensor(
                            out=MB.t[:, k, :], in0=GB.t[:, k, :], scalar=CBK.t[:, k, i:i + 1], in1=M.t[:, k, :],
                            op0=ALU.mult, op1=ALU.add),
                            R=[(GB, [k]), (M, [k]), (CBK, None)], W=[(MB, [k])])
                    for k in range(NCH):
                        p = ps()
                        mm_group(p, p.t[:, :], [(Wo.t[:, d, k * 128:(k + 1) * 128], MB.t[:, d, :]) for d in range(NCH)]
                                 + [(Wo.t[:, d, k * 128:(k + 1) * 128], ML.t[:, d, :]) for d in range(NCH)],
                                 R=[(Wo, None), (MB, None), (ML, None)])
                        K.op("act", lambda e, p=p, k=k, O=O: e.activation(out=O.t[:, k, :], in_=p.t[:, :], func=AF.Copy),
                             R=[(p, None)], W=[(O, [k])])
                    K.op("act", lambda e, O=O, SQ=SQ: e.activation(out=SQ.t[:, :, :], in_=O.t[:, :, :], func=AF.Square), R=[(O, None)], W=[(SQ, None)])
                    ctx[i] = (O, SQ)

                def p5as1a(i):
                    O, SQ = ctx.pop(i)
                    X = XR.get(); RS = RSR.get()
                    K.dma("sp", X.t[:, :, :], fm(xsrc, i * T, T), X, R=[(xsrc, [i])], W=[(X, None)])
                    rstd_from_sq(SQ, T, RS)
                    for k in range(NCH):
                        Tt = TR.get()
                        K.op("dve", lambda e, k=k, Tt=Tt, RS=RS, O=O: e.scalar_tensor_tensor(
                            out=Tt.t[:, :], in0=O.t[:, k, :], scalar=cvc(l, "g_mix_post", k), in1=RS.t[:, :], op0=ALU.mult, op1=ALU.mult),
                            R=[(O, [k]), (RS, None), (CV, None)], W=[(Tt, None)])
                        K.op("pool", lambda e, k=k, Tt=Tt, X=X: e.tensor_tensor(out=X.t[:, k, :], in0=X.t[:, k, :], in1=Tt.t[:, :], op=ALU.add),
                             R=[(Tt, None), (X, [k])], W=[(X, [k])])
                    K.dma("act", fm(XTd, i * T, T), X.t[:, :, :], X, R=[(X, None)], W=[(XTd, [i])])
                    K.op("act", lambda e, X=X: e.activation(out=SQB.t[:, :, :], in_=X.t[:, :, :], func=AF.Square),
                         R=[(X, None)], W=[(SQB, None)])
                    ctx[("b", i)] = X

                def p5as1b(i):
                    X = ctx.pop(("b", i))
                    RS2 = RSR.get()
                    rstd_from_sq(SQB, T, RS2)
                    for k in range(NCH):
                        K.op("dve", lambda e, k=k, X=X, RS2=RS2: e.scalar_tensor_tensor(
                            out=SQB.t[:, k, :], in0=X.t[:, k, :], scalar=cvc(l, "g_mlp_pre", k),
                            in1=RS2.t[:, :], op0=ALU.mult, op1=ALU.mult),
                            R=[(X, [k]), (RS2, None), (CV, None)], W=[(SQB, None)])
                    K.dma("act", fm(H2Td, i * T, T), SQB.t[:, :, :], SQB, R=[(SQB, None)], W=[(H2Td, [i])])

                p5aload(0)
                for step in range(NT + 1):
                    if step >= 1:
                        p5as1a(step - 1)
                    if step + 1 < NT:
                        p5aload(step + 1)
                    if step < NT:
                        p5as0(step)
                    if step >= 1:
                        p5as1b(step - 1)
                K.barrier()
                K.flush("p5a_%d" % l)

            with contextlib.ExitStack() as st:
                W1 = sb("W1", [128, 8, 4 * D], BF16, 1, st)
                load_w(W1, mlp_w1[l].rearrange("(c p) e -> p c e", p=128))
                H2R = Rot("5bH2", [128, 8, T], BF16, 2, st)
                HDR = Rot("5bHD", [128, 8, T], BF16, 3, st)
                RLR = Rot("5bRL", [128, T], F32, 3, st)
                for i in range(NT):
                    H2 = H2R.get()
                    K.dma("sp", H2.t[:, :, :], fm(H2Td, i * T, T), H2, R=[(H2Td, [i])], W=[(H2, None)])
                    for fq in range(4):
                        HD = HDR.get()
                        for fk in range(8):
                            f = fq * 8 + fk
                            RL = RLR.get()
                            p = ps()
                            mm_group(p, p.t[:, :], [(W1.t[:, d, f * 128:(f + 1) * 128], H2.t[:, d, :]) for d in range(NCH)],
                                     R=[(W1, None), (H2, None)])
                            K.op("act", lambda e, p=p, RL=RL: e.activation(out=RL.t[:, :], in_=p.t[:, :], func=AF.Relu),
                                 R=[(p, None)], W=[(RL, None)])
                            K.op("act", lambda e, RL=RL, HD=HD, fk=fk: e.activation(out=HD.t[:, fk, :], in_=RL.t[:, :], func=AF.Square),
                                 R=[(RL, None)], W=[(HD, None)])
                        K.dma("act", HIDd.t[fq * D:(fq + 1) * D, i * T:(i + 1) * T].rearrange("(c p) n -> p c n", p=128),
                              HD.t[:, :, :], HD, R=[(HD, None)], W=[(HIDd, [i])])
                K.barrier()
                K.flush("p5b_%d" % l)

            with contextlib.ExitStack() as st:
                W2 = sb("W2", [128, 32, D], BF16, 1, st)
                load_w(W2, mlp_w2[l].rearrange("(c p) e -> p c e", p=128))
                HDR = Rot("5cHD", [128, 8, T], BF16, 8, st)
                XR = Rot("5cX", [128, 8, T], F32, 1, st, 8)
                YR = Rot("5cY", [128, 8, T], F32, 1, st, 8)
                SQA = sb("5cSQa", [128, 8, T], BF16, 1, st)
                SQBH = sb("5cSQbH", [128, 8, T], BF16, 8, st)
                RSR = Rot("5cRS", [128, T], F32, 2, st)
                TR = Rot("5cT", [128, T], F32, 2, st)
                last = (l == depth - 1)
                ctx = {}

                def p5cload(i):
                    HD = []
                    for fq in range(4):
                        b = HDR.get()
                        K.dma("sp", b.t[:, :, :], HIDd.t[fq * D:(fq + 1) * D, i * T:(i + 1) * T].rearrange("(c p) n -> p c n", p=128),
                              b, R=[(HIDd, [i])], W=[(b, None)])
                        HD.append(b)
                    ctx[("hd", i)] = HD

                def p5cs0(i):
                    Y = YR.get()
                    HD = ctx.pop(("hd", i))
                    for k in range(NCH):
                        p = ps()
                        mm_group(p, p.t[:, :], [(W2.t[:, f, k * 128:(k + 1) * 128], HD[f // 8].t[:, f % 8, :]) for f in range(32)],
                                 R=[(W2, None)] + [(b, None) for b in HD])
                        K.op("act", lambda e, p=p, k=k, Y=Y: e.activation(out=Y.t[:, k, :], in_=p.t[:, :], func=AF.Copy),
                             R=[(p, None)], W=[(Y, [k])])
                    K.op("act", lambda e, Y=Y: e.activation(out=SQA.t[:, :, :], in_=Y.t[:, :, :], func=AF.Square), R=[(Y, None)], W=[(SQA, None)])
                    ctx[i] = Y

                def p5cs1a(i):
                    Y = ctx.pop(i)
                    X = XR.get(); RS = RSR.get()
                    K.dma("sp", X.t[:, :, :], fm(XTd, i * T, T), X, R=[(XTd, [i])], W=[(X, None)])
                    rstd_from_sq(SQA, T, RS)
                    for k in range(NCH):
                        Tt = TR.get()
                        K.op("dve", lambda e, k=k, Tt=Tt, RS=RS, Y=Y: e.scalar_tensor_tensor(
                            out=Tt.t[:, :], in0=Y.t[:, k, :], scalar=cvc(l, "g_mlp_post", k), in1=RS.t[:, :], op0=ALU.mult, op1=ALU.mult),
                            R=[(Y, [k]), (RS, None), (CV, None)], W=[(Tt, None)])
                        K.op("pool", lambda e, k=k, Tt=Tt, X=X: e.tensor_tensor(out=X.t[:, k, :], in0=X.t[:, k, :], in1=Tt.t[:, :], op=ALU.add),
                             R=[(Tt, None), (X, [k])], W=[(X, [k])])
                    if last:
                        K.dma("act", fm(yout, i * T, T), X.t[:, :, :], X, R=[(X, None)], W=[(yout, [i])])
                    else:
                        K.dma("act", fm(XTd, i * T, T), X.t[:, :, :], X, R=[(X, None)], W=[(XTd, [i])])
                        K.op("act", lambda e, X=X: e.activation(out=SQBH.t[:, :, :], in_=X.t[:, :, :], func=AF.Square),
                             R=[(X, None)], W=[(SQBH, None)])
                    ctx[("b", i)] = X

                def p5cs1b(i):
                    X = ctx.pop(("b", i))
                    if last:
                        return
                    RS2 = RSR.get()
                    rstd_from_sq(SQBH, T, RS2)
                    for k in range(NCH):
                        K.op("dve", lambda e, k=k, X=X, RS2=RS2: e.scalar_tensor_tensor(
                            out=SQBH.t[:, k, :], in0=X.t[:, k, :], scalar=cvc(l + 1, "g_mix_pre", k),
                            in1=RS2.t[:, :], op0=ALU.mult, op1=ALU.mult),
                            R=[(X, [k]), (RS2, None), (CV, None)], W=[(SQBH, None)])
                    K.dma("act", fm(HTd, HALO + i * T, T), SQBH.t[:, :, :], SQBH, R=[(SQBH, None)], W=[(HTd, [i])])

                p5cload(0)
                for step in range(NT + 1):
                    if step >= 1:
                        p5cs1a(step - 1)
                    if step + 1 < NT:
                        p5cload(step + 1)
                    if step < NT:
                        p5cs0(step)
                    if step >= 1:
                        p5cs1b(step - 1)
                K.barrier()
                K.flush("p5c_%d" % l)
    return nc


def _edge_tables(link):
    tab = np.ones((4, 8, 8), np.float32)
    for g, w in enumerate(POOL_W):
        h = w // 2
        S = 1 << 20
        first = np.array([w / float((t + h) - max(t - h, 0)) for t in range(8)], np.float32)
        last = np.array([w / float(min(t + h, S) - (t - h)) for t in range(S - 8, S)], np.float32)
        for k in (2 * g, 2 * g + 1):
            tab[0, k] = first
            tab[1, k] = last
            tab[2, k] = 1.0 if link else first
            tab[3, k] = 1.0 if link else last
    return tab


def _build_cv(P, link):
    cv = np.zeros((128, NCV), np.float32)

    def put(l, name, v):
        v = np.asarray(v, np.float32).reshape(-1, 8, 128)
        o = l * CPL + CV_OFF[name]
        n = v.shape[0]
        cv[:, o:o + 8 * n] = v.transpose(2, 0, 1).reshape(128, 8 * n)
    for l in range(DEPTH):
        put(l, "g_mix_pre", P["norm_mix_pre"][l])
        put(l, "g_mix_post", P["norm_mix_post"][l])
        put(l, "g_mem", P["norm_mem"][l])
        put(l, "pool_scale", P["pool_scale"][l])
        put(l, "conv_w", P["conv_w"][l])
        put(l, "conv_b", P["conv_b"][l])
        put(l, "lru_ba", P["lru_ba"][l])
        put(l, "lru_bx", P["lru_bx"][l])
        put(l, "lru_lam", P["lru_lambda"][l])
        put(l, "g_mlp_pre", P["norm_mlp_pre"][l])
        put(l, "g_mlp_post", P["norm_mlp_post"][l])
    cv[:, CV_EDGE:CV_EDGE + 256] = _edge_tables(link).reshape(1, 256)
    cv[:, CV_LINK] = 1.0 if link else 0.0
    return cv


_NC_CACHE = {}


def kernel(**inputs):
    P = {k: np.asarray(v) for k, v in inputs.items()}
    xs, xp = P["x_sample"], P["x_prompt"]
    ms, mp = P["mem_sample"], P["mem_prompt"]
    n = 8
    wnames = ["w_in", "pool_w", "lru_wa", "lru_wx", "w_kv", "w_out", "mlp_w1", "mlp_w2"]
    weights = {k: np.ascontiguousarray(P[k], dtype=np.float32) for k in wnames}
    cv_s = _build_cv(P, True)
    cv_p = _build_cv(P, False)
    in_maps = []
    for c in range(n):
        if c < 2:
            xT = np.ascontiguousarray(xs[c].T)
            memT = np.ascontiguousarray(np.stack([ms[c].T] * 4))
            cv = cv_s
        elif c in (4, 5):
            j0 = 4 * (c - 4)
            xT = np.ascontiguousarray(np.concatenate([xp[j0 + j].T for j in range(4)], axis=1))
            memT = np.ascontiguousarray(np.stack([mp[j0 + j].T for j in range(4)]))
            cv = cv_p
        else:
            xT = np.zeros((D, 4 * SEG), np.float32)
            memT = np.zeros((4, D, NMEM), np.float32)
            cv = cv_p
        m = {"xT": xT.astype(np.float32), "memT": memT.astype(np.float32), "cv": cv}
        m.update(weights)
        in_maps.append(m)
    if "nc" not in _NC_CACHE:
        _NC_CACHE["nc"] = build_program()
    res = run_bass_kernel_spmd(_NC_CACHE["nc"], in_maps, core_ids=list(range(n)))
    y_sample = np.stack([np.ascontiguousarray(res.results[c]["yT"].T) for c in range(2)]).astype(np.float32)
    yp = []
    for c in (4, 5):
        yT = res.results[c]["yT"]
        for j in range(4):
            yp.append(np.ascontiguousarray(yT[:, j * SEG:(j + 1) * SEG].T))
    y_prompt = np.stack(yp).astype(np.float32)
    return (y_prompt, y_sample)
```

```python
import contextlib
import numpy as np
import concourse.bass as bass
import concourse.mybir as mybir
from concourse.bass_utils import run_bass_kernel_spmd

F32 = mybir.dt.float32
BF16 = mybir.dt.bfloat16
AF = mybir.ActivationFunctionType
ALU = mybir.AluOpType

D = 1024
NCH = 8
T = 512
HALO = 8
TH = T + 2 * HALO
SEG = 4096
TPS = SEG // T
NMEM = 256
DEPTH = 4
EPS = 1e-6
POOL_W = (2, 4, 8, 16)

CV_NAMES = [("g_mix_pre", 8), ("g_mix_post", 8), ("g_mem", 8), ("pool_scale", 8), ("conv_w", 32),
            ("conv_b", 8), ("lru_ba", 16), ("lru_bx", 16), ("lru_lam", 16), ("g_mlp_pre", 8),
            ("g_mlp_post", 8)]
CV_OFF = {}
_o = 0
for _n, _c in CV_NAMES:
    CV_OFF[_n] = _o
    _o += _c
CPL = _o
CV_EDGE = DEPTH * CPL
CV_LINK = CV_EDGE + 256
NCV = CV_LINK + 1


class Reg:
    __slots__ = ("lw", "rd")

    def __init__(self):
        self.lw = None
        self.rd = []


class Buf:
    def __init__(self, name, t, nreg=1, space="sbuf"):
        self.name = name
        self.t = t
        self.regs = [Reg() for _ in range(nreg)]
        self.space = space
        self.sem = None
        self.dcount = 0
        self.last_dma = None


class Kern:
    ENGS = ("pe", "act", "dve", "pool", "sp")

    def __init__(self, nc):
        self.nc = nc
        self.prog = {e: [] for e in self.ENGS}
        self.cnt = {e: 0 for e in self.ENGS}
        self.waited = {e: {} for e in self.ENGS}
        self.semh = {}
        for e in ("pe", "act", "dve", "pool"):
            self.semh[e] = nc.alloc_semaphore(name="sem_" + e)
        self.dma_sems = []
        self.free_slots = []
        self.sw_events = []
        self.nslots = 0
        self.phase_bufs = []
        self.nbuf = 0
        self.ps_rr = 0

    def _deps(self, eng, R, W):
        deps = []
        for buf, regs in R:
            for r in (range(len(buf.regs)) if regs is None else regs):
                g = buf.regs[r]
                if g.lw is not None:
                    deps.append(g.lw)
        for buf, regs in W:
            for r in (range(len(buf.regs)) if regs is None else regs):
                g = buf.regs[r]
                if g.lw is not None:
                    deps.append(g.lw)
                deps.extend(g.rd)
        return deps

    def _commit(self, ev, R, W):
        for buf, regs in R:
            for r in (range(len(buf.regs)) if regs is None else regs):
                buf.regs[r].rd.append(ev)
        for buf, regs in W:
            for r in (range(len(buf.regs)) if regs is None else regs):
                g = buf.regs[r]
                g.lw = ev
                g.rd = []

    def _waits(self, eng, deps):
        need = {}
        for key, val in deps:
            if need.get(key, 0) < val:
                need[key] = val
        out = []
        for key, val in need.items():
            if key == eng and eng == "pe":
                continue
            if self.waited[eng].get(key, 0) >= val:
                continue
            self.waited[eng][key] = val
            out.append((self.semh[key], val))
        return out

    def op(self, eng, fn, R=(), W=()):
        deps = self._deps(eng, R, W)
        waits = self._waits(eng, deps)
        self.cnt[eng] += 1
        ev = (eng, self.cnt[eng])
        sem = self.semh[eng]

        def run(e, waits=waits, fn=fn, sem=sem):
            for s, v in waits:
                e.wait_ge(s, v)
            fn(e).then_inc(sem, 1)
        self.prog[eng].append(run)
        self._commit(ev, R, W)

    def dma(self, q, out_ap, in_ap, sb, R=(), W=(), cast=False):
        if cast:
            self.nslots += 1
            key = ("s", self.nslots)
            sem = self.nc.alloc_semaphore(name="w_%d" % self.nslots)
            self.semh[key] = sem
            deps = self._deps(q, R, W)
            if sb.last_dma is not None:
                deps.append(sb.last_dma)
            waits = self._waits(q, deps)
            ev = (key, 16)
            sb.last_dma = ev
            self.sw_events.append(ev)

            def run(e, waits=waits, sem=sem, out_ap=out_ap, in_ap=in_ap):
                for s_, v in waits:
                    e.wait_ge(s_, v)
                e.dma_start(out=out_ap, in_=in_ap, max_dma_last_dim=8192).then_inc(sem, 16)
            self.prog[q].append(run)
            self._commit(ev, R, W)
            return
        if sb.sem is None:
            if self.free_slots:
                sb.semkey, sb.sem, sb.dcount = self.free_slots.pop()
            else:
                sb.sem = self.nc.alloc_semaphore(name="d_" + sb.name)
                self.nslots += 1
                sb.semkey = ("d", self.nslots)
                self.semh[sb.semkey] = sb.sem
            self.dma_sems.append(sb)
        deps = self._deps(q, R, W)
        if sb.last_dma is not None:
            deps.append(sb.last_dma)
        waits = self._waits(q, deps)
        sb.dcount += 16
        ev = (sb.semkey, sb.dcount)
        sb.last_dma = ev
        sem = sb.sem
        kw = {}
        if cast:
            kw["max_dma_last_dim"] = 8192

        def run(e, waits=waits, sem=sem, out_ap=out_ap, in_ap=in_ap, kw=kw):
            for s, v in waits:
                e.wait_ge(s, v)
            e.dma_start(out=out_ap, in_=in_ap, **kw).then_inc(sem, 16)
        self.prog[q].append(run)
        self._commit(ev, R, W)

    def barrier(self):
        for e in self.ENGS:
            deps = [(o, self.cnt[o]) for o in ("pe", "act", "dve", "pool") if self.cnt[o] > 0]
            deps += [(b.semkey, b.dcount) for b in self.dma_sems]
            deps += self.sw_events
            waits = self._waits(e, deps)

            def run(eng, waits=waits):
                for s, v in waits:
                    eng.wait_ge(s, v)
            self.prog[e].append(run)

    def flush(self, name):
        nc = self.nc
        prog = self.prog
        with nc.Block(name) as block:
            @block.tensor
            def _(e):
                for f in prog["pe"]:
                    f(e)

            @block.scalar
            def _(e):
                for f in prog["act"]:
                    f(e)

            @block.vector
            def _(e):
                for f in prog["dve"]:
                    f(e)

            @block.gpsimd
            def _(e):
                for f in prog["pool"]:
                    f(e)

            @block.sync
            def _(e):
                for f in prog["sp"]:
                    f(e)
        self.prog = {e: [] for e in self.ENGS}
        for b in self.phase_bufs:
            if b.sem is not None:
                self.free_slots.append((b.semkey, b.sem, b.dcount))
                self.dma_sems.remove(b)
                b.sem = None
        self.phase_bufs = []
        self.sw_events = []


def build_program(nseg=4, depth=DEPTH):
    TOK = nseg * SEG
    NT = TOK // T
    nc = bass.Bass("TRN2", target_bir_lowering=False)
    K = Kern(nc)

    def din(name, shape, dt=F32):
        return nc.dram_tensor(name, list(shape), dt, kind="ExternalInput")

    xT = din("xT", [D, TOK])
    memT = din("memT", [nseg, D, NMEM])
    cvd = din("cv", [128, NCV])
    w_in = din("w_in", [DEPTH, D, 6 * D])
    pool_w = din("pool_w", [DEPTH, 4, 256, 256])
    lru_wa = din("lru_wa", [DEPTH, 2, 8, 128, 128])
    lru_wx = din("lru_wx", [DEPTH, 2, 8, 128, 128])
    w_kv = din("w_kv", [DEPTH, D, 2 * D])
    w_out = din("w_out", [DEPTH, D, D])
    mlp_w1 = din("mlp_w1", [DEPTH, D, 4 * D])
    mlp_w2 = din("mlp_w2", [DEPTH, 4 * D, D])
    yT = nc.dram_tensor("yT", [D, TOK], F32, kind="ExternalOutput")

    def dscr(name, shape, dt, nreg):
        return Buf(name, nc.dram_tensor(name, list(shape), dt), nreg, "dram")

    XTd = dscr("XTd", [D, TOK], F32, NT)
    HTd = dscr("HTd", [D, TOK + 2 * HALO], BF16, NT + 2)
    MTd = dscr("MTd", [D, TOK], F32, NT)
    GABd = dscr("GABd", [D, TOK], F32, NT)
    MLd = dscr("MLd", [D, TOK], BF16, NT)
    MTbd = dscr("MTbd", [D, TOK], BF16, NT)
    GABbd = dscr("GABbd", [D, TOK], BF16, NT)
    H2Td = dscr("H2Td", [D, TOK], BF16, NT)
    HIDd = dscr("HIDd", [4 * D, TOK], BF16, NT)
    KTd = dscr("KTd", [nseg, D, NMEM], BF16, nseg)
    Vd = dscr("Vd", [nseg, NMEM, D], BF16, nseg)
    xin = Buf("xin", xT, NT, "dram")
    yout = Buf("yout", yT, NT, "dram")

    def fm(dbuf, c0, n):
        return dbuf.t[:, c0:c0 + n].rearrange("(c p) n -> p c n", p=128)

    es = contextlib.ExitStack()

    def sb(name, shape, dt, nreg=1, stack=None):
        K.nbuf += 1
        name = "%s_%d" % (name, K.nbuf)
        t = (stack or es).enter_context(nc.sbuf_tensor(name, list(shape), dt))
        b = Buf(name, t, nreg)
        if stack is not None:
            K.phase_bufs.append(b)
        return b

    with es:
        CV = sb("CV", [128, NCV], F32)
        CC = sb("CC", [128, DEPTH * 16], F32)
        CC2 = sb("CC2", [128, DEPTH * 16], F32)
        ONES = sb("ONES", [128, 128], BF16)
        ONES2 = sb("ONES2", [128, 128], BF16)
        EPSC = sb("EPSC", [128, 1], F32)
        HBB = sb("HBB", [128, DEPTH * 32], F32)
        HC = sb("HC", [128, DEPTH * 16], F32)
        ZER = sb("ZER", [128, T], F32)
        ZB = sb("ZB", [128, 8, HALO], BF16)
        CF = sb("CF", [128, 8], F32)
        HBS = sb("HBS", [128, 8, NT], F32)
        PBS = sb("PBS", [128, 8, NT], F32)
        CBK = sb("CBK", [128, 8, NT], F32)
        TMPC = sb("TMPC", [128, 8], F32)
        PS = [Buf("ps%d" % i, es.enter_context(nc.psum_tensor("ps%d" % i, [128, T], F32)), 1, "psum")
              for i in range(8)]

        def ps():
            K.ps_rr = (K.ps_rr + 1) % 8
            return PS[K.ps_rr]

        def cvc(l, name, idx=0):
            c = l * CPL + CV_OFF[name] + idx
            return CV.t[:, c:c + 1]

        LINK = lambda: CV.t[:, CV_LINK:CV_LINK + 1]

        K.dma("sp", CV.t[:, :], cvd[:, :], CV, W=[(CV, None)])
        K.op("pool", lambda e: e.memset(ONES.t[:, :], 1.0), W=[(ONES, None)])
        K.op("pool", lambda e: e.memset(ZER.t[:, :], 0.0), W=[(ZER, None)])
        K.op("pool", lambda e: e.memset(ONES2.t[:, :], 2.0), W=[(ONES2, None)])
        K.op("pool", lambda e: e.memset(EPSC.t[:, :], EPS), W=[(EPSC, None)])
        K.op("pool", lambda e: e.memset(ZB.t[:, :, :], 0.0), W=[(ZB, None)])
        K.dma("sp", fm(HTd, 0, HALO), ZB.t[:, :, :], ZB, R=[(ZB, None)], W=[(HTd, [NT])])
        K.dma("sp", fm(HTd, TOK + HALO, HALO), ZB.t[:, :, :], ZB, R=[(ZB, None)], W=[(HTd, [NT + 1])])
        for l in range(depth):
            src = CV.t[:, l * CPL + CV_OFF["lru_lam"]: l * CPL + CV_OFF["lru_lam"] + 16]
            dst = CC.t[:, l * 16:(l + 1) * 16]
            dst2 = CC2.t[:, l * 16:(l + 1) * 16]
            K.op("act", lambda e, s=src, d=dst: e.activation(out=d, in_=s, func=AF.Exp, scale=-1.0),
                 R=[(CV, None)], W=[(CC, None)])
            K.op("act", lambda e, d=dst: e.activation(out=d, in_=d, func=AF.Ln, bias=1.0, scale=1.0),
                 R=[(CC, None)], W=[(CC, None)])
            K.op("dve", lambda e, d=dst: e.tensor_scalar(out=d, in0=d, scalar1=-8.0, scalar2=None, op0=ALU.mult),
                 R=[(CC, None)], W=[(CC, None)])
            K.op("dve", lambda e, d=dst, d2=dst2: e.tensor_scalar(out=d2, in0=d, scalar1=2.0, scalar2=None, op0=ALU.mult),
                 R=[(CC, None)], W=[(CC2, None)])
            K.op("dve", lambda e, d=dst, l=l: e.tensor_scalar(out=HC.t[:, l * 16:(l + 1) * 16], in0=d, scalar1=0.5, scalar2=None, op0=ALU.mult),
                 R=[(CC, None)], W=[(HC, None)])
            bsrc = CV.t[:, l * CPL + CV_OFF["lru_ba"]: l * CPL + CV_OFF["lru_ba"] + 32]
            K.op("dve", lambda e, bsrc=bsrc, l=l: e.tensor_scalar(out=HBB.t[:, l * 32:(l + 1) * 32], in0=bsrc, scalar1=0.5, scalar2=None, op0=ALU.mult),
                 R=[(CV, None)], W=[(HBB, None)])
        K.barrier()
        K.flush("setup")

        def mm_group(psb, out_ap, pairs, R):
            def fn(e, pairs=pairs, out_ap=out_ap):
                n = len(pairs)
                ins = None
                for i, (l, r) in enumerate(pairs):
                    ins = e.matmul(out_ap, l, r, start=(i == 0), stop=(i == n - 1))
                return ins
            K.op("pe", fn, R=R, W=[(psb, None)])

        def rstd_from_sq(SQ, ncols, RS):
            p = ps()
            mm_group(p, p.t[:, 0:ncols], [(ONES.t[:, :], SQ.t[:, k, 0:ncols]) for k in range(NCH)],
                     R=[(ONES, None), (SQ, None)])
            K.op("act", lambda e, p=p: e.activation(out=RS.t[:, 0:ncols], in_=p.t[:, 0:ncols], func=AF.Ln,
                                                    bias=EPSC.t[:, 0:1], scale=1.0 / D),
                 R=[(p, None), (EPSC, None)], W=[(RS, None)])
            K.op("act", lambda e: e.activation(out=RS.t[:, 0:ncols], in_=RS.t[:, 0:ncols], func=AF.Exp, scale=-0.5),
                 R=[(RS, None)], W=[(RS, None)])

        def norm_to_bf16(X, ncols, SQ, RS, Hout, hoff, l, gname):
            K.op("act", lambda e: e.activation(out=SQ.t[:, :, 0:ncols], in_=X.t[:, :, 0:ncols], func=AF.Square),
                 R=[(X, None)], W=[(SQ, None)])
            rstd_from_sq(SQ, ncols, RS)
            for k in range(NCH):
                K.op("dve", lambda e, k=k: e.scalar_tensor_tensor(
                    out=Hout.t[:, k, hoff:hoff + ncols], in0=X.t[:, k, 0:ncols], scalar=cvc(l, gname, k),
                    in1=RS.t[:, 0:ncols], op0=ALU.mult, op1=ALU.mult),
                    R=[(X, [k]), (RS, None), (CV, None)], W=[(Hout, [k])])

        def load_w(W, dram_ap, q="pool"):
            shp = list(W.t.shape)
            n = shp[-1]
            if len(shp) == 3 and n > 2048:
                for n0 in range(0, n, 2048):
                    K.dma(q, W.t[:, :, n0:n0 + 2048], dram_ap[:, :, n0:n0 + 2048], W, W=[(W, None)], cast=True)
            else:
                K.dma(q, W.t[:], dram_ap, W, W=[(W, None)], cast=True)

        def w_in_cols(l, c0, n=D):
            return w_in[l, :, c0:c0 + n].rearrange("(c p) e -> p c e", p=128)

        def ht_regs(i):
            r = [i]
            r.append(i - 1 if i > 0 else NT)
            r.append(i + 1 if i < NT - 1 else NT + 1)
            return r

        def load_h_halo(Hh, i):
            K.dma("sp", Hh.t[:, :, :], fm(HTd, i * T, TH), Hh, R=[(HTd, ht_regs(i))], W=[(Hh, None)])
            if i % TPS == 0 and i > 0:
                K.op("dve", lambda e: e.tensor_scalar(out=Hh.t[:, :, 0:HALO], in0=Hh.t[:, :, 0:HALO], scalar1=LINK(),
                                                      scalar2=None, op0=ALU.mult),
                     R=[(Hh, None), (CV, None)], W=[(Hh, None)])
            if i % TPS == TPS - 1 and i < NT - 1:
                K.op("dve", lambda e: e.tensor_scalar(out=Hh.t[:, :, T + HALO:TH], in0=Hh.t[:, :, T + HALO:TH],
                                                      scalar1=LINK(), scalar2=None, op0=ALU.mult),
                     R=[(Hh, None), (CV, None)], W=[(Hh, None)])

        def proj_halo(Wt, Hh, U):
            p = ps()
            for k in range(NCH):
                mm_group(p, p.t[:, k * 16:k * 16 + 8],
                         [(Wt.t[:, d, k * 128:(k + 1) * 128], Hh.t[:, d, 0:HALO]) for d in range(NCH)],
                         R=[(Wt, None), (Hh, None)])
                mm_group(p, p.t[:, k * 16 + 8:k * 16 + 16],
                         [(Wt.t[:, d, k * 128:(k + 1) * 128], Hh.t[:, d, T + HALO:TH]) for d in range(NCH)],
                         R=[(Wt, None), (Hh, None)])
            pv = p.t[:, 0:128].rearrange("p (k c) -> p k c", c=16)
            K.op("act", lambda e, pv=pv: e.activation(out=U.t[:, :, 0:HALO], in_=pv[:, :, 0:8], func=AF.Copy),
                 R=[(p, None)], W=[(U, None)])
            K.op("act", lambda e, pv=pv: e.activation(out=U.t[:, :, T + HALO:TH], in_=pv[:, :, 8:16], func=AF.Copy),
                 R=[(p, None)], W=[(U, None)])
            for k in range(NCH):
                p = ps()
                mm_group(p, p.t[:, :], [(Wt.t[:, d, k * 128:(k + 1) * 128], Hh.t[:, d, HALO:HALO + T]) for d in range(NCH)],
                         R=[(Wt, None), (Hh, None)])
                K.op("act", lambda e, p=p, k=k: e.activation(out=U.t[:, k, HALO:HALO + T], in_=p.t[:, :], func=AF.Copy),
                     R=[(p, None)], W=[(U, [k])])

        def gate_sig(Wg, Hrhs_fn, k, out_ap, Gbuf, gregs, R, tanh=False):
            p = ps()
            mm_group(p, p.t[:, :], [(Wg.t[:, d, k * 128:(k + 1) * 128], Hrhs_fn(d)) for d in range(NCH)], R=R)
            if tanh:
                K.op("act", lambda e, p=p, out_ap=out_ap: e.activation(out=out_ap, in_=p.t[:, :], func=AF.Tanh, scale=0.5),
                     R=[(p, None)], W=[(Gbuf, gregs)])
            else:
                K.op("act", lambda e, p=p, out_ap=out_ap: e.activation(out=out_ap, in_=p.t[:, :], func=AF.Sigmoid),
                     R=[(p, None)], W=[(Gbuf, gregs)])

        class Rot:
            def __init__(self, name, shape, dt, n, stack, nreg=1):
                self.b = [sb("%s%d" % (name, j), shape, dt, nreg, stack) for j in range(n)]
                self.i = 0

            def get(self):
                self.i = (self.i + 1) % len(self.b)
                return self.b[self.i]

        for l in range(depth):
            if l == 0:
                with contextlib.ExitStack() as st:
                    XR = Rot("paX", [128, 8, T], F32, 2, st, 8)
                    SQR = Rot("paSQ", [128, 8, T], BF16, 1, st)
                    RSR = Rot("paRS", [128, T], F32, 2, st)
                    HR = Rot("paH", [128, 8, T], BF16, 2, st, 8)
                    for i in range(NT):
                        X = XR.get(); SQ = SQR.get(); RS = RSR.get(); H = HR.get()
                        K.dma("sp", X.t[:, :, :], fm(xin, i * T, T), X, R=[(xin, [i])], W=[(X, None)])
                        norm_to_bf16(X, T, SQ, RS, H, 0, l, "g_mix_pre")
                        K.dma("act", fm(HTd, HALO + i * T, T), H.t[:, :, :], H, R=[(H, None)], W=[(HTd, [i])])
                    K.barrier()
                    K.flush("pa")

            with contextlib.ExitStack() as st:
                WKV = sb("WKV", [128, 8, 2 * D], BF16, 1, st)
                load_w(WKV, w_kv[l].rearrange("(c p) e -> p c e", p=128))
                MX = Rot("p0X", [128, 8, NMEM], F32, 2, st, 8)
                MSQ = Rot("p0SQ", [128, 8, NMEM], BF16, 1, st)
                MRS = Rot("p0RS", [128, NMEM], F32, 1, st)
                MN = Rot("p0MN", [128, 8, NMEM], BF16, 1, st, 8)
                KTs = Rot("p0KT", [128, 8, NMEM], BF16, 2, st)
                Vs = Rot("p0V", [128, 2, D], BF16, 2, st)
                for s in range(nseg):
                    X = MX.get(); SQ = MSQ.get(); RS = MRS.get(); Mn = MN.get(); KTb = KTs.get(); Vb = Vs.get()
                    K.dma("sp", X.t[:, :, :], memT[s].rearrange("(c p) n -> p c n", p=128), X, W=[(X, None)])
                    norm_to_bf16(X, NMEM, SQ, RS, Mn, 0, l, "g_mem")
                    for k in range(NCH):
                        p = ps()
                        mm_group(p, p.t[:, 0:NMEM], [(WKV.t[:, d, k * 128:(k + 1) * 128], Mn.t[:, d, :]) for d in range(NCH)],
                                 R=[(WKV, None), (Mn, None)])
                        K.op("act", lambda e, p=p, k=k, KTb=KTb: e.activation(out=KTb.t[:, k, :], in_=p.t[:, 0:NMEM], func=AF.Copy),
                             R=[(p, None)], W=[(KTb, None)])
                    for mc in range(2):
                        for hv in range(2):
                            p = ps()
                            mm_group(p, p.t[:, :], [(Mn.t[:, d, mc * 128:(mc + 1) * 128], WKV.t[:, d, D + hv * 512:D + (hv + 1) * 512])
                                                    for d in range(NCH)], R=[(WKV, None), (Mn, None)])
                            K.op("act", lambda e, p=p, mc=mc, hv=hv, Vb=Vb: e.activation(
                                out=Vb.t[:, mc, hv * 512:(hv + 1) * 512], in_=p.t[:, :], func=AF.Copy),
                                R=[(p, None)], W=[(Vb, None)])
                    K.dma("sp", KTd.t[s].rearrange("(c p) n -> p c n", p=128), KTb.t[:, :, :], KTb, R=[(KTb, None)], W=[(KTd, [s])])
                    K.dma("sp", Vd.t[s].rearrange("(c p) n -> p c n", p=128), Vb.t[:, :, :], Vb, R=[(Vb, None)], W=[(Vd, [s])])
                K.barrier()
                K.flush("p0_%d" % l)

            with contextlib.ExitStack() as st:
                Wp = sb("Wp", [128, 8, D], BF16, 1, st)
                Wg = sb("Wg0", [128, 8, D], BF16, 1, st)
                PW = sb("PW", [128, 4, 2, 256], BF16, 1, st)
                load_w(Wp, w_in_cols(l, 0))
                load_w(Wg, w_in_cols(l, 3 * D))
                load_w(PW, pool_w[l].rearrange("g (c p) e -> p g c e", p=128))
                HhR = Rot("p1H", [128, 8, TH], BF16, 2, st)
                UR = Rot("p1U", [128, 8, TH], F32, 2, st, 8)
                G0R = Rot("p1G", [128, 8, T], F32, 2, st, 8)
                S2 = sb("p1S2", [128, 6, TH], F32, 1, st)
                S4 = sb("p1S4", [128, 4, TH], F32, 1, st)
                WIN = sb("p1WIN", [128, 8, T], F32, 1, st)
                PB = sb("p1PB", [128, 8, T], BF16, 4, st)
                MR = Rot("p1M", [128, 8, T], F32, 2, st, 8)
                A0, A1 = HALO, HALO + T
                ctx = {}
                tt = lambda e, o, a, b: e.tensor_tensor(out=o, in0=a, in1=b, op=ALU.add)

                def p1s0(i):
                    Hh = HhR.get(); U = UR.get(); G = G0R.get()
                    load_h_halo(Hh, i)
                    proj_halo(Wp, Hh, U)
                    for k in range(NCH):
                        gate_sig(Wg, lambda d, Hh=Hh: Hh.t[:, d, A0:A1], k, G.t[:, k, :], G, [k], R=[(Wg, None), (Hh, None)])
                    ctx[i] = (U, G)

                def p1s1(i):
                    U, G = ctx.pop(i)
                    M = MR.get()
                    K.op("pool", lambda e, U=U: tt(e, WIN.t[:, 0:2, :], U.t[:, 0:2, A0 - 1:A1 - 1], U.t[:, 0:2, A0:A1]),
                         R=[(U, [0, 1])], W=[(WIN, None)])
                    K.op("pool", lambda e, U=U: tt(e, S2.t[:, :, 1:TH], U.t[:, 2:8, 0:TH - 1], U.t[:, 2:8, 1:TH]),
                         R=[(U, [2, 3, 4, 5, 6, 7])], W=[(S2, None)])
                    K.op("pool", lambda e: tt(e, WIN.t[:, 2:4, :], S2.t[:, 0:2, A0 - 1:A1 - 1], S2.t[:, 0:2, A0 + 1:A1 + 1]),
                         R=[(S2, None)], W=[(WIN, None)])
                    K.op("pool", lambda e: tt(e, S4.t[:, :, 2:TH - 2], S2.t[:, 2:6, 1:TH - 3], S2.t[:, 2:6, 3:TH - 1]),
                         R=[(S2, None)], W=[(S4, None)])
                    K.op("pool", lambda e: tt(e, WIN.t[:, 4:6, :], S4.t[:, 0:2, A0 - 2:A1 - 2], S4.t[:, 0:2, A0 + 2:A1 + 2]),
                         R=[(S4, None)], W=[(WIN, None)])
                    K.op("pool", lambda e: tt(e, S2.t[:, 0:2, 4:TH - 4], S4.t[:, 2:4, 2:TH - 6], S4.t[:, 2:4, 6:TH - 2]),
                         R=[(S4, None)], W=[(S2, None)])
                    K.op("pool", lambda e: tt(e, WIN.t[:, 6:8, :], S2.t[:, 0:2, A0 - 4:A1 - 4], S2.t[:, 0:2, A0 + 4:A1 + 4]),
                         R=[(S2, None)], W=[(WIN, None)])
                    edges = []
                    if i % TPS == 0:
                        edges.append((0 if i == 0 else 2, 0))
                    if i % TPS == TPS - 1:
                        edges.append((1 if i == NT - 1 else 3, T - 8))
                    for kind, c0 in edges:
                        ev = CV.t[:, CV_EDGE + kind * 64:CV_EDGE + (kind + 1) * 64].rearrange("p (k c) -> p k c", c=8)
                        K.op("dve", lambda e, ev=ev, c0=c0: e.tensor_tensor(out=WIN.t[:, :, c0:c0 + 8], in0=WIN.t[:, :, c0:c0 + 8],
                                                                            in1=ev, op=ALU.mult),
                             R=[(WIN, None), (CV, None)], W=[(WIN, None)])
                    for g in range(4):
                        K.op("dve", lambda e, g=g, U=U: e.scalar_tensor_tensor(
                            out=PB.t[:, 2 * g:2 * g + 2, :], in0=WIN.t[:, 2 * g:2 * g + 2, :], scalar=1.0 / POOL_W[g],
                            in1=U.t[:, 2 * g:2 * g + 2, A0:A1], op0=ALU.mult, op1=ALU.subtract),
                            R=[(WIN, None), (U, [2 * g, 2 * g + 1])], W=[(PB, [g])])
                    for k in range(NCH):
                        g, co = k // 2, k % 2
                        p = ps()
                        mm_group(p, p.t[:, :], [(PW.t[:, g, cc, co * 128:(co + 1) * 128], PB.t[:, 2 * g + cc, :]) for cc in range(2)],
                                 R=[(PW, None), (PB, [g])])
                        K.op("dve", lambda e, p=p, k=k, G=G, M=M: e.scalar_tensor_tensor(
                            out=M.t[:, k, :], in0=p.t[:, :], scalar=cvc(l, "pool_scale", k), in1=G.t[:, k, :],
                            op0=ALU.mult, op1=ALU.mult),
                            R=[(p, None), (G, [k]), (CV, None)], W=[(M, [k])])
                    K.dma("act", fm(MTd, i * T, T), M.t[:, :, :], M, R=[(M, None)], W=[(MTd, [i])])

                for step in range(NT + 1):
                    if step < NT:
                        p1s0(step)
                    if step >= 1:
                        p1s1(step - 1)
                K.barrier()
                K.flush("p1_%d" % l)

            with contextlib.ExitStack() as st:
                Wq = sb("Wq", [128, 8, D], BF16, 1, st)
                Wg = sb("Wg2", [128, 8, D], BF16, 1, st)
                load_w(Wq, w_in_cols(l, 2 * D))
                load_w(Wg, w_in_cols(l, 5 * D))
                KTR = Rot("p2KT", [128, 8, NMEM], BF16, 2, st)
                VR = Rot("p2V", [128, 2, D], BF16, 2, st)
                HmR = Rot("p2H", [128, 8, T], BF16, 3, st)
                MR = Rot("p2M", [128, 8, T], F32, 2, st, 8)
                QR = Rot("p2Q", [128, 8, T], BF16, 2, st, 8)
                MbR = Rot("p2Mb", [128, 8, T], BF16, 2, st, 8)
                ER = Rot("p2E", [128, 2, T], BF16, 3, st)
                RR = Rot("p2R", [128, T], F32, 3, st)
                T1R = Rot("p2T1", [128, T], F32, 3, st)
                GR = Rot("p2G", [128, T], F32, 3, st)
                ctx = {}
                kvs = {}

                def p2s0(i):
                    Hm = HmR.get(); Q = QR.get()
                    K.dma("sp", Hm.t[:, :, :], fm(HTd, HALO + i * T, T), Hm, R=[(HTd, [i])], W=[(Hm, None)])
                    for k in range(NCH):
                        p = ps()
                        mm_group(p, p.t[:, :], [(Wq.t[:, d, k * 128:(k + 1) * 128], Hm.t[:, d, :]) for d in range(NCH)],
                                 R=[(Wq, None), (Hm, None)])
                        K.op("act", lambda e, p=p, k=k, Q=Q: e.activation(out=Q.t[:, k, :], in_=p.t[:, :], func=AF.Copy),
                             R=[(p, None)], W=[(Q, [k])])
                    ctx[i] = (Hm, Q)

                def p2s1(i):
                    Hm, Q = ctx.pop(i)
                    M = MR.get()
                    s = i // TPS
                    if i % TPS == 0:
                        KTb = KTR.get(); Vb = VR.get()
                        K.dma("sp", KTb.t[:, :, :], KTd.t[s].rearrange("(c p) n -> p c n", p=128), KTb, R=[(KTd, [s])], W=[(KTb, None)])
                        K.dma("sp", Vb.t[:, :, :], Vd.t[s].rearrange("(c p) n -> p c n", p=128), Vb, R=[(Vd, [s])], W=[(Vb, None)])
                        kvs["kt"], kvs["v"] = KTb, Vb
                    KTb, Vb = kvs["kt"], kvs["v"]
                    K.dma("sp", M.t[:, :, :], fm(MTd, i * T, T), M, R=[(MTd, [i])], W=[(M, None)])

                    def scores(h):
                        E = ER.get()
                        for mc in range(2):
                            p = ps()
                            mm_group(p, p.t[:, :], [(KTb.t[:, 2 * h + dc, mc * 128:(mc + 1) * 128], Q.t[:, 2 * h + dc, :]) for dc in range(2)],
                                     R=[(KTb, None), (Q, [2 * h, 2 * h + 1])])
                            K.op("act", lambda e, p=p, mc=mc, E=E: e.activation(out=E.t[:, mc, :], in_=p.t[:, :], func=AF.Exp, scale=1.0 / 16.0),
                                 R=[(p, None)], W=[(E, None)])
                        return E

                    def rest(h, E):
                        Rc = RR.get()
                        p = ps()
                        mm_group(p, p.t[:, :], [(ONES2.t[:, :], E.t[:, mc, :]) for mc in range(2)], R=[(ONES2, None), (E, None)])
                        K.op("dve", lambda e, p=p, Rc=Rc: e.reciprocal(out=Rc.t[:, :], in_=p.t[:, :]), R=[(p, None)], W=[(Rc, None)])
                        for dvc in range(2):
                            k = 2 * h + dvc
                            T1 = T1R.get(); G = GR.get()
                            p = ps()
                            mm_group(p, p.t[:, :], [(Vb.t[:, mc, k * 128:(k + 1) * 128], E.t[:, mc, :]) for mc in range(2)],
                                     R=[(Vb, None), (E, None)])
                            K.op("dve", lambda e, p=p, T1=T1, Rc=Rc: e.tensor_tensor(out=T1.t[:, :], in0=p.t[:, :], in1=Rc.t[:, :], op=ALU.mult),
                                 R=[(p, None), (Rc, None)], W=[(T1, None)])
                            gate_sig(Wg, lambda d, Hm=Hm: Hm.t[:, d, :], k, G.t[:, :], G, None, R=[(Wg, None), (Hm, None)], tanh=True)
                            K.op("dve", lambda e, T1=T1, G=G: e.scalar_tensor_tensor(out=T1.t[:, :], in0=G.t[:, :], scalar=1.0, in1=T1.t[:, :],
                                                                                     op0=ALU.add, op1=ALU.mult),
                                 R=[(T1, None), (G, None)], W=[(T1, None)])
                            K.op("pool", lambda e, T1=T1, M=M, Mb=Mb, k=k: e.tensor_tensor(out=Mb.t[:, k, :], in0=M.t[:, k, :], in1=T1.t[:, :], op=ALU.add),
                                 R=[(T1, None), (M, [k])], W=[(Mb, [k])])

                    Mb = MbR.get()
                    Es = {0: scores(0)}
                    for h in range(4):
                        if h + 1 < 4:
                            Es[h + 1] = scores(h + 1)
                        rest(h, Es.pop(h))
                    K.dma("act", fm(MTbd, i * T, T), Mb.t[:, :, :], Mb, R=[(Mb, None)], W=[(MTbd, [i])])

                for step in range(NT + 1):
                    if step < NT:
                        p2s0(step)
                    if step >= 1:
                        p2s1(step - 1)
                K.barrier()
                K.flush("p2_%d" % l)

            with contextlib.ExitStack() as st:
                Wl = sb("Wl", [128, 8, D], BF16, 1, st)
                Wg = sb("Wg1", [128, 8, D], BF16, 1, st)
                WA = sb("WA", [128, 2, 8, 128], BF16, 1, st)
                WX = sb("WX", [128, 2, 8, 128], BF16, 1, st)
                load_w(Wl, w_in_cols(l, D))
                load_w(Wg, w_in_cols(l, 4 * D))
                load_w(WA, lru_wa[l].rearrange("r b i j -> i r b j"))
                load_w(WX, lru_wx[l].rearrange("r b i j -> i r b j"))
                HhR = Rot("p3H", [128, 8, TH], BF16, 2, st)
                UL = sb("p3U", [128, 8, TH], F32, 8, st)
                XCR = Rot("p3XC", [128, 8, T], F32, 2, st, 8)
                XCbR = Rot("p3XCb", [128, 8, T], BF16, 2, st, 8)
                MLR = Rot("p3ML", [128, 8, T], BF16, 1, st, 8)
                GBR = Rot("p3GB", [128, 8, T], BF16, 2, st, 8)
                RR = Rot("p3r", [128, T], F32, 1, st)
                IXR = Rot("p3ix", [128, T], F32, 2, st)
                AR = [Rot("p3a%d" % r, [128, T], F32, 3, st) for r in range(2)]
                A2R = Rot("p3a2", [128, 2, T], F32, 2, st)
                BR = [Rot("p3b%d" % r, [128, T], F32, 2, st) for r in range(2)]
                HFR = Rot("p3hf", [128, T], F32, 2, st)
                HBR = Rot("p3hb", [128, T], F32, 2, st)
                ABR = Rot("p3ab", [128, T], F32, 2, st)
                GR = Rot("p3G", [128, T], F32, 3, st)
                A0, A1 = HALO, HALO + T
                ctx = {}

                def p3new(i):
                    Hh = HhR.get(); XC = XCR.get(); XCb = XCbR.get()
                    load_h_halo(Hh, i)
                    proj_halo(Wl, Hh, UL)
                    ctx[i] = dict(Hh=Hh, XC=XC, XCb=XCb)

                def p3conv(i, k):
                    XC = ctx[i]["XC"]
                    K.op("dve", lambda e, k=k, XC=XC: e.tensor_scalar(out=XC.t[:, k, :], in0=UL.t[:, k, A0 - 2:A1 - 2],
                                                                      scalar1=cvc(l, "conv_w", k), scalar2=cvc(l, "conv_b", k),
                                                                      op0=ALU.mult, op1=ALU.add),
                         R=[(UL, [k]), (CV, None)], W=[(XC, [k])])
                    for j in range(1, 4):
                        K.op("dve", lambda e, k=k, j=j, XC=XC: e.scalar_tensor_tensor(
                            out=XC.t[:, k, :], in0=UL.t[:, k, A0 - 2 + j:A1 - 2 + j], scalar=cvc(l, "conv_w", 8 * j + k),
                            in1=XC.t[:, k, :], op0=ALU.mult, op1=ALU.add),
                            R=[(UL, [k]), (XC, [k]), (CV, None)], W=[(XC, [k])])

                def p3xcb(i, k):
                    XC, XCb = ctx[i]["XC"], ctx[i]["XCb"]
                    K.op("act", lambda e, k=k, XC=XC, XCb=XCb: e.activation(out=XCb.t[:, k, :], in_=XC.t[:, k, :], func=AF.Copy),
                         R=[(XC, [k])], W=[(XCb, [k])])

                def p3front(i, k):
                    c = ctx[i]
                    Hh, XC, XCb = c["Hh"], c["XC"], c["XCb"]
                    G = GR.get()
                    gate_sig(Wg, lambda d, Hh=Hh: Hh.t[:, d, A0:A1], k, G.t[:, :], G, None, R=[(Wg, None), (Hh, None)], tanh=True)
                    K.op("dve", lambda e, G=G: e.tensor_scalar(out=G.t[:, :], in0=G.t[:, :], scalar1=1.0, scalar2=None, op0=ALU.add),
                         R=[(G, None)], W=[(G, None)])
                    A2 = A2R.get()
                    AB2 = []
                    IXs = []
                    for r in range(2):
                        Rt = RR.get(); IX = IXR.get(); At = AR[r].get()
                        col = l * 16 + r * 8 + k
                        cc = CC.t[:, col:col + 1]
                        hc = HC.t[:, col:col + 1]
                        hba = HBB.t[:, l * 32 + r * 8 + k:l * 32 + r * 8 + k + 1]
                        hbx = HBB.t[:, l * 32 + 16 + r * 8 + k:l * 32 + 16 + r * 8 + k + 1]
                        p = ps()
                        mm_group(p, p.t[:, :], [(WA.t[:, r, k, :], XCb.t[:, k, :])], R=[(WA, None), (XCb, [k])])
                        K.op("act", lambda e, p=p, Rt=Rt, hba=hba: e.activation(out=Rt.t[:, :], in_=p.t[:, :], func=AF.Tanh, bias=hba, scale=0.5),
                             R=[(p, None), (HBB, None)], W=[(Rt, None)])
                        px = ps()
                        mm_group(px, px.t[:, :], [(WX.t[:, r, k, :], XCb.t[:, k, :])], R=[(WX, None), (XCb, [k])])
                        K.op("act", lambda e, px=px, hbx=hbx: e.activation(out=px.t[:, :], in_=px.t[:, :], func=AF.Tanh, bias=hbx, scale=0.5),
                             R=[(px, None), (HBB, None)], W=[(px, None)])
                        K.op("dve", lambda e, px=px, IX=IX, XC=XC, k=k: e.scalar_tensor_tensor(
                            out=IX.t[:, :], in0=px.t[:, :], scalar=1.0, in1=XC.t[:, k, :], op0=ALU.add, op1=ALU.mult),
                            R=[(px, None), (XC, [k])], W=[(IX, None)])
                        K.op("act", lambda e, At=At, Rt=Rt, hc=hc: e.activation(out=At.t[:, :], in_=Rt.t[:, :], func=AF.Exp, bias=hc, scale=hc),
                             R=[(Rt, None), (HC, None)], W=[(At, None)])
                        K.op("act", lambda e, A2=A2, Rt=Rt, cc=cc, r=r: e.activation(out=A2.t[:, r, :], in_=Rt.t[:, :], func=AF.Exp, bias=cc, scale=cc),
                             R=[(Rt, None), (CC, None)], W=[(A2, None)])
                        AB2.append(At)
                        IXs.append(IX)
                    K.op("act", lambda e, A2=A2: e.activation(out=A2.t[:, :, :], in_=A2.t[:, :, :], func=AF.Sqrt, bias=1.0 / 16.0, scale=-1.0 / 16.0),
                         R=[(A2, None)], W=[(A2, None)])
                    Bs = []
                    for r in range(2):
                        Bt = BR[r].get()
                        K.op("pool", lambda e, IX=IXs[r], A2=A2, Bt=Bt, r=r: e.tensor_tensor(out=Bt.t[:, :], in0=IX.t[:, :], in1=A2.t[:, r, :], op=ALU.mult),
                             R=[(IXs[r], None), (A2, None)], W=[(Bt, None)])
                        Bs.append(Bt)
                    c[("f", k)] = (G, [(AB2[0], Bs[0]), (AB2[1], Bs[1])])

                def p3back(i, k):
                    c = ctx[i]
                    ML, GB = c["ML"], c["GB"]
                    G, AB2 = c.pop(("f", k))
                    HF = HFR.get(); HB = HBR.get(); AB = ABR.get()
                    (Af, Bf), (Ab, Bb) = AB2
                    K.op("dve", lambda e, HF=HF, Af=Af, Bf=Bf, k=k: e.tensor_tensor_scan(
                        out=HF.t[:, :], data0=Af.t[:, :], data1=Bf.t[:, :], initial=CF.t[:, k:k + 1], op0=ALU.mult, op1=ALU.add),
                        R=[(Af, None), (Bf, None), (CF, None)], W=[(HF, None)])
                    K.op("dve", lambda e, HB=HB, Ab=Ab, Bb=Bb: e.tensor_tensor_scan(
                        out=HB.t[:, ::-1], data0=Ab.t[:, ::-1], data1=Bb.t[:, ::-1], initial=0.0, op0=ALU.mult, op1=ALU.add),
                        R=[(Ab, None), (Bb, None)], W=[(HB, None)])
                    K.op("dve", lambda e, AB=AB, Ab=Ab: e.tensor_tensor_scan(
                        out=AB.t[:, ::-1], data0=Ab.t[:, ::-1], data1=ZER.t[:, ::-1], initial=1.0, op0=ALU.mult, op1=ALU.add),
                        R=[(Ab, None), (ZER, None)], W=[(AB, None)])
                    K.op("pool", lambda e, HF=HF, k=k: e.tensor_copy(out=CF.t[:, k:k + 1], in_=HF.t[:, T - 1:T]),
                         R=[(HF, None)], W=[(CF, None)])
                    K.op("pool", lambda e, HB=HB, k=k, i=i: e.tensor_copy(out=HBS.t[:, k, i:i + 1], in_=HB.t[:, 0:1]),
                         R=[(HB, None)], W=[(HBS, None)])
                    K.op("pool", lambda e, HF=HF, HB=HB: e.tensor_tensor(out=HF.t[:, :], in0=HF.t[:, :], in1=HB.t[:, :], op=ALU.add),
                         R=[(HF, None), (HB, None)], W=[(HF, None)])
                    K.op("pool", lambda e, AB=AB, k=k, i=i: e.tensor_copy(out=PBS.t[:, k, i:i + 1], in_=AB.t[:, 0:1]),
                         R=[(AB, None)], W=[(PBS, None)])
                    K.op("pool", lambda e, AB=AB, G=G, GB=GB, k=k: e.tensor_tensor(out=GB.t[:, k, :], in0=AB.t[:, :], in1=G.t[:, :], op=ALU.mult),
                         R=[(AB, None), (G, None)], W=[(GB, [k])])
                    K.op("pool", lambda e, HF=HF, G=G, ML=ML, k=k: e.tensor_tensor(out=ML.t[:, k, :], in0=HF.t[:, :], in1=G.t[:, :], op=ALU.mult),
                         R=[(HF, None), (G, None)], W=[(ML, [k])])

                for step in range(NT + 1):
                    inew = step if step < NT else None
                    iold = step - 1 if step >= 1 else None
                    if inew is not None:
                        p3new(inew)
                    if iold is not None:
                        c = ctx[iold]
                        c["ML"] = MLR.get(); c["GB"] = GBR.get()
                        if iold == 0:
                            K.op("dve", lambda e: e.memset(CF.t[:, :], 0.0), W=[(CF, None)])
                        elif iold % TPS == 0:
                            K.op("dve", lambda e: e.tensor_scalar(out=CF.t[:, :], in0=CF.t[:, :], scalar1=LINK(), scalar2=None, op0=ALU.mult),
                                 R=[(CF, None), (CV, None)], W=[(CF, None)])
                    for k in range(NCH):
                        if inew is not None:
                            p3conv(inew, k)
                        if iold is not None:
                            p3front(iold, k)
                            if k >= 1:
                                p3back(iold, k - 1)
                        if inew is not None:
                            p3xcb(inew, k)
                    if iold is not None:
                        p3back(iold, NCH - 1)
                        c = ctx.pop(iold)
                        K.dma("pool", fm(MLd, iold * T, T), c["ML"].t[:, :, :], c["ML"], R=[(c["ML"], None)], W=[(MLd, [iold])])
                        K.dma("pool", fm(GABbd, iold * T, T), c["GB"].t[:, :, :], c["GB"], R=[(c["GB"], None)], W=[(GABbd, [iold])])
                K.op("dve", lambda e: e.memset(CBK.t[:, :, NT - 1:NT], 0.0), W=[(CBK, None)])
                for i in range(NT - 2, -1, -1):
                    K.op("dve", lambda e, i=i: e.tensor_tensor(out=TMPC.t[:, :], in0=PBS.t[:, :, i + 1], in1=CBK.t[:, :, i + 1], op=ALU.mult),
                         R=[(PBS, None), (CBK, None)], W=[(TMPC, None)])
                    K.op("dve", lambda e, i=i: e.tensor_tensor(out=CBK.t[:, :, i], in0=TMPC.t[:, :], in1=HBS.t[:, :, i + 1], op=ALU.add),
                         R=[(TMPC, None), (HBS, None)], W=[(CBK, None)])
                    if (i + 1) % TPS == 0:
                        K.op("dve", lambda e, i=i: e.tensor_scalar(out=CBK.t[:, :, i], in0=CBK.t[:, :, i], scalar1=LINK(), scalar2=None, op0=ALU.mult),
                             R=[(CBK, None), (CV, None)], W=[(CBK, None)])
                K.barrier()
                K.flush("p3_%d" % l)

            with contextlib.ExitStack() as st:
                Wo = sb("Wo", [128, 8, D], BF16, 1, st)
                load_w(Wo, w_out[l].rearrange("(c p) e -> p c e", p=128))
                MR = Rot("5aM", [128, 8, T], BF16, 2, st, 8)
                GBR = Rot("5aGB", [128, 8, T], BF16, 2, st, 8)
                MLR5 = Rot("5aML", [128, 8, T], BF16, 2, st)
                XR = Rot("5aX", [128, 8, T], F32, 1, st, 8)
                MBR = Rot("5aMB", [128, 8, T], BF16, 2, st, 8)
                OR_ = Rot("5aO", [128, 8, T], F32, 2, st, 8)
                SQAR = Rot("5aSQa", [128, 8, T], BF16, 2, st)
                SQB = sb("5aSQb", [128, 8, T], BF16, 8, st)
                RSR = Rot("5aRS", [128, T], F32, 2, st)
                TR = Rot("5aT", [128, T], F32, 3, st)
                xsrc = xin if l == 0 else XTd
                ctx = {}

                def p5aload(i):
                    M = MR.get(); GB = GBR.get(); ML = MLR5.get()
                    K.dma("sp", ML.t[:, :, :], fm(MLd, i * T, T), ML, R=[(MLd, [i])], W=[(ML, None)])
                    K.dma("sp", M.t[:, :, :], fm(MTbd, i * T, T), M, R=[(MTbd, [i])], W=[(M, None)])
                    K.dma("sp", GB.t[:, :, :], fm(GABbd, i * T, T), GB, R=[(GABbd, [i])], W=[(GB, None)])
                    ctx[("ld", i)] = (M, GB, ML)

                def p5amb(i):
                    M, GB, ML = ctx.pop(("ld", i))
                    MB = MBR.get()
                    for k in range(NCH):
                        K.op("dve", lambda e, k=k, M=M, GB=GB, MB=MB, i=i: e.scalar_tensor_tensor(
                            out=MB.t[:, k, :], in0=GB.t[:, k, :], scalar=CBK.t[:, k, i:i + 1], in1=M.t[:, k, :],
                            op0=ALU.mult, op1=ALU.add),
                            R=[(GB, [k]), (M, [k]), (CBK, None)], W=[(MB, [k])])
                    ctx[("mb", i)] = (MB, ML)

                def p5as0(i):
                    MB, ML = ctx.pop(("mb", i))
                    O = OR_.get(); SQ = SQAR.get()
                    for k in range(NCH):
                        p = ps()
                        mm_group(p, p.t[:, :], [(Wo.t[:, d, k * 128:(k + 1) * 128], MB.t[:, d, :]) for d in range(NCH)]
                                 + [(Wo.t[:, d, k * 128:(k + 1) * 128], ML.t[:, d, :]) for d in range(NCH)],
                                 R=[(Wo, None), (MB, None), (ML, None)])
                        K.op("act", lambda e, p=p, k=k, O=O: e.activation(out=O.t[:, k, :], in_=p.t[:, :], func=AF.Copy),
                             R=[(p, None)], W=[(O, [k])])
                    K.op("act", lambda e, O=O, SQ=SQ: e.activation(out=SQ.t[:, :, :], in_=O.t[:, :, :], func=AF.Square), R=[(O, None)], W=[(SQ, None)])
                    ctx[i] = (O, SQ)

                def p5as1a(i):
                    O, SQ = ctx.pop(i)
                    X = XR.get(); RS = RSR.get()
                    K.dma("sp", X.t[:, :, :], fm(xsrc, i * T, T), X, R=[(xsrc, [i])], W=[(X, None)])
                    rstd_from_sq(SQ, T, RS)
                    for k in range(NCH):
                        Tt = TR.get()
                        K.op("dve", lambda e, k=k, Tt=Tt, RS=RS, O=O: e.scalar_tensor_tensor(
                            out=Tt.t[:, :], in0=O.t[:, k, :], scalar=cvc(l, "g_mix_post", k), in1=RS.t[:, :], op0=ALU.mult, op1=ALU.mult),
                            R=[(O, [k]), (RS, None), (CV, None)], W=[(Tt, None)])
                        K.op("pool", lambda e, k=k, Tt=Tt, X=X: e.tensor_tensor(out=X.t[:, k, :], in0=X.t[:, k, :], in1=Tt.t[:, :], op=ALU.add),
                             R=[(Tt, None), (X, [k])], W=[(X, [k])])
                    K.dma("act", fm(XTd, i * T, T), X.t[:, :, :], X, R=[(X, None)], W=[(XTd, [i])])
                    K.op("act", lambda e, X=X: e.activation(out=SQB.t[:, :, :], in_=X.t[:, :, :], func=AF.Square),
                         R=[(X, None)], W=[(SQB, None)])
                    ctx[("b", i)] = X

                def p5as1b(i):
                    X = ctx.pop(("b", i))
                    RS2 = RSR.get()
                    rstd_from_sq(SQB, T, RS2)
                    for k in range(NCH):
                        K.op("dve", lambda e, k=k, X=X, RS2=RS2: e.scalar_tensor_tensor(
                            out=SQB.t[:, k, :], in0=X.t[:, k, :], scalar=cvc(l, "g_mlp_pre", k),
                            in1=RS2.t[:, :], op0=ALU.mult, op1=ALU.mult),
                            R=[(X, [k]), (RS2, None), (CV, None)], W=[(SQB, None)])
                    K.dma("act", fm(H2Td, i * T, T), SQB.t[:, :, :], SQB, R=[(SQB, None)], W=[(H2Td, [i])])

                p5aload(0)
                p5amb(0)
                for step in range(NT + 1):
                    if step >= 1:
                        p5as1a(step - 1)
                    if step + 1 < NT:
                        p5aload(step + 1)
                    if step < NT:
                        p5as0(step)
                    if step >= 1:
                        p5as1b(step - 1)
                    if step + 1 < NT:
                        p5amb(step + 1)
                K.barrier()
                K.flush("p5a_%d" % l)

            with contextlib.ExitStack() as st:
                W1 = sb("W1", [128, 8, 4 * D], BF16, 1, st)
                load_w(W1, mlp_w1[l].rearrange("(c p) e -> p c e", p=128))
                H2R = Rot("5bH2", [128, 8, T], BF16, 2, st)
                HDR = Rot("5bHD", [128, 8, T], BF16, 3, st)
                RLR = Rot("5bRL", [128, T], F32, 3, st)
                for i in range(NT):
                    H2 = H2R.get()
                    K.dma("sp", H2.t[:, :, :], fm(H2Td, i * T, T), H2, R=[(H2Td, [i])], W=[(H2, None)])
                    for fq in range(4):
                        HD = HDR.get()
                        for fk in range(8):
                            f = fq * 8 + fk
                            RL = RLR.get()
                            p = ps()
                            mm_group(p, p.t[:, :], [(W1.t[:, d, f * 128:(f + 1) * 128], H2.t[:, d, :]) for d in range(NCH)],
                                     R=[(W1, None), (H2, None)])
                            K.op("act", lambda e, p=p, RL=RL: e.activation(out=RL.t[:, :], in_=p.t[:, :], func=AF.Relu),
                                 R=[(p, None)], W=[(RL, None)])
                            K.op("act", lambda e, RL=RL, HD=HD, fk=fk: e.activation(out=HD.t[:, fk, :], in_=RL.t[:, :], func=AF.Square),
                                 R=[(RL, None)], W=[(HD, None)])
                        K.dma("act", HIDd.t[fq * D:(fq + 1) * D, i * T:(i + 1) * T].rearrange("(c p) n -> p c n", p=128),
                              HD.t[:, :, :], HD, R=[(HD, None)], W=[(HIDd, [i])])
                K.barrier()
                K.flush("p5b_%d" % l)

            with contextlib.ExitStack() as st:
                W2 = sb("W2", [128, 32, D], BF16, 1, st)
                load_w(W2, mlp_w2[l].rearrange("(c p) e -> p c e", p=128))
                HDR = Rot("5cHD", [128, 8, T], BF16, 8, st)
                XR = Rot("5cX", [128, 8, T], F32, 1, st, 8)
                YR = Rot("5cY", [128, 8, T], F32, 1, st, 8)
                SQA = sb("5cSQa", [128, 8, T], BF16, 1, st)
                SQBH = sb("5cSQbH", [128, 8, T], BF16, 8, st)
                RSR = Rot("5cRS", [128, T], F32, 2, st)
                TR = Rot("5cT", [128, T], F32, 2, st)
                last = (l == depth - 1)
                ctx = {}

                def p5cload(i):
                    HD = []
                    for fq in range(4):
                        b = HDR.get()
                        K.dma("sp", b.t[:, :, :], HIDd.t[fq * D:(fq + 1) * D, i * T:(i + 1) * T].rearrange("(c p) n -> p c n", p=128),
                              b, R=[(HIDd, [i])], W=[(b, None)])
                        HD.append(b)
                    ctx[("hd", i)] = HD

                def p5cs0(i):
                    Y = YR.get()
                    HD = ctx.pop(("hd", i))
                    for k in range(NCH):
                        p = ps()
                        mm_group(p, p.t[:, :], [(W2.t[:, f, k * 128:(k + 1) * 128], HD[f // 8].t[:, f % 8, :]) for f in range(32)],
                                 R=[(W2, None)] + [(b, None) for b in HD])
                        K.op("act", lambda e, p=p, k=k, Y=Y: e.activation(out=Y.t[:, k, :], in_=p.t[:, :], func=AF.Copy),
                             R=[(p, None)], W=[(Y, [k])])
                    K.op("act", lambda e, Y=Y: e.activation(out=SQA.t[:, :, :], in_=Y.t[:, :, :], func=AF.Square), R=[(Y, None)], W=[(SQA, None)])
                    ctx[i] = Y

                def p5cs1a(i):
                    Y = ctx.pop(i)
                    X = XR.get(); RS = RSR.get()
                    K.dma("sp", X.t[:, :, :], fm(XTd, i * T, T), X, R=[(XTd, [i])], W=[(X, None)])
                    rstd_from_sq(SQA, T, RS)
                    for k in range(NCH):
                        Tt = TR.get()
                        K.op("dve", lambda e, k=k, Tt=Tt, RS=RS, Y=Y: e.scalar_tensor_tensor(
                            out=Tt.t[:, :], in0=Y.t[:, k, :], scalar=cvc(l, "g_mlp_post", k), in1=RS.t[:, :], op0=ALU.mult, op1=ALU.mult),
                            R=[(Y, [k]), (RS, None), (CV, None)], W=[(Tt, None)])
                        K.op("pool", lambda e, k=k, Tt=Tt, X=X: e.tensor_tensor(out=X.t[:, k, :], in0=X.t[:, k, :], in1=Tt.t[:, :], op=ALU.add),
                             R=[(Tt, None), (X, [k])], W=[(X, [k])])
                    if last:
                        K.dma("act", fm(yout, i * T, T), X.t[:, :, :], X, R=[(X, None)], W=[(yout, [i])])
                    else:
                        K.dma("act", fm(XTd, i * T, T), X.t[:, :, :], X, R=[(X, None)], W=[(XTd, [i])])
                        K.op("act", lambda e, X=X: e.activation(out=SQBH.t[:, :, :], in_=X.t[:, :, :], func=AF.Square),
                             R=[(X, None)], W=[(SQBH, None)])
                    ctx[("b", i)] = X

                def p5cs1b(i):
                    X = ctx.pop(("b", i))
                    if last:
                        return
                    RS2 = RSR.get()
                    rstd_from_sq(SQBH, T, RS2)
                    for k in range(NCH):
                        K.op("dve", lambda e, k=k, X=X, RS2=RS2: e.scalar_tensor_tensor(
                            out=SQBH.t[:, k, :], in0=X.t[:, k, :], scalar=cvc(l + 1, "g_mix_pre", k),
                            in1=RS2.t[:, :], op0=ALU.mult, op1=ALU.mult),
                            R=[(X, [k]), (RS2, None), (CV, None)], W=[(SQBH, None)])
                    K.dma("act", fm(HTd, HALO + i * T, T), SQBH.t[:, :, :], SQBH, R=[(SQBH, None)], W=[(HTd, [i])])

                p5cload(0)
                for step in range(NT + 1):
                    if step >= 1:
                        p5cs1a(step - 1)
                    if step + 1 < NT:
                        p5cload(step + 1)
                    if step < NT:
                        p5cs0(step)
                    if step >= 1:
                        p5cs1b(step - 1)
                K.barrier()
                K.flush("p5c_%d" % l)
    return nc


def _edge_tables(link):
    tab = np.ones((4, 8, 8), np.float32)
    for g, w in enumerate(POOL_W):
        h = w // 2
        S = 1 << 20
        first = np.array([w / float((t + h) - max(t - h, 0)) for t in range(8)], np.float32)
        last = np.array([w / float(min(t + h, S) - (t - h)) for t in range(S - 8, S)], np.float32)
        for k in (2 * g, 2 * g + 1):
            tab[0, k] = first
            tab[1, k] = last
            tab[2, k] = 1.0 if link else first
            tab[3, k] = 1.0 if link else last
    return tab


def _build_cv(P, link):
    cv = np.zeros((128, NCV), np.float32)

    def put(l, name, v):
        v = np.asarray(v, np.float32).reshape(-1, 8, 128)
        o = l * CPL + CV_OFF[name]
        n = v.shape[0]
        cv[:, o:o + 8 * n] = v.transpose(2, 0, 1).reshape(128, 8 * n)
    for l in range(DEPTH):
        put(l, "g_mix_pre", P["norm_mix_pre"][l])
        put(l, "g_mix_post", P["norm_mix_post"][l])
        put(l, "g_mem", P["norm_mem"][l])
        put(l, "pool_scale", P["pool_scale"][l])
        put(l, "conv_w", P["conv_w"][l])
        put(l, "conv_b", P["conv_b"][l])
        put(l, "lru_ba", P["lru_ba"][l])
        put(l, "lru_bx", P["lru_bx"][l])
        put(l, "lru_lam", P["lru_lambda"][l])
        put(l, "g_mlp_pre", P["norm_mlp_pre"][l])
        put(l, "g_mlp_post", P["norm_mlp_post"][l])
    cv[:, CV_EDGE:CV_EDGE + 256] = _edge_tables(link).reshape(1, 256)
    cv[:, CV_LINK] = 1.0 if link else 0.0
    return cv


_NC_CACHE = {}


def kernel(**inputs):
    P = {k: np.asarray(v) for k, v in inputs.items()}
    xs, xp = P["x_sample"], P["x_prompt"]
    ms, mp = P["mem_sample"], P["mem_prompt"]
    n = 8
    wnames = ["w_in", "pool_w", "lru_wa", "lru_wx", "w_kv", "w_out", "mlp_w1", "mlp_w2"]
    weights = {k: np.ascontiguousarray(P[k], dtype=np.float32) for k in wnames}
    cv_s = _build_cv(P, True)
    cv_p = _build_cv(P, False)
    in_maps = []
    for c in range(n):
        if c < 2:
            xT = np.ascontiguousarray(xs[c].T)
            memT = np.ascontiguousarray(np.stack([ms[c].T] * 4))
            cv = cv_s
        elif c in (4, 5):
            j0 = 4 * (c - 4)
            xT = np.ascontiguousarray(np.concatenate([xp[j0 + j].T for j in range(4)], axis=1))
            memT = np.ascontiguousarray(np.stack([mp[j0 + j].T for j in range(4)]))
            cv = cv_p
        else:
            xT = np.zeros((D, 4 * SEG), np.float32)
            memT = np.zeros((4, D, NMEM), np.float32)
            cv = cv_p
        m = {"xT": xT.astype(np.float32), "memT": memT.astype(np.float32), "cv": cv}
        m.update(weights)
        in_maps.append(m)
    if "nc" not in _NC_CACHE:
        _NC_CACHE["nc"] = build_program()
    res = run_bass_kernel_spmd(_NC_CACHE["nc"], in_maps, core_ids=list(range(n)))
    y_sample = np.stack([np.ascontiguousarray(res.results[c]["yT"].T) for c in range(2)]).astype(np.float32)
    yp = []
    for c in (4, 5):
        yT = res.results[c]["yT"]
        for j in range(4):
            yp.append(np.ascontiguousarray(yT[:, j * SEG:(j + 1) * SEG].T))
    y_prompt = np.stack(yp).astype(np.float32)
    return (y_prompt, y_sample)
```

```python
import contextlib
import numpy as np
import concourse.bass as bass
import concourse.mybir as mybir
from concourse.bass_utils import run_bass_kernel_spmd

F32 = mybir.dt.float32
BF16 = mybir.dt.bfloat16
AF = mybir.ActivationFunctionType
ALU = mybir.AluOpType

D = 1024
NCH = 8
T = 512
HALO = 8
TH = T + 2 * HALO
SEG = 4096
TPS = SEG // T
NMEM = 256
DEPTH = 4
EPS = 1e-6
POOL_W = (2, 4, 8, 16)

CV_NAMES = [("g_mix_pre", 8), ("g_mix_post", 8), ("g_mem", 8), ("pool_scale", 8), ("conv_w", 32),
            ("conv_b", 8), ("lru_ba", 16), ("lru_bx", 16), ("lru_lam", 16), ("g_mlp_pre", 8),
            ("g_mlp_post", 8)]
CV_OFF = {}
_o = 0
for _n, _c in CV_NAMES:
    CV_OFF[_n] = _o
    _o += _c
CPL = _o
CV_EDGE = DEPTH * CPL
CV_LINK = CV_EDGE + 256
NCV = CV_LINK + 1


class Reg:
    __slots__ = ("lw", "rd")

    def __init__(self):
        self.lw = None
        self.rd = []


class Buf:
    def __init__(self, name, t, nreg=1, space="sbuf"):
        self.name = name
        self.t = t
        self.regs = [Reg() for _ in range(nreg)]
        self.space = space
        self.sem = None
        self.dcount = 0
        self.last_dma = None


class Kern:
    ENGS = ("pe", "act", "dve", "pool", "sp")

    def __init__(self, nc):
        self.nc = nc
        self.prog = {e: [] for e in self.ENGS}
        self.cnt = {e: 0 for e in self.ENGS}
        self.waited = {e: {} for e in self.ENGS}
        self.semh = {}
        for e in ("pe", "act", "dve", "pool"):
            self.semh[e] = nc.alloc_semaphore(name="sem_" + e)
        self.dma_sems = []
        self.free_slots = []
        self.sw_events = []
        self.nslots = 0
        self.phase_bufs = []
        self.nbuf = 0
        self.ps_rr = 0

    def _deps(self, eng, R, W):
        deps = []
        for buf, regs in R:
            for r in (range(len(buf.regs)) if regs is None else regs):
                g = buf.regs[r]
                if g.lw is not None:
                    deps.append(g.lw)
        for buf, regs in W:
            for r in (range(len(buf.regs)) if regs is None else regs):
                g = buf.regs[r]
                if g.lw is not None:
                    deps.append(g.lw)
                deps.extend(g.rd)
        return deps

    def _commit(self, ev, R, W):
        for buf, regs in R:
            for r in (range(len(buf.regs)) if regs is None else regs):
                buf.regs[r].rd.append(ev)
        for buf, regs in W:
            for r in (range(len(buf.regs)) if regs is None else regs):
                g = buf.regs[r]
                g.lw = ev
                g.rd = []

    def _waits(self, eng, deps):
        need = {}
        for key, val in deps:
            if need.get(key, 0) < val:
                need[key] = val
        out = []
        for key, val in need.items():
            if key == eng and eng == "pe":
                continue
            if self.waited[eng].get(key, 0) >= val:
                continue
            self.waited[eng][key] = val
            out.append((self.semh[key], val))
        return out

    def op(self, eng, fn, R=(), W=()):
        deps = self._deps(eng, R, W)
        waits = self._waits(eng, deps)
        self.cnt[eng] += 1
        ev = (eng, self.cnt[eng])
        sem = self.semh[eng]

        def run(e, waits=waits, fn=fn, sem=sem):
            for s, v in waits:
                e.wait_ge(s, v)
            fn(e).then_inc(sem, 1)
        self.prog[eng].append(run)
        self._commit(ev, R, W)

    def dma(self, q, out_ap, in_ap, sb, R=(), W=(), cast=False):
        if cast:
            self.nslots += 1
            key = ("s", self.nslots)
            sem = self.nc.alloc_semaphore(name="w_%d" % self.nslots)
            self.semh[key] = sem
            deps = self._deps(q, R, W)
            if sb.last_dma is not None:
                deps.append(sb.last_dma)
            waits = self._waits(q, deps)
            ev = (key, 16)
            sb.last_dma = ev
            self.sw_events.append(ev)

            def run(e, waits=waits, sem=sem, out_ap=out_ap, in_ap=in_ap):
                for s_, v in waits:
                    e.wait_ge(s_, v)
                e.dma_start(out=out_ap, in_=in_ap, max_dma_last_dim=8192).then_inc(sem, 16)
            self.prog[q].append(run)
            self._commit(ev, R, W)
            return
        if sb.sem is None:
            if self.free_slots:
                sb.semkey, sb.sem, sb.dcount = self.free_slots.pop()
            else:
                sb.sem = self.nc.alloc_semaphore(name="d_" + sb.name)
                self.nslots += 1
                sb.semkey = ("d", self.nslots)
                self.semh[sb.semkey] = sb.sem
            self.dma_sems.append(sb)
        deps = self._deps(q, R, W)
        if sb.last_dma is not None:
            deps.append(sb.last_dma)
        waits = self._waits(q, deps)
        sb.dcount += 16
        ev = (sb.semkey, sb.dcount)
        sb.last_dma = ev
        sem = sb.sem
        kw = {}
        if cast:
            kw["max_dma_last_dim"] = 8192

        def run(e, waits=waits, sem=sem, out_ap=out_ap, in_ap=in_ap, kw=kw):
            for s, v in waits:
                e.wait_ge(s, v)
            e.dma_start(out=out_ap, in_=in_ap, **kw).then_inc(sem, 16)
        self.prog[q].append(run)
        self._commit(ev, R, W)

    def barrier(self):
        for e in self.ENGS:
            deps = [(o, self.cnt[o]) for o in ("pe", "act", "dve", "pool") if self.cnt[o] > 0]
            deps += [(b.semkey, b.dcount) for b in self.dma_sems]
            deps += self.sw_events
            waits = self._waits(e, deps)

            def run(eng, waits=waits):
                for s, v in waits:
                    eng.wait_ge(s, v)
            self.prog[e].append(run)

    def flush(self, name):
        nc = self.nc
        prog = self.prog
        with nc.Block(name) as block:
            @block.tensor
            def _(e):
                for f in prog["pe"]:
                    f(e)

            @block.scalar
            def _(e):
                for f in prog["act"]:
                    f(e)

            @block.vector
            def _(e):
                for f in prog["dve"]:
                    f(e)

            @block.gpsimd
            def _(e):
                for f in prog["pool"]:
                    f(e)

            @block.sync
            def _(e):
                for f in prog["sp"]:
                    f(e)
        self.prog = {e: [] for e in self.ENGS}
        for b in self.phase_bufs:
            if b.sem is not None:
                self.free_slots.append((b.semkey, b.sem, b.dcount))
                self.dma_sems.remove(b)
                b.sem = None
        self.phase_bufs = []
        self.sw_events = []


def build_program(nseg=4, depth=DEPTH):
    TOK = nseg * SEG
    NT = TOK // T
    nc = bass.Bass("TRN2", target_bir_lowering=False)
    K = Kern(nc)

    def din(name, shape, dt=F32):
        return nc.dram_tensor(name, list(shape), dt, kind="ExternalInput")

    xT = din("xT", [D, TOK])
    memT = din("memT", [nseg, D, NMEM])
    cvd = din("cv", [128, NCV])
    w_in = din("w_in", [DEPTH, D, 6 * D])
    pool_w = din("pool_w", [DEPTH, 4, 256, 256])
    lru_wa = din("lru_wa", [DEPTH, 2, 8, 128, 128])
    lru_wx = din("lru_wx", [DEPTH, 2, 8, 128, 128])
    w_kv = din("w_kv", [DEPTH, D, 2 * D])
    w_out = din("w_out", [DEPTH, D, D])
    mlp_w1 = din("mlp_w1", [DEPTH, D, 4 * D])
    mlp_w2 = din("mlp_w2", [DEPTH, 4 * D, D])
    yT = nc.dram_tensor("yT", [D, TOK], F32, kind="ExternalOutput")

    def dscr(name, shape, dt, nreg):
        return Buf(name, nc.dram_tensor(name, list(shape), dt), nreg, "dram")

    XTd = dscr("XTd", [D, TOK], F32, NT)
    HTd = dscr("HTd", [D, TOK + 2 * HALO], BF16, NT + 2)
    MTd = dscr("MTd", [D, TOK], F32, NT)
    GABd = dscr("GABd", [D, TOK], F32, NT)
    MLd = dscr("MLd", [D, TOK], BF16, NT)
    MTbd = dscr("MTbd", [D, TOK], BF16, NT)
    GABbd = dscr("GABbd", [D, TOK], BF16, NT)
    H2Td = dscr("H2Td", [D, TOK], BF16, NT)
    HIDd = dscr("HIDd", [4 * D, TOK], BF16, NT)
    KTd = dscr("KTd", [nseg, D, NMEM], BF16, nseg)
    Vd = dscr("Vd", [nseg, NMEM, D], BF16, nseg)
    xin = Buf("xin", xT, NT, "dram")
    yout = Buf("yout", yT, NT, "dram")

    def fm(dbuf, c0, n):
        return dbuf.t[:, c0:c0 + n].rearrange("(c p) n -> p c n", p=128)

    es = contextlib.ExitStack()

    def sb(name, shape, dt, nreg=1, stack=None):
        K.nbuf += 1
        name = "%s_%d" % (name, K.nbuf)
        t = (stack or es).enter_context(nc.sbuf_tensor(name, list(shape), dt))
        b = Buf(name, t, nreg)
        if stack is not None:
            K.phase_bufs.append(b)
        return b

    with es:
        CV = sb("CV", [128, NCV], F32)
        CC = sb("CC", [128, DEPTH * 16], F32)
        CC2 = sb("CC2", [128, DEPTH * 16], F32)
        ONES = sb("ONES", [128, 128], BF16)
        ONES2 = sb("ONES2", [128, 128], BF16)
        EPSC = sb("EPSC", [128, 1], F32)
        HBB = sb("HBB", [128, DEPTH * 32], F32)
        HC = sb("HC", [128, DEPTH * 16], F32)
        ZER = sb("ZER", [128, T], F32)
        ZB = sb("ZB", [128, 8, HALO], BF16)
        CF = sb("CF", [128, 8], F32)
        HBS = sb("HBS", [128, 8, NT], F32)
        PBS = sb("PBS", [128, 8, NT], F32)
        CBK = sb("CBK", [128, 8, NT], F32)
        TMPC = sb("TMPC", [128, 8], F32)
        PS = [Buf("ps%d" % i, es.enter_context(nc.psum_tensor("ps%d" % i, [128, T], F32)), 1, "psum")
              for i in range(8)]

        def ps():
            K.ps_rr = (K.ps_rr + 1) % 8
            return PS[K.ps_rr]

        def cvc(l, name, idx=0):
            c = l * CPL + CV_OFF[name] + idx
            return CV.t[:, c:c + 1]

        LINK = lambda: CV.t[:, CV_LINK:CV_LINK + 1]

        K.dma("sp", CV.t[:, :], cvd[:, :], CV, W=[(CV, None)])
        K.op("pool", lambda e: e.memset(ONES.t[:, :], 1.0), W=[(ONES, None)])
        K.op("pool", lambda e: e.memset(ZER.t[:, :], 0.0), W=[(ZER, None)])
        K.op("pool", lambda e: e.memset(ONES2.t[:, :], 2.0), W=[(ONES2, None)])
        K.op("pool", lambda e: e.memset(EPSC.t[:, :], EPS), W=[(EPSC, None)])
        K.op("pool", lambda e: e.memset(ZB.t[:, :, :], 0.0), W=[(ZB, None)])
        K.dma("sp", fm(HTd, 0, HALO), ZB.t[:, :, :], ZB, R=[(ZB, None)], W=[(HTd, [NT])])
        K.dma("sp", fm(HTd, TOK + HALO, HALO), ZB.t[:, :, :], ZB, R=[(ZB, None)], W=[(HTd, [NT + 1])])
        for l in range(depth):
            src = CV.t[:, l * CPL + CV_OFF["lru_lam"]: l * CPL + CV_OFF["lru_lam"] + 16]
            dst = CC.t[:, l * 16:(l + 1) * 16]
            dst2 = CC2.t[:, l * 16:(l + 1) * 16]
            K.op("act", lambda e, s=src, d=dst: e.activation(out=d, in_=s, func=AF.Exp, scale=-1.0),
                 R=[(CV, None)], W=[(CC, None)])
            K.op("act", lambda e, d=dst: e.activation(out=d, in_=d, func=AF.Ln, bias=1.0, scale=1.0),
                 R=[(CC, None)], W=[(CC, None)])
            K.op("dve", lambda e, d=dst: e.tensor_scalar(out=d, in0=d, scalar1=-8.0, scalar2=None, op0=ALU.mult),
                 R=[(CC, None)], W=[(CC, None)])
            K.op("dve", lambda e, d=dst, d2=dst2: e.tensor_scalar(out=d2, in0=d, scalar1=2.0, scalar2=None, op0=ALU.mult),
                 R=[(CC, None)], W=[(CC2, None)])
            K.op("dve", lambda e, d=dst, l=l: e.tensor_scalar(out=HC.t[:, l * 16:(l + 1) * 16], in0=d, scalar1=0.5, scalar2=None, op0=ALU.mult),
                 R=[(CC, None)], W=[(HC, None)])
            bsrc = CV.t[:, l * CPL + CV_OFF["lru_ba"]: l * CPL + CV_OFF["lru_ba"] + 32]
            K.op("dve", lambda e, bsrc=bsrc, l=l: e.tensor_scalar(out=HBB.t[:, l * 32:(l + 1) * 32], in0=bsrc, scalar1=0.5, scalar2=None, op0=ALU.mult),
                 R=[(CV, None)], W=[(HBB, None)])
        K.barrier()
        K.flush("setup")

        def mm_group(psb, out_ap, pairs, R):
            def fn(e, pairs=pairs, out_ap=out_ap):
                n = len(pairs)
                ins = None
                for i, (l, r) in enumerate(pairs):
                    ins = e.matmul(out_ap, l, r, start=(i == 0), stop=(i == n - 1))
                return ins
            K.op("pe", fn, R=R, W=[(psb, None)])

        def rstd_from_sq(SQ, ncols, RS):
            p = ps()
            mm_group(p, p.t[:, 0:ncols], [(ONES.t[:, :], SQ.t[:, k, 0:ncols]) for k in range(NCH)],
                     R=[(ONES, None), (SQ, None)])
            K.op("act", lambda e, p=p: e.activation(out=RS.t[:, 0:ncols], in_=p.t[:, 0:ncols], func=AF.Ln,
                                                    bias=EPSC.t[:, 0:1], scale=1.0 / D),
                 R=[(p, None), (EPSC, None)], W=[(RS, None)])
            K.op("act", lambda e: e.activation(out=RS.t[:, 0:ncols], in_=RS.t[:, 0:ncols], func=AF.Exp, scale=-0.5),
                 R=[(RS, None)], W=[(RS, None)])

        def norm_to_bf16(X, ncols, SQ, RS, Hout, hoff, l, gname):
            K.op("act", lambda e: e.activation(out=SQ.t[:, :, 0:ncols], in_=X.t[:, :, 0:ncols], func=AF.Square),
                 R=[(X, None)], W=[(SQ, None)])
            rstd_from_sq(SQ, ncols, RS)
            for k in range(NCH):
                K.op("dve", lambda e, k=k: e.scalar_tensor_tensor(
                    out=Hout.t[:, k, hoff:hoff + ncols], in0=X.t[:, k, 0:ncols], scalar=cvc(l, gname, k),
                    in1=RS.t[:, 0:ncols], op0=ALU.mult, op1=ALU.mult),
                    R=[(X, [k]), (RS, None), (CV, None)], W=[(Hout, [k])])

        def load_w(W, dram_ap, q="pool"):
            shp = list(W.t.shape)
            n = shp[-1]
            if len(shp) == 3 and n > 2048:
                for n0 in range(0, n, 2048):
                    K.dma(q, W.t[:, :, n0:n0 + 2048], dram_ap[:, :, n0:n0 + 2048], W, W=[(W, None)], cast=True)
            else:
                K.dma(q, W.t[:], dram_ap, W, W=[(W, None)], cast=True)

        def w_in_cols(l, c0, n=D):
            return w_in[l, :, c0:c0 + n].rearrange("(c p) e -> p c e", p=128)

        def ht_regs(i):
            r = [i]
            r.append(i - 1 if i > 0 else NT)
            r.append(i + 1 if i < NT - 1 else NT + 1)
            return r

        def load_h_halo(Hh, i):
            K.dma("sp", Hh.t[:, :, :], fm(HTd, i * T, TH), Hh, R=[(HTd, ht_regs(i))], W=[(Hh, None)])
            if i % TPS == 0 and i > 0:
                K.op("dve", lambda e: e.tensor_scalar(out=Hh.t[:, :, 0:HALO], in0=Hh.t[:, :, 0:HALO], scalar1=LINK(),
                                                      scalar2=None, op0=ALU.mult),
                     R=[(Hh, None), (CV, None)], W=[(Hh, None)])
            if i % TPS == TPS - 1 and i < NT - 1:
                K.op("dve", lambda e: e.tensor_scalar(out=Hh.t[:, :, T + HALO:TH], in0=Hh.t[:, :, T + HALO:TH],
                                                      scalar1=LINK(), scalar2=None, op0=ALU.mult),
                     R=[(Hh, None), (CV, None)], W=[(Hh, None)])

        def proj_halo(Wt, Hh, U):
            p = ps()
            for k in range(NCH):
                mm_group(p, p.t[:, k * 16:k * 16 + 8],
                         [(Wt.t[:, d, k * 128:(k + 1) * 128], Hh.t[:, d, 0:HALO]) for d in range(NCH)],
                         R=[(Wt, None), (Hh, None)])
                mm_group(p, p.t[:, k * 16 + 8:k * 16 + 16],
                         [(Wt.t[:, d, k * 128:(k + 1) * 128], Hh.t[:, d, T + HALO:TH]) for d in range(NCH)],
                         R=[(Wt, None), (Hh, None)])
            pv = p.t[:, 0:128].rearrange("p (k c) -> p k c", c=16)
            K.op("act", lambda e, pv=pv: e.activation(out=U.t[:, :, 0:HALO], in_=pv[:, :, 0:8], func=AF.Copy),
                 R=[(p, None)], W=[(U, None)])
            K.op("act", lambda e, pv=pv: e.activation(out=U.t[:, :, T + HALO:TH], in_=pv[:, :, 8:16], func=AF.Copy),
                 R=[(p, None)], W=[(U, None)])
            for k in range(NCH):
                p = ps()
                mm_group(p, p.t[:, :], [(Wt.t[:, d, k * 128:(k + 1) * 128], Hh.t[:, d, HALO:HALO + T]) for d in range(NCH)],
                         R=[(Wt, None), (Hh, None)])
                K.op("act", lambda e, p=p, k=k: e.activation(out=U.t[:, k, HALO:HALO + T], in_=p.t[:, :], func=AF.Copy),
                     R=[(p, None)], W=[(U, [k])])

        def gate_sig(Wg, Hrhs_fn, k, out_ap, Gbuf, gregs, R, tanh=False):
            p = ps()
            mm_group(p, p.t[:, :], [(Wg.t[:, d, k * 128:(k + 1) * 128], Hrhs_fn(d)) for d in range(NCH)], R=R)
            if tanh:
                K.op("act", lambda e, p=p, out_ap=out_ap: e.activation(out=out_ap, in_=p.t[:, :], func=AF.Tanh, scale=0.5),
                     R=[(p, None)], W=[(Gbuf, gregs)])
            else:
                K.op("act", lambda e, p=p, out_ap=out_ap: e.activation(out=out_ap, in_=p.t[:, :], func=AF.Sigmoid),
                     R=[(p, None)], W=[(Gbuf, gregs)])

        class Rot:
            def __init__(self, name, shape, dt, n, stack, nreg=1):
                self.b = [sb("%s%d" % (name, j), shape, dt, nreg, stack) for j in range(n)]
                self.i = 0

            def get(self):
                self.i = (self.i + 1) % len(self.b)
                return self.b[self.i]

        for l in range(depth):
            if l == 0:
                with contextlib.ExitStack() as st:
                    XR = Rot("paX", [128, 8, T], F32, 2, st, 8)
                    SQR = Rot("paSQ", [128, 8, T], BF16, 1, st)
                    RSR = Rot("paRS", [128, T], F32, 2, st)
                    HR = Rot("paH", [128, 8, T], BF16, 2, st, 8)
                    for i in range(NT):
                        X = XR.get(); SQ = SQR.get(); RS = RSR.get(); H = HR.get()
                        K.dma("sp", X.t[:, :, :], fm(xin, i * T, T), X, R=[(xin, [i])], W=[(X, None)])
                        norm_to_bf16(X, T, SQ, RS, H, 0, l, "g_mix_pre")
                        K.dma("act", fm(HTd, HALO + i * T, T), H.t[:, :, :], H, R=[(H, None)], W=[(HTd, [i])])
                    K.barrier()
                    K.flush("pa")

            with contextlib.ExitStack() as st:
                WKV = sb("WKV", [128, 8, 2 * D], BF16, 1, st)
                load_w(WKV, w_kv[l].rearrange("(c p) e -> p c e", p=128))
                MX = Rot("p0X", [128, 8, NMEM], F32, 2, st, 8)
                MSQ = Rot("p0SQ", [128, 8, NMEM], BF16, 1, st)
                MRS = Rot("p0RS", [128, NMEM], F32, 1, st)
                MN = Rot("p0MN", [128, 8, NMEM], BF16, 1, st, 8)
                KTs = Rot("p0KT", [128, 8, NMEM], BF16, 2, st)
                Vs = Rot("p0V", [128, 2, D], BF16, 2, st)
                for s in range(nseg):
                    X = MX.get(); SQ = MSQ.get(); RS = MRS.get(); Mn = MN.get(); KTb = KTs.get(); Vb = Vs.get()
                    K.dma("sp", X.t[:, :, :], memT[s].rearrange("(c p) n -> p c n", p=128), X, W=[(X, None)])
                    norm_to_bf16(X, NMEM, SQ, RS, Mn, 0, l, "g_mem")
                    for k in range(NCH):
                        p = ps()
                        mm_group(p, p.t[:, 0:NMEM], [(WKV.t[:, d, k * 128:(k + 1) * 128], Mn.t[:, d, :]) for d in range(NCH)],
                                 R=[(WKV, None), (Mn, None)])
                        K.op("act", lambda e, p=p, k=k, KTb=KTb: e.activation(out=KTb.t[:, k, :], in_=p.t[:, 0:NMEM], func=AF.Copy),
                             R=[(p, None)], W=[(KTb, None)])
                    for mc in range(2):
                        for hv in range(2):
                            p = ps()
                            mm_group(p, p.t[:, :], [(Mn.t[:, d, mc * 128:(mc + 1) * 128], WKV.t[:, d, D + hv * 512:D + (hv + 1) * 512])
                                                    for d in range(NCH)], R=[(WKV, None), (Mn, None)])
                            K.op("act", lambda e, p=p, mc=mc, hv=hv, Vb=Vb: e.activation(
                                out=Vb.t[:, mc, hv * 512:(hv + 1) * 512], in_=p.t[:, :], func=AF.Copy),
                                R=[(p, None)], W=[(Vb, None)])
                    K.dma("sp", KTd.t[s].rearrange("(c p) n -> p c n", p=128), KTb.t[:, :, :], KTb, R=[(KTb, None)], W=[(KTd, [s])])
                    K.dma("sp", Vd.t[s].rearrange("(c p) n -> p c n", p=128), Vb.t[:, :, :], Vb, R=[(Vb, None)], W=[(Vd, [s])])
                K.barrier()
                K.flush("p0_%d" % l)

            with contextlib.ExitStack() as st:
                Wp = sb("Wp", [128, 8, D], BF16, 1, st)
                Wg = sb("Wg0", [128, 8, D], BF16, 1, st)
                PW = sb("PW", [128, 4, 2, 256], BF16, 1, st)
                load_w(Wp, w_in_cols(l, 0))
                load_w(Wg, w_in_cols(l, 3 * D))
                load_w(PW, pool_w[l].rearrange("g (c p) e -> p g c e", p=128))
                HhR = Rot("p1H", [128, 8, TH], BF16, 2, st)
                UR = Rot("p1U", [128, 8, TH], F32, 2, st, 8)
                G0R = Rot("p1G", [128, 8, T], F32, 2, st, 8)
                S2 = sb("p1S2", [128, 6, TH], F32, 1, st)
                S4 = sb("p1S4", [128, 4, TH], F32, 1, st)
                WIN = sb("p1WIN", [128, 8, T], F32, 1, st)
                PB = sb("p1PB", [128, 8, T], BF16, 4, st)
                MR = Rot("p1M", [128, 8, T], F32, 2, st, 8)
                A0, A1 = HALO, HALO + T
                ctx = {}
                tt = lambda e, o, a, b: e.tensor_tensor(out=o, in0=a, in1=b, op=ALU.add)

                def p1s0(i):
                    Hh = HhR.get(); U = UR.get(); G = G0R.get()
                    load_h_halo(Hh, i)
                    proj_halo(Wp, Hh, U)
                    for k in range(NCH):
                        gate_sig(Wg, lambda d, Hh=Hh: Hh.t[:, d, A0:A1], k, G.t[:, k, :], G, [k], R=[(Wg, None), (Hh, None)])
                    ctx[i] = (U, G)

                def p1s1(i):
                    U, G = ctx.pop(i)
                    M = MR.get()
                    K.op("pool", lambda e, U=U: tt(e, WIN.t[:, 0:2, :], U.t[:, 0:2, A0 - 1:A1 - 1], U.t[:, 0:2, A0:A1]),
                         R=[(U, [0, 1])], W=[(WIN, None)])
                    K.op("pool", lambda e, U=U: tt(e, S2.t[:, :, 1:TH], U.t[:, 2:8, 0:TH - 1], U.t[:, 2:8, 1:TH]),
                         R=[(U, [2, 3, 4, 5, 6, 7])], W=[(S2, None)])
                    K.op("pool", lambda e: tt(e, WIN.t[:, 2:4, :], S2.t[:, 0:2, A0 - 1:A1 - 1], S2.t[:, 0:2, A0 + 1:A1 + 1]),
                         R=[(S2, None)], W=[(WIN, None)])
                    K.op("pool", lambda e: tt(e, S4.t[:, :, 2:TH - 2], S2.t[:, 2:6, 1:TH - 3], S2.t[:, 2:6, 3:TH - 1]),
                         R=[(S2, None)], W=[(S4, None)])
                    K.op("pool", lambda e: tt(e, WIN.t[:, 4:6, :], S4.t[:, 0:2, A0 - 2:A1 - 2], S4.t[:, 0:2, A0 + 2:A1 + 2]),
                         R=[(S4, None)], W=[(WIN, None)])
                    K.op("pool", lambda e: tt(e, S2.t[:, 0:2, 4:TH - 4], S4.t[:, 2:4, 2:TH - 6], S4.t[:, 2:4, 6:TH - 2]),
                         R=[(S4, None)], W=[(S2, None)])
                    K.op("pool", lambda e: tt(e, WIN.t[:, 6:8, :], S2.t[:, 0:2, A0 - 4:A1 - 4], S2.t[:, 0:2, A0 + 4:A1 + 4]),
                         R=[(S2, None)], W=[(WIN, None)])
                    edges = []
                    if i % TPS == 0:
                        edges.append((0 if i == 0 else 2, 0))
                    if i % TPS == TPS - 1:
                        edges.append((1 if i == NT - 1 else 3, T - 8))
                    for kind, c0 in edges:
                        ev = CV.t[:, CV_EDGE + kind * 64:CV_EDGE + (kind + 1) * 64].rearrange("p (k c) -> p k c", c=8)
                        K.op("dve", lambda e, ev=ev, c0=c0: e.tensor_tensor(out=WIN.t[:, :, c0:c0 + 8], in0=WIN.t[:, :, c0:c0 + 8],
                                                                            in1=ev, op=ALU.mult),
                             R=[(WIN, None), (CV, None)], W=[(WIN, None)])
                    for g in range(4):
                        K.op("dve", lambda e, g=g, U=U: e.scalar_tensor_tensor(
                            out=PB.t[:, 2 * g:2 * g + 2, :], in0=WIN.t[:, 2 * g:2 * g + 2, :], scalar=1.0 / POOL_W[g],
                            in1=U.t[:, 2 * g:2 * g + 2, A0:A1], op0=ALU.mult, op1=ALU.subtract),
                            R=[(WIN, None), (U, [2 * g, 2 * g + 1])], W=[(PB, [g])])
                    for k in range(NCH):
                        g, co = k // 2, k % 2
                        p = ps()
                        mm_group(p, p.t[:, :], [(PW.t[:, g, cc, co * 128:(co + 1) * 128], PB.t[:, 2 * g + cc, :]) for cc in range(2)],
                                 R=[(PW, None), (PB, [g])])
                        K.op("dve", lambda e, p=p, k=k, G=G, M=M: e.scalar_tensor_tensor(
                            out=M.t[:, k, :], in0=p.t[:, :], scalar=cvc(l, "pool_scale", k), in1=G.t[:, k, :],
                            op0=ALU.mult, op1=ALU.mult),
                            R=[(p, None), (G, [k]), (CV, None)], W=[(M, [k])])
                    K.dma("act", fm(MTd, i * T, T), M.t[:, :, :], M, R=[(M, None)], W=[(MTd, [i])])

                for step in range(NT + 1):
                    if step < NT:
                        p1s0(step)
                    if step >= 1:
                        p1s1(step - 1)
                K.barrier()
                K.flush("p1_%d" % l)

            with contextlib.ExitStack() as st:
                Wq = sb("Wq", [128, 8, D], BF16, 1, st)
                Wg = sb("Wg2", [128, 8, D], BF16, 1, st)
                load_w(Wq, w_in_cols(l, 2 * D))
                load_w(Wg, w_in_cols(l, 5 * D))
                KTR = Rot("p2KT", [128, 8, NMEM], BF16, 2, st)
                VR = Rot("p2V", [128, 2, D], BF16, 2, st)
                HmR = Rot("p2H", [128, 8, T], BF16, 3, st)
                MR = Rot("p2M", [128, 8, T], F32, 2, st, 8)
                QR = Rot("p2Q", [128, 8, T], BF16, 2, st, 8)
                MbR = Rot("p2Mb", [128, 8, T], BF16, 2, st, 8)
                ER = Rot("p2E", [128, 2, T], BF16, 3, st)
                RR = Rot("p2R", [128, T], F32, 3, st)
                T1R = Rot("p2T1", [128, T], F32, 3, st)
                GR = Rot("p2G", [128, T], F32, 3, st)
                ctx = {}
                kvs = {}

                def p2s0(i):
                    Hm = HmR.get(); Q = QR.get()
                    K.dma("sp", Hm.t[:, :, :], fm(HTd, HALO + i * T, T), Hm, R=[(HTd, [i])], W=[(Hm, None)])
                    for k in range(NCH):
                        p = ps()
                        mm_group(p, p.t[:, :], [(Wq.t[:, d, k * 128:(k + 1) * 128], Hm.t[:, d, :]) for d in range(NCH)],
                                 R=[(Wq, None), (Hm, None)])
                        K.op("act", lambda e, p=p, k=k, Q=Q: e.activation(out=Q.t[:, k, :], in_=p.t[:, :], func=AF.Copy),
                             R=[(p, None)], W=[(Q, [k])])
                    ctx[i] = (Hm, Q)

                def p2s1(i):
                    Hm, Q = ctx.pop(i)
                    M = MR.get()
                    s = i // TPS
                    if i % TPS == 0:
                        KTb = KTR.get(); Vb = VR.get()
                        K.dma("sp", KTb.t[:, :, :], KTd.t[s].rearrange("(c p) n -> p c n", p=128), KTb, R=[(KTd, [s])], W=[(KTb, None)])
                        K.dma("sp", Vb.t[:, :, :], Vd.t[s].rearrange("(c p) n -> p c n", p=128), Vb, R=[(Vd, [s])], W=[(Vb, None)])
                        kvs["kt"], kvs["v"] = KTb, Vb
                    KTb, Vb = kvs["kt"], kvs["v"]
                    K.dma("sp", M.t[:, :, :], fm(MTd, i * T, T), M, R=[(MTd, [i])], W=[(M, None)])

                    def scores(h):
                        E = ER.get()
                        for mc in range(2):
                            p = ps()
                            mm_group(p, p.t[:, :], [(KTb.t[:, 2 * h + dc, mc * 128:(mc + 1) * 128], Q.t[:, 2 * h + dc, :]) for dc in range(2)],
                                     R=[(KTb, None), (Q, [2 * h, 2 * h + 1])])
                            K.op("act", lambda e, p=p, mc=mc, E=E: e.activation(out=E.t[:, mc, :], in_=p.t[:, :], func=AF.Exp, scale=1.0 / 16.0),
                                 R=[(p, None)], W=[(E, None)])
                        return E

                    def rest(h, E):
                        Rc = RR.get()
                        p = ps()
                        mm_group(p, p.t[:, :], [(ONES2.t[:, :], E.t[:, mc, :]) for mc in range(2)], R=[(ONES2, None), (E, None)])
                        K.op("dve", lambda e, p=p, Rc=Rc: e.reciprocal(out=Rc.t[:, :], in_=p.t[:, :]), R=[(p, None)], W=[(Rc, None)])
                        for dvc in range(2):
                            k = 2 * h + dvc
                            T1 = T1R.get(); G = GR.get()
                            p = ps()
                            mm_group(p, p.t[:, :], [(Vb.t[:, mc, k * 128:(k + 1) * 128], E.t[:, mc, :]) for mc in range(2)],
                                     R=[(Vb, None), (E, None)])
                            K.op("dve", lambda e, p=p, T1=T1, Rc=Rc: e.tensor_tensor(out=T1.t[:, :], in0=p.t[:, :], in1=Rc.t[:, :], op=ALU.mult),
                                 R=[(p, None), (Rc, None)], W=[(T1, None)])
                            gate_sig(Wg, lambda d, Hm=Hm: Hm.t[:, d, :], k, G.t[:, :], G, None, R=[(Wg, None), (Hm, None)], tanh=True)
                            K.op("dve", lambda e, T1=T1, G=G: e.scalar_tensor_tensor(out=T1.t[:, :], in0=G.t[:, :], scalar=1.0, in1=T1.t[:, :],
                                                                                     op0=ALU.add, op1=ALU.mult),
                                 R=[(T1, None), (G, None)], W=[(T1, None)])
                            K.op("pool", lambda e, T1=T1, M=M, Mb=Mb, k=k: e.tensor_tensor(out=Mb.t[:, k, :], in0=M.t[:, k, :], in1=T1.t[:, :], op=ALU.add),
                                 R=[(T1, None), (M, [k])], W=[(Mb, [k])])

                    Mb = MbR.get()
                    Es = {0: scores(0)}
                    for h in range(4):
                        if h + 1 < 4:
                            Es[h + 1] = scores(h + 1)
                        rest(h, Es.pop(h))
                    K.dma("act", fm(MTbd, i * T, T), Mb.t[:, :, :], Mb, R=[(Mb, None)], W=[(MTbd, [i])])

                for step in range(NT + 1):
                    if step < NT:
                        p2s0(step)
                    if step >= 1:
                        p2s1(step - 1)
                K.barrier()
                K.flush("p2_%d" % l)

            with contextlib.ExitStack() as st:
                Wl = sb("Wl", [128, 8, D], BF16, 1, st)
                Wg = sb("Wg1", [128, 8, D], BF16, 1, st)
                WA = sb("WA", [128, 2, 8, 128], BF16, 1, st)
                WX = sb("WX", [128, 2, 8, 128], BF16, 1, st)
                load_w(Wl, w_in_cols(l, D))
                load_w(Wg, w_in_cols(l, 4 * D))
                load_w(WA, lru_wa[l].rearrange("r b i j -> i r b j"))
                load_w(WX, lru_wx[l].rearrange("r b i j -> i r b j"))
                HhR = Rot("p3H", [128, 8, TH], BF16, 2, st)
                UL = sb("p3U", [128, 8, TH], F32, 8, st)
                XCR = Rot("p3XC", [128, 8, T], F32, 2, st, 8)
                XCbR = Rot("p3XCb", [128, 8, T], BF16, 2, st, 8)
                MLR = Rot("p3ML", [128, 8, T], BF16, 1, st, 8)
                GBR = Rot("p3GB", [128, 8, T], BF16, 2, st, 8)
                RR = Rot("p3r", [128, T], F32, 1, st)
                IXR = Rot("p3ix", [128, T], F32, 2, st)
                AR = [Rot("p3a%d" % r, [128, T], F32, 3, st) for r in range(2)]
                A2R = Rot("p3a2", [128, 2, T], F32, 2, st)
                BR = [Rot("p3b%d" % r, [128, T], F32, 2, st) for r in range(2)]
                HFR = Rot("p3hf", [128, T], F32, 2, st)
                HBR = Rot("p3hb", [128, T], F32, 2, st)
                ABR = Rot("p3ab", [128, T], F32, 2, st)
                GR = Rot("p3G", [128, T], F32, 3, st)
                A0, A1 = HALO, HALO + T
                ctx = {}

                def p3new(i):
                    Hh = HhR.get(); XC = XCR.get(); XCb = XCbR.get()
                    load_h_halo(Hh, i)
                    proj_halo(Wl, Hh, UL)
                    ctx[i] = dict(Hh=Hh, XC=XC, XCb=XCb)

                def p3conv(i, k):
                    XC = ctx[i]["XC"]
                    acc = ps()
                    K.op("dve", lambda e, k=k, acc=acc: e.tensor_scalar(out=acc.t[:, :], in0=UL.t[:, k, A0 - 2:A1 - 2],
                                                                        scalar1=cvc(l, "conv_w", k), scalar2=cvc(l, "conv_b", k),
                                                                        op0=ALU.mult, op1=ALU.add),
                         R=[(UL, [k]), (CV, None)], W=[(acc, None)])
                    for j in range(1, 4):
                        last = (j == 3)
                        out_ap = XC.t[:, k, :] if last else acc.t[:, :]
                        K.op("dve", lambda e, k=k, j=j, acc=acc, out_ap=out_ap: e.scalar_tensor_tensor(
                            out=out_ap, in0=UL.t[:, k, A0 - 2 + j:A1 - 2 + j], scalar=cvc(l, "conv_w", 8 * j + k),
                            in1=acc.t[:, :], op0=ALU.mult, op1=ALU.add),
                            R=[(UL, [k]), (acc, None), (CV, None)], W=[(XC, [k])] if last else [(acc, None)])

                def p3xcb(i, k):
                    XC, XCb = ctx[i]["XC"], ctx[i]["XCb"]
                    K.op("act", lambda e, k=k, XC=XC, XCb=XCb: e.activation(out=XCb.t[:, k, :], in_=XC.t[:, k, :], func=AF.Copy),
                         R=[(XC, [k])], W=[(XCb, [k])])

                def p3front(i, k):
                    c = ctx[i]
                    Hh, XC, XCb = c["Hh"], c["XC"], c["XCb"]
                    G = GR.get()
                    gate_sig(Wg, lambda d, Hh=Hh: Hh.t[:, d, A0:A1], k, G.t[:, :], G, None, R=[(Wg, None), (Hh, None)], tanh=True)
                    K.op("dve", lambda e, G=G: e.tensor_scalar(out=G.t[:, :], in0=G.t[:, :], scalar1=1.0, scalar2=None, op0=ALU.add),
                         R=[(G, None)], W=[(G, None)])
                    A2 = A2R.get()
                    AB2 = []
                    IXs = []
                    for r in range(2):
                        Rt = RR.get(); IX = IXR.get(); At = AR[r].get()
                        col = l * 16 + r * 8 + k
                        cc = CC.t[:, col:col + 1]
                        hc = HC.t[:, col:col + 1]
                        hba = HBB.t[:, l * 32 + r * 8 + k:l * 32 + r * 8 + k + 1]
                        hbx = HBB.t[:, l * 32 + 16 + r * 8 + k:l * 32 + 16 + r * 8 + k + 1]
                        p = ps()
                        mm_group(p, p.t[:, :], [(WA.t[:, r, k, :], XCb.t[:, k, :])], R=[(WA, None), (XCb, [k])])
                        K.op("act", lambda e, p=p, Rt=Rt, hba=hba: e.activation(out=Rt.t[:, :], in_=p.t[:, :], func=AF.Tanh, bias=hba, scale=0.5),
                             R=[(p, None), (HBB, None)], W=[(Rt, None)])
                        px = ps()
                        mm_group(px, px.t[:, :], [(WX.t[:, r, k, :], XCb.t[:, k, :])], R=[(WX, None), (XCb, [k])])
                        K.op("act", lambda e, px=px, hbx=hbx: e.activation(out=px.t[:, :], in_=px.t[:, :], func=AF.Tanh, bias=hbx, scale=0.5),
                             R=[(px, None), (HBB, None)], W=[(px, None)])
                        K.op("dve", lambda e, px=px, IX=IX, XC=XC, k=k: e.scalar_tensor_tensor(
                            out=IX.t[:, :], in0=px.t[:, :], scalar=1.0, in1=XC.t[:, k, :], op0=ALU.add, op1=ALU.mult),
                            R=[(px, None), (XC, [k])], W=[(IX, None)])
                        K.op("act", lambda e, At=At, Rt=Rt, hc=hc: e.activation(out=At.t[:, :], in_=Rt.t[:, :], func=AF.Exp, bias=hc, scale=hc),
                             R=[(Rt, None), (HC, None)], W=[(At, None)])
                        K.op("act", lambda e, A2=A2, Rt=Rt, cc=cc, r=r: e.activation(out=A2.t[:, r, :], in_=Rt.t[:, :], func=AF.Exp, bias=cc, scale=cc),
                             R=[(Rt, None), (CC, None)], W=[(A2, None)])
                        AB2.append(At)
                        IXs.append(IX)
                    K.op("act", lambda e, A2=A2: e.activation(out=A2.t[:, :, :], in_=A2.t[:, :, :], func=AF.Sqrt, bias=1.0 / 16.0, scale=-1.0 / 16.0),
                         R=[(A2, None)], W=[(A2, None)])
                    Bs = []
                    for r in range(2):
                        Bt = BR[r].get()
                        K.op("pool", lambda e, IX=IXs[r], A2=A2, Bt=Bt, r=r: e.tensor_tensor(out=Bt.t[:, :], in0=IX.t[:, :], in1=A2.t[:, r, :], op=ALU.mult),
                             R=[(IXs[r], None), (A2, None)], W=[(Bt, None)])
                        Bs.append(Bt)
                    c[("f", k)] = (G, [(AB2[0], Bs[0]), (AB2[1], Bs[1])])

                def p3back(i, k):
                    c = ctx[i]
                    ML, GB = c["ML"], c["GB"]
                    G, AB2 = c.pop(("f", k))
                    HF = HFR.get(); HB = HBR.get(); AB = ABR.get()
                    (Af, Bf), (Ab, Bb) = AB2
                    K.op("dve", lambda e, HF=HF, Af=Af, Bf=Bf, k=k: e.tensor_tensor_scan(
                        out=HF.t[:, :], data0=Af.t[:, :], data1=Bf.t[:, :], initial=CF.t[:, k:k + 1], op0=ALU.mult, op1=ALU.add),
                        R=[(Af, None), (Bf, None), (CF, None)], W=[(HF, None)])
                    K.op("dve", lambda e, HB=HB, Ab=Ab, Bb=Bb: e.tensor_tensor_scan(
                        out=HB.t[:, ::-1], data0=Ab.t[:, ::-1], data1=Bb.t[:, ::-1], initial=0.0, op0=ALU.mult, op1=ALU.add),
                        R=[(Ab, None), (Bb, None)], W=[(HB, None)])
                    K.op("dve", lambda e, AB=AB, Ab=Ab: e.tensor_tensor_scan(
                        out=AB.t[:, ::-1], data0=Ab.t[:, ::-1], data1=ZER.t[:, ::-1], initial=1.0, op0=ALU.mult, op1=ALU.add),
                        R=[(Ab, None), (ZER, None)], W=[(AB, None)])
                    K.op("pool", lambda e, HF=HF, k=k: e.tensor_copy(out=CF.t[:, k:k + 1], in_=HF.t[:, T - 1:T]),
                         R=[(HF, None)], W=[(CF, None)])
                    K.op("pool", lambda e, HB=HB, k=k, i=i: e.tensor_copy(out=HBS.t[:, k, i:i + 1], in_=HB.t[:, 0:1]),
                         R=[(HB, None)], W=[(HBS, None)])
                    K.op("pool", lambda e, HF=HF, HB=HB: e.tensor_tensor(out=HF.t[:, :], in0=HF.t[:, :], in1=HB.t[:, :], op=ALU.add),
                         R=[(HF, None), (HB, None)], W=[(HF, None)])
                    K.op("pool", lambda e, AB=AB, k=k, i=i: e.tensor_copy(out=PBS.t[:, k, i:i + 1], in_=AB.t[:, 0:1]),
                         R=[(AB, None)], W=[(PBS, None)])
                    K.op("pool", lambda e, AB=AB, G=G, GB=GB, k=k: e.tensor_tensor(out=GB.t[:, k, :], in0=AB.t[:, :], in1=G.t[:, :], op=ALU.mult),
                         R=[(AB, None), (G, None)], W=[(GB, [k])])
                    K.op("pool", lambda e, HF=HF, G=G, ML=ML, k=k: e.tensor_tensor(out=ML.t[:, k, :], in0=HF.t[:, :], in1=G.t[:, :], op=ALU.mult),
                         R=[(HF, None), (G, None)], W=[(ML, [k])])

                for step in range(NT + 1):
                    inew = step if step < NT else None
                    iold = step - 1 if step >= 1 else None
                    if inew is not None:
                        p3new(inew)
                    if iold is not None:
                        c = ctx[iold]
                        c["ML"] = MLR.get(); c["GB"] = GBR.get()
                        if iold == 0:
                            K.op("dve", lambda e: e.memset(CF.t[:, :], 0.0), W=[(CF, None)])
                        elif iold % TPS == 0:
                            K.op("dve", lambda e: e.tensor_scalar(out=CF.t[:, :], in0=CF.t[:, :], scalar1=LINK(), scalar2=None, op0=ALU.mult),
                                 R=[(CF, None), (CV, None)], W=[(CF, None)])
                    for k in range(NCH):
                        if inew is not None:
                            p3conv(inew, k)
                        if iold is not None:
                            p3front(iold, k)
                            if k >= 1:
                                p3back(iold, k - 1)
                        if inew is not None:
                            p3xcb(inew, k)
                    if iold is not None:
                        p3back(iold, NCH - 1)
                        c = ctx.pop(iold)
                        K.dma("pool", fm(MLd, iold * T, T), c["ML"].t[:, :, :], c["ML"], R=[(c["ML"], None)], W=[(MLd, [iold])])
                        K.dma("pool", fm(GABbd, iold * T, T), c["GB"].t[:, :, :], c["GB"], R=[(c["GB"], None)], W=[(GABbd, [iold])])
                K.op("dve", lambda e: e.memset(CBK.t[:, :, NT - 1:NT], 0.0), W=[(CBK, None)])
                for i in range(NT - 2, -1, -1):
                    K.op("dve", lambda e, i=i: e.tensor_tensor(out=TMPC.t[:, :], in0=PBS.t[:, :, i + 1], in1=CBK.t[:, :, i + 1], op=ALU.mult),
                         R=[(PBS, None), (CBK, None)], W=[(TMPC, None)])
                    K.op("dve", lambda e, i=i: e.tensor_tensor(out=CBK.t[:, :, i], in0=TMPC.t[:, :], in1=HBS.t[:, :, i + 1], op=ALU.add),
                         R=[(TMPC, None), (HBS, None)], W=[(CBK, None)])
                    if (i + 1) % TPS == 0:
                        K.op("dve", lambda e, i=i: e.tensor_scalar(out=CBK.t[:, :, i], in0=CBK.t[:, :, i], scalar1=LINK(), scalar2=None, op0=ALU.mult),
                             R=[(CBK, None), (CV, None)], W=[(CBK, None)])
                K.barrier()
                K.flush("p3_%d" % l)

            with contextlib.ExitStack() as st:
                Wo = sb("Wo", [128, 8, D], BF16, 1, st)
                load_w(Wo, w_out[l].rearrange("(c p) e -> p c e", p=128))
                MR = Rot("5aM", [128, 8, T], BF16, 2, st, 8)
                GBR = Rot("5aGB", [128, 8, T], BF16, 2, st, 8)
                MLR5 = Rot("5aML", [128, 8, T], BF16, 2, st)
                XR = Rot("5aX", [128, 8, T], F32, 1, st, 8)
                MBR = Rot("5aMB", [128, 8, T], BF16, 2, st, 8)
                OR_ = Rot("5aO", [128, 8, T], F32, 2, st, 8)
                SQAR = Rot("5aSQa", [128, 8, T], BF16, 2, st)
                SQB = sb("5aSQb", [128, 8, T], BF16, 8, st)
                RSR = Rot("5aRS", [128, T], F32, 2, st)
                TR = Rot("5aT", [128, T], F32, 3, st)
                xsrc = xin if l == 0 else XTd
                ctx = {}

                def p5aload(i):
                    M = MR.get(); GB = GBR.get(); ML = MLR5.get()
                    K.dma("sp", ML.t[:, :, :], fm(MLd, i * T, T), ML, R=[(MLd, [i])], W=[(ML, None)])
                    K.dma("sp", M.t[:, :, :], fm(MTbd, i * T, T), M, R=[(MTbd, [i])], W=[(M, None)])
                    K.dma("sp", GB.t[:, :, :], fm(GABbd, i * T, T), GB, R=[(GABbd, [i])], W=[(GB, None)])
                    ctx[("ld", i)] = (M, GB, ML)

                def p5amb(i):
                    M, GB, ML = ctx.pop(("ld", i))
                    MB = MBR.get()
                    for k in range(NCH):
                        K.op("dve", lambda e, k=k, M=M, GB=GB, MB=MB, i=i: e.scalar_tensor_tensor(
                            out=MB.t[:, k, :], in0=GB.t[:, k, :], scalar=CBK.t[:, k, i:i + 1], in1=M.t[:, k, :],
                            op0=ALU.mult, op1=ALU.add),
                            R=[(GB, [k]), (M, [k]), (CBK, None)], W=[(MB, [k])])
                    ctx[("mb", i)] = (MB, ML)

                def p5as0(i):
                    MB, ML = ctx.pop(("mb", i))
                    O = OR_.get(); SQ = SQAR.get()
                    for k in range(NCH):
                        p = ps()
                        mm_group(p, p.t[:, :], [(Wo.t[:, d, k * 128:(k + 1) * 128], MB.t[:, d, :]) for d in range(NCH)]
                                 + [(Wo.t[:, d, k * 128:(k + 1) * 128], ML.t[:, d, :]) for d in range(NCH)],
                                 R=[(Wo, None), (MB, None), (ML, None)])
                        K.op("act", lambda e, p=p, k=k, O=O: e.activation(out=O.t[:, k, :], in_=p.t[:, :], func=AF.Copy),
                             R=[(p, None)], W=[(O, [k])])
                    K.op("act", lambda e, O=O, SQ=SQ: e.activation(out=SQ.t[:, :, :], in_=O.t[:, :, :], func=AF.Square), R=[(O, None)], W=[(SQ, None)])
                    ctx[i] = (O, SQ)

                def p5as1a(i):
                    O, SQ = ctx.pop(i)
                    X = XR.get(); RS = RSR.get()
                    K.dma("sp", X.t[:, :, :], fm(xsrc, i * T, T), X, R=[(xsrc, [i])], W=[(X, None)])
                    rstd_from_sq(SQ, T, RS)
                    for k in range(NCH):
                        Tt = TR.get()
                        K.op("dve", lambda e, k=k, Tt=Tt, RS=RS, O=O: e.scalar_tensor_tensor(
                            out=Tt.t[:, :], in0=O.t[:, k, :], scalar=cvc(l, "g_mix_post", k), in1=RS.t[:, :], op0=ALU.mult, op1=ALU.mult),
                            R=[(O, [k]), (RS, None), (CV, None)], W=[(Tt, None)])
                        K.op("pool", lambda e, k=k, Tt=Tt, X=X: e.tensor_tensor(out=X.t[:, k, :], in0=X.t[:, k, :], in1=Tt.t[:, :], op=ALU.add),
                             R=[(Tt, None), (X, [k])], W=[(X, [k])])
                    K.dma("act", fm(XTd, i * T, T), X.t[:, :, :], X, R=[(X, None)], W=[(XTd, [i])])
                    K.op("act", lambda e, X=X: e.activation(out=SQB.t[:, :, :], in_=X.t[:, :, :], func=AF.Square),
                         R=[(X, None)], W=[(SQB, None)])
                    ctx[("b", i)] = X

                def p5as1b(i):
                    X = ctx.pop(("b", i))
                    RS2 = RSR.get()
                    rstd_from_sq(SQB, T, RS2)
                    for k in range(NCH):
                        K.op("dve", lambda e, k=k, X=X, RS2=RS2: e.scalar_tensor_tensor(
                            out=SQB.t[:, k, :], in0=X.t[:, k, :], scalar=cvc(l, "g_mlp_pre", k),
                            in1=RS2.t[:, :], op0=ALU.mult, op1=ALU.mult),
                            R=[(X, [k]), (RS2, None), (CV, None)], W=[(SQB, None)])
                    K.dma("act", fm(H2Td, i * T, T), SQB.t[:, :, :], SQB, R=[(SQB, None)], W=[(H2Td, [i])])

                p5aload(0)
                p5amb(0)
                for step in range(NT + 1):
                    if step >= 1:
                        p5as1a(step - 1)
                    if step + 1 < NT:
                        p5aload(step + 1)
                    if step < NT:
                        p5as0(step)
                    if step >= 1:
                        p5as1b(step - 1)
                    if step + 1 < NT:
                        p5amb(step + 1)
                K.barrier()
                K.flush("p5a_%d" % l)

            with contextlib.ExitStack() as st:
                W1 = sb("W1", [128, 8, 4 * D], BF16, 1, st)
                load_w(W1, mlp_w1[l].rearrange("(c p) e -> p c e", p=128))
                H2R = Rot("5bH2", [128, 8, T], BF16, 2, st)
                HDR = Rot("5bHD", [128, 8, T], BF16, 3, st)
                RLR = Rot("5bRL", [128, T], F32, 3, st)
                for i in range(NT):
                    H2 = H2R.get()
                    K.dma("sp", H2.t[:, :, :], fm(H2Td, i * T, T), H2, R=[(H2Td, [i])], W=[(H2, None)])
                    for fq in range(4):
                        HD = HDR.get()
                        for fk in range(8):
                            f = fq * 8 + fk
                            RL = RLR.get()
                            p = ps()
                            mm_group(p, p.t[:, :], [(W1.t[:, d, f * 128:(f + 1) * 128], H2.t[:, d, :]) for d in range(NCH)],
                                     R=[(W1, None), (H2, None)])
                            K.op("act", lambda e, p=p, RL=RL: e.activation(out=RL.t[:, :], in_=p.t[:, :], func=AF.Relu),
                                 R=[(p, None)], W=[(RL, None)])
                            K.op("act", lambda e, RL=RL, HD=HD, fk=fk: e.activation(out=HD.t[:, fk, :], in_=RL.t[:, :], func=AF.Square),
                                 R=[(RL, None)], W=[(HD, None)])
                        K.dma("act", HIDd.t[fq * D:(fq + 1) * D, i * T:(i + 1) * T].rearrange("(c p) n -> p c n", p=128),
                              HD.t[:, :, :], HD, R=[(HD, None)], W=[(HIDd, [i])])
                K.barrier()
                K.flush("p5b_%d" % l)

            with contextlib.ExitStack() as st:
                W2 = sb("W2", [128, 32, D], BF16, 1, st)
                load_w(W2, mlp_w2[l].rearrange("(c p) e -> p c e", p=128))
                HDR = Rot("5cHD", [128, 8, T], BF16, 8, st)
                XR = Rot("5cX", [128, 8, T], F32, 1, st, 8)
                YR = Rot("5cY", [128, 8, T], F32, 1, st, 8)
                SQA = sb("5cSQa", [128, 8, T], BF16, 1, st)
                SQBH = sb("5cSQbH", [128, 8, T], BF16, 8, st)
                RSR = Rot("5cRS", [128, T], F32, 2, st)
                TR = Rot("5cT", [128, T], F32, 2, st)
                last = (l == depth - 1)
                ctx = {}

                def p5cload(i):
                    HD = []
                    for fq in range(4):
                        b = HDR.get()
                        K.dma("sp", b.t[:, :, :], HIDd.t[fq * D:(fq + 1) * D, i * T:(i + 1) * T].rearrange("(c p) n -> p c n", p=128),
                              b, R=[(HIDd, [i])], W=[(b, None)])
                        HD.append(b)
                    ctx[("hd", i)] = HD

                def p5cs0(i):
                    Y = YR.get()
                    HD = ctx.pop(("hd", i))
                    for k in range(NCH):
                        p = ps()
                        mm_group(p, p.t[:, :], [(W2.t[:, f, k * 128:(k + 1) * 128], HD[f // 8].t[:, f % 8, :]) for f in range(32)],
                                 R=[(W2, None)] + [(b, None) for b in HD])
                        K.op("act", lambda e, p=p, k=k, Y=Y: e.activation(out=Y.t[:, k, :], in_=p.t[:, :], func=AF.Copy),
                             R=[(p, None)], W=[(Y, [k])])
                    K.op("act", lambda e, Y=Y: e.activation(out=SQA.t[:, :, :], in_=Y.t[:, :, :], func=AF.Square), R=[(Y, None)], W=[(SQA, None)])
                    ctx[i] = Y

                def p5cs1a(i):
                    Y = ctx.pop(i)
                    X = XR.get(); RS = RSR.get()
                    K.dma("sp", X.t[:, :, :], fm(XTd, i * T, T), X, R=[(XTd, [i])], W=[(X, None)])
                    rstd_from_sq(SQA, T, RS)
                    for k in range(NCH):
                        Tt = TR.get()
                        K.op("dve", lambda e, k=k, Tt=Tt, RS=RS, Y=Y: e.scalar_tensor_tensor(
                            out=Tt.t[:, :], in0=Y.t[:, k, :], scalar=cvc(l, "g_mlp_post", k), in1=RS.t[:, :], op0=ALU.mult, op1=ALU.mult),
                            R=[(Y, [k]), (RS, None), (CV, None)], W=[(Tt, None)])
                        K.op("pool", lambda e, k=k, Tt=Tt, X=X: e.tensor_tensor(out=X.t[:, k, :], in0=X.t[:, k, :], in1=Tt.t[:, :], op=ALU.add),
                             R=[(Tt, None), (X, [k])], W=[(X, [k])])
                    if last:
                        K.dma("act", fm(yout, i * T, T), X.t[:, :, :], X, R=[(X, None)], W=[(yout, [i])])
                    else:
                        K.dma("act", fm(XTd, i * T, T), X.t[:, :, :], X, R=[(X, None)], W=[(XTd, [i])])
                        K.op("act", lambda e, X=X: e.activation(out=SQBH.t[:, :, :], in_=X.t[:, :, :], func=AF.Square),
                             R=[(X, None)], W=[(SQBH, None)])
                    ctx[("b", i)] = X

                def p5cs1b(i):
                    X = ctx.pop(("b", i))
                    if last:
                        return
                    RS2 = RSR.get()
                    rstd_from_sq(SQBH, T, RS2)
                    for k in range(NCH):
                        K.op("dve", lambda e, k=k, X=X, RS2=RS2: e.scalar_tensor_tensor(
                            out=SQBH.t[:, k, :], in0=X.t[:, k, :], scalar=cvc(l + 1, "g_mix_pre", k),
                            in1=RS2.t[:, :], op0=ALU.mult, op1=ALU.mult),
                            R=[(X, [k]), (RS2, None), (CV, None)], W=[(SQBH, None)])
                    K.dma("act", fm(HTd, HALO + i * T, T), SQBH.t[:, :, :], SQBH, R=[(SQBH, None)], W=[(HTd, [i])])

                p5cload(0)
                for step in range(NT + 1):
                    if step >= 1:
                        p5cs1a(step - 1)
                    if step + 1 < NT:
                        p5cload(step + 1)
                    if step < NT:
                        p5cs0(step)
                    if step >= 1:
                        p5cs1b(step - 1)
                K.barrier()
                K.flush("p5c_%d" % l)
    return nc


def _edge_tables(link):
    tab = np.ones((4, 8, 8), np.float32)
    for g, w in enumerate(POOL_W):
        h = w // 2
        S = 1 << 20
        first = np.array([w / float((t + h) - max(t - h, 0)) for t in range(8)], np.float32)
        last = np.array([w / float(min(t + h, S) - (t - h)) for t in range(S - 8, S)], np.float32)
        for k in (2 * g, 2 * g + 1):
            tab[0, k] = first
            tab[1, k] = last
            tab[2, k] = 1.0 if link else first
            tab[3, k] = 1.0 if link else last
    return tab


def _build_cv(P, link):
    cv = np.zeros((128, NCV), np.float32)

    def put(l, name, v):
        v = np.asarray(v, np.float32).reshape(-1, 8, 128)
        o = l * CPL + CV_OFF[name]
        n = v.shape[0]
        cv[:, o:o + 8 * n] = v.transpose(2, 0, 1).reshape(128, 8 * n)
    for l in range(DEPTH):
        put(l, "g_mix_pre", P["norm_mix_pre"][l])
        put(l, "g_mix_post", P["norm_mix_post"][l])
        put(l, "g_mem", P["norm_mem"][l])
        put(l, "pool_scale", P["pool_scale"][l])
        put(l, "conv_w", P["conv_w"][l])
        put(l, "conv_b", P["conv_b"][l])
        put(l, "lru_ba", P["lru_ba"][l])
        put(l, "lru_bx", P["lru_bx"][l])
        put(l, "lru_lam", P["lru_lambda"][l])
        put(l, "g_mlp_pre", P["norm_mlp_pre"][l])
        put(l, "g_mlp_post", P["norm_mlp_post"][l])
    cv[:, CV_EDGE:CV_EDGE + 256] = _edge_tables(link).reshape(1, 256)
    cv[:, CV_LINK] = 1.0 if link else 0.0
    return cv


_NC_CACHE = {}


def kernel(**inputs):
    P = {k: np.asarray(v) for k, v in inputs.items()}
    xs, xp = P["x_sample"], P["x_prompt"]
    ms, mp = P["mem_sample"], P["mem_prompt"]
    n = 8
    wnames = ["w_in", "pool_w", "lru_wa", "lru_wx", "w_kv", "w_out", "mlp_w1", "mlp_w2"]
    weights = {k: np.ascontiguousarray(P[k], dtype=np.float32) for k in wnames}
    cv_s = _build_cv(P, True)
    cv_p = _build_cv(P, False)
    in_maps = []
    for c in range(n):
        if c < 2:
            xT = np.ascontiguousarray(xs[c].T)
            memT = np.ascontiguousarray(np.stack([ms[c].T] * 4))
            cv = cv_s
        elif c in (4, 5):
            j0 = 4 * (c - 4)
            xT = np.ascontiguousarray(np.concatenate([xp[j0 + j].T for j in range(4)], axis=1))
            memT = np.ascontiguousarray(np.stack([mp[j0 + j].T for j in range(4)]))
            cv = cv_p
        else:
            xT = np.zeros((D, 4 * SEG), np.float32)
            memT = np.zeros((4, D, NMEM), np.float32)
            cv = cv_p
        m = {"xT": xT.astype(np.float32), "memT": memT.astype(np.float32), "cv": cv}
        m.update(weights)
        in_maps.append(m)
    if "nc" not in _NC_CACHE:
        _NC_CACHE["nc"] = build_program()
    res = run_bass_kernel_spmd(_NC_CACHE["nc"], in_maps, core_ids=list(range(n)))
    y_sample = np.stack([np.ascontiguousarray(res.results[c]["yT"].T) for c in range(2)]).astype(np.float32)
    yp = []
    for c in (4, 5):
        yT = res.results[c]["yT"]
        for j in range(4):
            yp.append(np.ascontiguousarray(yT[:, j * SEG:(j + 1) * SEG].T))
    y_prompt = np.stack(yp).astype(np.float32)
    return (y_prompt, y_sample)
```
